# Optimizing a Trainium2 kernel written in Bass

```python
import math
import jax, jax.numpy as jnp
from jax import lax
import numpy as np

D_MODEL = 2048
BATCH = 4
SEQ = 4096
DEPTH = 2

N_SUBLAYERS = 3
N_BRANCHES = 4
MIX_WIDTH = D_MODEL // 4
HEAD_DIM = 64
ROPE_THETA = 10000.0
NORM_EPS = 1e-6
Q_BLOCK = 128
NEG_INF = -1e30
FORCE_SCORE = 1e30
D_FF = ((8 * D_MODEL // 3 + 255) // 256) * 256
MACARON_WEIGHT = 0.5

NSA_HEADS = MIX_WIDTH // HEAD_DIM
NSA_KV_HEADS = 2
NSA_GROUP = NSA_HEADS // NSA_KV_HEADS
CMP_BLOCK = 32
CMP_STRIDE = 16
CMP_HIDDEN = 256
SLC_BLOCK = 64
SLC_TOPK = 16
WIN_SIZE = 512
NSA_SPLITS = (NSA_HEADS * HEAD_DIM,) + (NSA_KV_HEADS * HEAD_DIM,) * 6 + (3 * NSA_HEADS,)

RWKV_HEADS = MIX_WIDTH // HEAD_DIM
RWKV_WIDTH = MIX_WIDTH
RWKV_DECAY_LORA = 64
RWKV_AAA_LORA = 64
RWKV_GATE_LORA = 128
RWKV_GN_EPS = 64e-5
RWKV_SPLITS = (RWKV_WIDTH, RWKV_DECAY_LORA, RWKV_WIDTH, RWKV_WIDTH, RWKV_AAA_LORA, RWKV_GATE_LORA)

GLA_HEADS = 4
GLA_DK = MIX_WIDTH // 2
GLA_DV = MIX_WIDTH
GLA_DK_HEAD = GLA_DK // GLA_HEADS
GLA_DV_HEAD = GLA_DV // GLA_HEADS
GLA_GATE_LORA = 16
GLA_TAU = 16.0
GLA_CHUNK = 64
GLA_SPLITS = (GLA_DK, GLA_DK, GLA_DV, GLA_GATE_LORA, GLA_DV)

MLA_HEADS = MIX_WIDTH // 64
MLA_Q_RANK = 384
MLA_KV_RANK = 128
MLA_NOPE = 64
MLA_ROPE = 32
MLA_V = MIX_WIDTH // MLA_HEADS
MLA_SPLITS = (MLA_Q_RANK, MLA_KV_RANK, MLA_ROPE)

BRANCH_SPLITS = (sum(NSA_SPLITS), sum(RWKV_SPLITS), sum(GLA_SPLITS), sum(MLA_SPLITS), N_BRANCHES * D_MODEL)
PROJ_IN = sum(BRANCH_SPLITS)

kernel_name = 'hybrid_nsa_rwkv7_gla_mla_macaron'


def _split(t, sizes):
    return jnp.split(t, np.cumsum(sizes)[:-1].tolist(), axis=-1)


def _rmsnorm(x, g):
    xf = x.astype(jnp.float32)
    y = xf * lax.rsqrt(jnp.mean(xf * xf, axis=-1, keepdims=True) + NORM_EPS)
    return (y * g.astype(jnp.float32)).astype(x.dtype)


def _rope(x, positions):
    d = x.shape[-1]
    inv = ROPE_THETA ** (-jnp.arange(0, d, 2, dtype=jnp.float32) / d)
    ang = positions.astype(jnp.float32)[:, None] * inv[None, :]
    cos, sin = jnp.cos(ang)[:, None, :], jnp.sin(ang)[:, None, :]
    x1, x2 = jnp.split(x.astype(jnp.float32), 2, axis=-1)
    return jnp.concatenate([x1 * cos - x2 * sin, x1 * sin + x2 * cos], axis=-1).astype(x.dtype)


def _modulation(c, w, b):
    m = jax.nn.silu(c) @ w + b
    shift, scale, gate = jnp.split(m[:, None, :], 3, axis=-1)
    return shift, scale, gate


def _sandwich(x, c, fn, w_mod, b_mod, g_pre, g_post, res_weight):
    shift, scale, gate = _modulation(c, w_mod, b_mod)
    u = _rmsnorm(x, g_pre) * (1.0 + scale) + shift
    y = _rmsnorm(fn(u), g_post)
    return x + res_weight * gate * y


def _swiglu(u, wg, wu, wd):
    return (jax.nn.silu(u @ wg) * (u @ wu)) @ wd


def _causal_attention_blocked(q, k, v, scale):
    B, H, S, dq = q.shape
    nb = S // Q_BLOCK
    qb = q.reshape(B, H, nb, Q_BLOCK, dq).transpose(2, 0, 1, 3, 4)
    k_pos = jnp.arange(S)

    def one_block(args):
        q_i, i = args
        s = jnp.einsum('bhqd,bhkd->bhqk', q_i, k).astype(jnp.float32) * scale
        q_pos = i * Q_BLOCK + jnp.arange(Q_BLOCK)
        s = jnp.where(k_pos[None, :] <= q_pos[:, None], s, NEG_INF)
        p = jax.nn.softmax(s, axis=-1)
        return jnp.einsum('bhqk,bhkd->bhqd', p.astype(v.dtype), v)

    o = lax.map(one_block, (qb, jnp.arange(nb)))
    return o.transpose(1, 2, 0, 3, 4).reshape(B, H, S, v.shape[-1])


def _nsa(q, k_cmp, v_cmp, k_slc, v_slc, k_win, v_win, gate_logits, cmp_pos, cmp_w1, cmp_w2, positions):
    B, S = q.shape[0], q.shape[1]
    Hk, G, d = NSA_KV_HEADS, NSA_GROUP, HEAD_DIM
    scale = d ** -0.5
    nq = S // Q_BLOCK
    t_pos = np.arange(S)
    qg = _rope(q, positions).reshape(B, S, Hk, G, d).transpose(0, 2, 3, 1, 4)
    heads_first = lambda t: t.transpose(0, 2, 1, 3)
    kc_in, vc_in = heads_first(_rope(k_cmp, positions)), heads_first(v_cmp)
    ks, vs = heads_first(_rope(k_slc, positions)), heads_first(v_slc)
    kw, vw = heads_first(_rope(k_win, positions)), heads_first(v_win)

    n_cmp = (S - CMP_BLOCK) // CMP_STRIDE + 1
    cmp_idx = np.arange(n_cmp)[:, None] * CMP_STRIDE + np.arange(CMP_BLOCK)[None, :]

    def compress(t, pos_emb, w1, w2):
        blocks = (t[:, :, cmp_idx] + pos_emb).reshape(B, Hk, n_cmp, CMP_BLOCK * d)
        return jax.nn.gelu(blocks @ w1) @ w2

    kc = compress(kc_in, cmp_pos[0], cmp_w1[0], cmp_w2[0])
    vc = compress(vc_in, cmp_pos[1], cmp_w1[1], cmp_w2[1])
    cmp_mask = cmp_idx[:, -1][None, :] <= t_pos[:, None]
    s_cmp = jnp.einsum('bhgsd,bhnd->bhgsn', qg, kc).astype(jnp.float32) * scale
    p_cmp = jax.nn.softmax(jnp.where(cmp_mask, s_cmp, NEG_INF), axis=-1) * cmp_mask
    o_cmp = jnp.einsum('bhgsn,bhnd->bhgsd', p_cmp.astype(vc.dtype), vc)

    n_sel = S // SLC_BLOCK
    blk = np.arange(n_sel)
    overlap = ((cmp_idx[:, :1] < (blk[None, :] + 1) * SLC_BLOCK) &
               (cmp_idx[:, -1:] >= blk[None, :] * SLC_BLOCK)).astype(np.float32)
    imp = jnp.einsum('bhgsn,nj->bhsj', p_cmp, overlap)
    cur = (t_pos // SLC_BLOCK)[:, None]
    valid = blk[None, :] <= cur
    forced = valid & ((blk[None, :] == 0) | (blk[None, :] >= cur - 1))
    imp = jnp.where(forced, FORCE_SCORE, jnp.where(valid, imp, NEG_INF))
    k_top = min(SLC_TOPK, n_sel)
    _, sel_idx = lax.top_k(imp, k_top)

    k_blocks = ks.reshape(B, Hk, n_sel, SLC_BLOCK, d)
    v_blocks = vs.reshape(B, Hk, n_sel, SLC_BLOCK, d)
    q_chunks = qg.reshape(B, Hk, G, nq, Q_BLOCK, d).transpose(3, 0, 1, 2, 4, 5)
    idx_chunks = sel_idx.reshape(B, Hk, nq, Q_BLOCK, k_top).transpose(2, 0, 1, 3, 4)
    b_ix = jnp.arange(B)[:, None, None, None]
    h_ix = jnp.arange(Hk)[None, :, None, None]

    def selected_block(args):
        q_c, idx_c, blk_id = args
        kb = k_blocks[b_ix, h_ix, idx_c]
        vb = v_blocks[b_ix, h_ix, idx_c]
        s = jnp.einsum('bhgqd,bhqnld->bhgqnl', q_c, kb).astype(jnp.float32) * scale
        q_pos = blk_id * Q_BLOCK + jnp.arange(Q_BLOCK)
        k_pos = idx_c[..., None] * SLC_BLOCK + jnp.arange(SLC_BLOCK)
        mask = (k_pos <= q_pos[:, None, None])[:, :, None]
        s = jnp.where(mask, s, NEG_INF).reshape(B, Hk, G, Q_BLOCK, k_top * SLC_BLOCK)
        p = jax.nn.softmax(s, axis=-1).reshape(B, Hk, G, Q_BLOCK, k_top, SLC_BLOCK)
        return jnp.einsum('bhgqnl,bhqnld->bhgqd', p.astype(vb.dtype), vb)

    o_slc = lax.map(selected_block, (q_chunks, idx_chunks, jnp.arange(nq)))
    o_slc = o_slc.transpose(1, 2, 3, 0, 4, 5).reshape(B, Hk, G, S, d)

    n_wb = WIN_SIZE // Q_BLOCK

    def band(t):
        tp = jnp.pad(t, ((0, 0), (0, 0), (WIN_SIZE, 0), (0, 0))).reshape(B, Hk, nq + n_wb, Q_BLOCK, d)
        return jnp.concatenate([tp[:, :, i:i + nq] for i in range(n_wb + 1)], axis=3)

    kwb, vwb = band(kw), band(vw)
    q_pos = t_pos.reshape(nq, Q_BLOCK, 1)
    k_pos = ((np.arange(nq)[:, None] - n_wb) * Q_BLOCK + np.arange((n_wb + 1) * Q_BLOCK)[None, :])[:, None, :]
    win_mask = (k_pos >= 0) & (k_pos <= q_pos) & (k_pos > q_pos - WIN_SIZE)
    s_win = jnp.einsum('bhgnqd,bhnkd->bhgnqk', qg.reshape(B, Hk, G, nq, Q_BLOCK, d), kwb).astype(jnp.float32) * scale
    p_win = jax.nn.softmax(jnp.where(win_mask, s_win, NEG_INF), axis=-1)
    o_win = jnp.einsum('bhgnqk,bhnkd->bhgnqd', p_win.astype(vwb.dtype), vwb).reshape(B, Hk, G, S, d)

    gates = jax.nn.sigmoid(gate_logits.reshape(B, S, Hk, G, 3).transpose(0, 2, 3, 1, 4))
    o = gates[..., 0:1] * o_cmp + gates[..., 1:2] * o_slc + gates[..., 2:3] * o_win
    return o.transpose(0, 3, 1, 2, 4).reshape(B, S, NSA_HEADS * d)


def _rwkv7(p_rwkv, mu, w0, w_w2, a0, a_w2, g_w2, k_k, k_a, r_k, ln_w, ln_b):
    B, S, _ = p_rwkv.shape
    H, N = RWKV_HEADS, HEAD_DIM
    prev = jnp.pad(p_rwkv, ((0, 0), (1, 0), (0, 0)))[:, :-1]
    xs = p_rwkv + (prev - p_rwkv) * mu
    r, w_lo, k, v, a_lo, g_lo = [t.astype(jnp.float32) for t in _split(xs, RWKV_SPLITS)]
    log_w = -math.exp(-0.5) * jax.nn.sigmoid(w0 + jnp.tanh(w_lo) @ w_w2)
    a = jax.nn.sigmoid(a0 + a_lo @ a_w2)
    g = jax.nn.sigmoid(g_lo) @ g_w2
    heads = lambda t: t.reshape(B, S, H, N)
    kk = heads(k * k_k)
    kk = kk / jnp.maximum(jnp.sqrt(jnp.sum(kk * kk, axis=-1, keepdims=True)), 1e-12)
    k = k * (1.0 + (a - 1.0) * k_a)
    r, log_w, k, v, a = heads(r), heads(log_w), heads(k), heads(v), heads(a)

    def step(state, inp):
        r_t, w_t, k_t, v_t, kk_t, a_t = inp
        s_kk = jnp.einsum('bhvk,bhk->bhv', state, kk_t)
        state = (state * w_t[:, :, None, :] - s_kk[..., None] * (kk_t * a_t)[:, :, None, :]
                 + v_t[..., None] * k_t[:, :, None, :])
        return state, jnp.einsum('bhvk,bhk->bhv', state, r_t)

    tm = lambda t: t.transpose(1, 0, 2, 3)
    state0 = jnp.zeros((B, H, N, N), jnp.float32)
    _, y = lax.scan(step, state0, (tm(r), tm(jnp.exp(log_w)), tm(k), tm(v), tm(kk), tm(a)))
    y = y.transpose(1, 0, 2, 3)
    mean = jnp.mean(y, axis=-1, keepdims=True)
    var = jnp.mean(jnp.square(y - mean), axis=-1, keepdims=True)
    y = ((y - mean) * lax.rsqrt(var + RWKV_GN_EPS)).reshape(B, S, H * N) * ln_w + ln_b
    bonus = jnp.sum(r * k * r_k.reshape(H, N), axis=-1, keepdims=True) * v
    y = (y + bonus.reshape(B, S, H * N)) * g
    return y.astype(p_rwkv.dtype)


def _gla(q, k, v, alpha_lo, r, alpha_w2, alpha_b, norm_g):
    B, S, _ = q.shape
    H, dk, dv, C = GLA_HEADS, GLA_DK_HEAD, GLA_DV_HEAD, GLA_CHUNK
    nc = S // C
    log_a = jax.nn.log_sigmoid((alpha_lo @ alpha_w2 + alpha_b).astype(jnp.float32)) / GLA_TAU

    def chunks(t, dh):
        return t.astype(jnp.float32).reshape(B, nc, C, H, dh).transpose(1, 0, 3, 2, 4)

    qc, kc, vc, gc = chunks(q, dk) * dk ** -0.5, chunks(k, dk), chunks(v, dv), chunks(log_a, dk)
    causal = np.tril(np.ones((C, C), dtype=bool))[:, :, None]

    def step(state, inp):
        q_t, k_t, v_t, g_t = inp
        b = jnp.cumsum(g_t, axis=2)
        decay = jnp.exp(jnp.where(causal, b[:, :, :, None, :] - b[:, :, None, :, :], -jnp.inf))
        attn = jnp.einsum('bhid,bhjd,bhijd->bhij', q_t, k_t, decay)
        b_last = b[:, :, -1:, :]
        o = attn @ v_t + (q_t * jnp.exp(b)) @ state
        state = (state * jnp.exp(b_last).transpose(0, 1, 3, 2)
                 + jnp.einsum('bhjd,bhje->bhde', k_t * jnp.exp(b_last - b), v_t))
        return state, o

    _, o = lax.scan(step, jnp.zeros((B, H, dk, dv), jnp.float32), (qc, kc, vc, gc))
    o = o.transpose(1, 0, 3, 2, 4).reshape(B, S, H, dv)
    o = o * lax.rsqrt(jnp.mean(o * o, axis=-1, keepdims=True) + NORM_EPS)
    o = o.reshape(B, S, H * dv) * norm_g * jax.nn.silu(r.astype(jnp.float32))
    return o.astype(q.dtype)


def _mla(c_q, c_kv, k_rope, q_norm, w_uq, kv_norm, w_ukv, positions):
    B, S, _ = c_q.shape
    H = MLA_HEADS
    q = (_rmsnorm(c_q, q_norm) @ w_uq).reshape(B, S, H, MLA_NOPE + MLA_ROPE)
    q = jnp.concatenate([q[..., :MLA_NOPE], _rope(q[..., MLA_NOPE:], positions)], axis=-1)
    kv = (_rmsnorm(c_kv, kv_norm) @ w_ukv).reshape(B, S, H, MLA_NOPE + MLA_V)
    k_pe = jnp.broadcast_to(_rope(k_rope[:, :, None, :], positions), (B, S, H, MLA_ROPE))
    k = jnp.concatenate([kv[..., :MLA_NOPE], k_pe], axis=-1)
    v = kv[..., MLA_NOPE:]
    o = _causal_attention_blocked(q.transpose(0, 2, 1, 3), k.transpose(0, 2, 1, 3), v.transpose(0, 2, 1, 3),
                                  (MLA_NOPE + MLA_ROPE) ** -0.5)
    return o.transpose(0, 2, 1, 3).reshape(B, S, H * MLA_V)


def _token_mixing(u, w_in, w_branch, w_out, cmp_pos, cmp_w1, cmp_w2,
                  rwkv_mu, rwkv_w0, rwkv_w_w2, rwkv_a0, rwkv_a_w2, rwkv_g_w2, rwkv_k_k, rwkv_k_a,
                  rwkv_r_k, rwkv_ln_w, rwkv_ln_b, gla_alpha_w2, gla_alpha_b, gla_norm_g,
                  mla_q_norm, mla_w_uq, mla_kv_norm, mla_w_ukv):
    B, S, _ = u.shape
    positions = jnp.arange(S)
    proj = u @ w_in
    p_nsa, p_rwkv, p_gla, p_mla, p_gate = _split(proj, BRANCH_SPLITS)
    nq_, kc_, vc_, ks_, vs_, kw_, vw_, gl_ = _split(p_nsa, NSA_SPLITS)
    kvh = lambda t: t.reshape(B, S, NSA_KV_HEADS, HEAD_DIM)
    y_nsa = _nsa(nq_.reshape(B, S, NSA_HEADS, HEAD_DIM), kvh(kc_), kvh(vc_), kvh(ks_), kvh(vs_),
                 kvh(kw_), kvh(vw_), gl_, cmp_pos, cmp_w1, cmp_w2, positions)
    y_rwkv = _rwkv7(p_rwkv, rwkv_mu, rwkv_w0, rwkv_w_w2, rwkv_a0, rwkv_a_w2, rwkv_g_w2,
                    rwkv_k_k, rwkv_k_a, rwkv_r_k, rwkv_ln_w, rwkv_ln_b)
    gq, gk, gv, ga, gr = _split(p_gla, GLA_SPLITS)
    y_gla = _gla(gq, gk, gv, ga, gr, gla_alpha_w2, gla_alpha_b, gla_norm_g)
    mq, mkv, mkr = _split(p_mla, MLA_SPLITS)
    y_mla = _mla(mq, mkv, mkr, mla_q_norm, mla_w_uq, mla_kv_norm, mla_w_ukv, positions)
    gates = jax.nn.sigmoid(p_gate.reshape(B, S, N_BRANCHES, D_MODEL))
    ys = (y_nsa, y_rwkv, y_gla, y_mla)
    merged = gates[:, :, 0] * (ys[0] @ w_branch[0])
    for i in range(1, N_BRANCHES):
        merged = merged + gates[:, :, i] * (ys[i] @ w_branch[i])
    return merged @ w_out


def setup_inputs(seed: int = 0) -> dict:
    key = jax.random.key(seed)
    ks = iter(jax.random.split(key, 40))
    nrm = lambda shape, s: jax.random.normal(next(ks), shape, jnp.float32) * s
    gain = lambda shape: 1.0 + nrm(shape, 0.02)
    L, D = DEPTH, D_MODEL
    return {
        'x': nrm((BATCH, SEQ, D), 1.0),
        'c': nrm((BATCH, D), 1.0),
        'ada_w': nrm((L, N_SUBLAYERS, D, 3 * D), 0.5 * D ** -0.5),
        'ada_b': nrm((L, N_SUBLAYERS, 3 * D), 0.01),
        'pre_g': gain((L, N_SUBLAYERS, D)),
        'post_g': gain((L, N_SUBLAYERS, D)),
        'ffn_wg': nrm((L, 2, D, D_FF), D ** -0.5),
        'ffn_wu': nrm((L, 2, D, D_FF), D ** -0.5),
        'ffn_wd': nrm((L, 2, D_FF, D), D_FF ** -0.5),
        'mix_w_in': nrm((L, D, PROJ_IN), D ** -0.5),
        'mix_w_branch': nrm((L, N_BRANCHES, MIX_WIDTH, D), MIX_WIDTH ** -0.5),
        'mix_w_out': nrm((L, D, D), D ** -0.5),
        'nsa_cmp_pos': nrm((L, 2, CMP_BLOCK, HEAD_DIM), 0.02),
        'nsa_cmp_w1': nrm((L, 2, CMP_BLOCK * HEAD_DIM, CMP_HIDDEN), (CMP_BLOCK * HEAD_DIM) ** -0.5),
        'nsa_cmp_w2': nrm((L, 2, CMP_HIDDEN, HEAD_DIM), CMP_HIDDEN ** -0.5),
        'rwkv_mu': jax.random.uniform(next(ks), (L, sum(RWKV_SPLITS)), jnp.float32),
        'rwkv_w0': nrm((L, RWKV_WIDTH), 0.5),
        'rwkv_w_w2': nrm((L, RWKV_DECAY_LORA, RWKV_WIDTH), RWKV_DECAY_LORA ** -0.5),
        'rwkv_a0': nrm((L, RWKV_WIDTH), 0.1),
        'rwkv_a_w2': nrm((L, RWKV_AAA_LORA, RWKV_WIDTH), 0.5 * RWKV_AAA_LORA ** -0.5),
        'rwkv_g_w2': nrm((L, RWKV_GATE_LORA, RWKV_WIDTH), RWKV_GATE_LORA ** -0.5),
        'rwkv_k_k': 0.85 + nrm((L, RWKV_WIDTH), 0.02),
        'rwkv_k_a': gain((L, RWKV_WIDTH)),
        'rwkv_r_k': nrm((L, RWKV_WIDTH), 0.1),
        'rwkv_ln_w': gain((L, RWKV_WIDTH)),
        'rwkv_ln_b': nrm((L, RWKV_WIDTH), 0.01),
        'gla_alpha_w2': nrm((L, GLA_GATE_LORA, GLA_DK), GLA_GATE_LORA ** -0.5),
        'gla_alpha_b': 1.0 + nrm((L, GLA_DK), 0.5),
        'gla_norm_g': gain((L, GLA_DV)),
        'mla_q_norm': gain((L, MLA_Q_RANK)),
        'mla_w_uq': nrm((L, MLA_Q_RANK, MLA_HEADS * (MLA_NOPE + MLA_ROPE)), MLA_Q_RANK ** -0.5),
        'mla_kv_norm': gain((L, MLA_KV_RANK)),
        'mla_w_ukv': nrm((L, MLA_KV_RANK, MLA_HEADS * (MLA_NOPE + MLA_V)), MLA_KV_RANK ** -0.5),
    }


def reference(x, c, ada_w, ada_b, pre_g, post_g, ffn_wg, ffn_wu, ffn_wd, mix_w_in, mix_w_branch,
              mix_w_out, nsa_cmp_pos, nsa_cmp_w1, nsa_cmp_w2, rwkv_mu, rwkv_w0, rwkv_w_w2, rwkv_a0,
              rwkv_a_w2, rwkv_g_w2, rwkv_k_k, rwkv_k_a, rwkv_r_k, rwkv_ln_w, rwkv_ln_b, gla_alpha_w2,
              gla_alpha_b, gla_norm_g, mla_q_norm, mla_w_uq, mla_kv_norm, mla_w_ukv):
    for l in range(DEPTH):
        x = _sandwich(x, c, lambda u: _swiglu(u, ffn_wg[l, 0], ffn_wu[l, 0], ffn_wd[l, 0]),
                      ada_w[l, 0], ada_b[l, 0], pre_g[l, 0], post_g[l, 0], MACARON_WEIGHT)
        x = _sandwich(x, c, lambda u: _token_mixing(
            u, mix_w_in[l], mix_w_branch[l], mix_w_out[l], nsa_cmp_pos[l], nsa_cmp_w1[l], nsa_cmp_w2[l],
            rwkv_mu[l], rwkv_w0[l], rwkv_w_w2[l], rwkv_a0[l], rwkv_a_w2[l], rwkv_g_w2[l], rwkv_k_k[l],
            rwkv_k_a[l], rwkv_r_k[l], rwkv_ln_w[l], rwkv_ln_b[l], gla_alpha_w2[l], gla_alpha_b[l],
            gla_norm_g[l], mla_q_norm[l], mla_w_uq[l], mla_kv_norm[l], mla_w_ukv[l]),
            ada_w[l, 1], ada_b[l, 1], pre_g[l, 1], post_g[l, 1], 1.0)
        x = _sandwich(x, c, lambda u: _swiglu(u, ffn_wg[l, 1], ffn_wu[l, 1], ffn_wd[l, 1]),
                      ada_w[l, 2], ada_b[l, 2], pre_g[l, 2], post_g[l, 2], MACARON_WEIGHT)
    return x
```

```python
import numpy as np
from contextlib import ExitStack
import concourse.bass as bass
import concourse.mybir as mybir
from concourse.bass_utils import run_bass_kernel_spmd

F32 = mybir.dt.float32
BF16 = mybir.dt.bfloat16
AF = mybir.ActivationFunctionType
ALU = mybir.AluOpType
AX = mybir.AxisListType

D = 2048
KC = D // 128
DFF = 5632
NFF = DFF // 128
EPS = 1e-6
NCORES = 8


class Trk:
    __slots__ = ("w", "r")

    def __init__(self):
        self.w = None
        self.r = {}


class Buf:
    def __init__(self, t, trk=None, psum=False):
        self.t = t
        self.trk = trk or Trk()
        self.psum = psum

    def __getitem__(self, idx):
        return self.t[idx]


class K:
    def __init__(self, nc, es, n_dma_sems=40):
        self.nc = nc
        self.es = es
        self.eng = {"pe": nc.tensor, "dve": nc.vector, "act": nc.scalar, "pool": nc.gpsimd, "sp": nc.sync}
        self.esem = {k: es.enter_context(nc.semaphore("es_" + k)) for k in self.eng}
        self.ecnt = {k: 0 for k in self.eng}
        self.waited = {k: {} for k in self.eng}
        self.dsems = [[es.enter_context(nc.semaphore("ds%d" % i)), 0] for i in range(n_dma_sems)]
        self.dnext = 0
        self.uid = 0
        self.ninstr = 0
        self.banks = [Buf(es.enter_context(nc.psum_tensor("bank%d" % i, [128, 512], F32)), psum=True) for i in range(7)]
        self.bankb = Buf(es.enter_context(nc.psum_tensor("bankb", [128, 1024], BF16)), psum=True)

    def sb(self, shape, dt, name=None, es=None):
        self.uid += 1
        return Buf((es or self.es).enter_context(self.nc.sbuf_tensor("sb%d" % self.uid, list(shape), dt)))

    def rot(self, n, shape, dt, es=None):
        return Rot([self.sb(shape, dt, es=es) for _ in range(n)])

    def barrier(self):
        deps = {id(self.esem[e]): (self.esem[e], self.ecnt[e]) for e in self.eng if self.ecnt[e] > 0}
        for s, c in self.dsems:
            if c > 0:
                deps[id(s)] = (s, c)
        for e in self.eng:
            self._wait(e, dict(deps))

    def ps(self, shape, dt=F32, name=None):
        self.uid += 1
        return Buf(self.es.enter_context(self.nc.psum_tensor(name or "ps%d" % self.uid, list(shape), dt)))

    def dram(self, name, shape, dt, kind="Internal"):
        return Buf(self.nc.dram_tensor(name, list(shape), dt, kind=kind).ap())

    def _deps(self, R, W):
        deps = {}

        def add(d):
            if d is None:
                return
            s, v = d
            k = id(s)
            if k not in deps or deps[k][1] < v:
                deps[k] = (s, v)

        for b in R:
            add(b.trk.w)
        for b in W:
            add(b.trk.w)
            for d in b.trk.r.values():
                add(d)
        return deps

    def _wait(self, e, deps, skip_sem=None):
        h = self.eng[e]
        wd = self.waited[e]
        for k, (s, v) in deps.items():
            if skip_sem is not None and s is skip_sem:
                continue
            if wd.get(k, 0) < v:
                h.wait_ge(s, v)
                wd[k] = v

    def _commit(self, d, R, W):
        for b in W:
            b.trk.w = d
            b.trk.r = {}
        for b in R:
            b.trk.r[id(d[0])] = d

    def op(self, e, fn, R=(), W=()):
        if any(b.psum for b in R):
            W = list(W) + [b for b in R if b.psum]
            R = [b for b in R if not b.psum]
        deps = self._deps(R, W)
        self._wait(e, deps, skip_sem=self.esem["pe"] if e == "pe" else None)
        ins = fn(self.eng[e])
        self.ecnt[e] += 1
        ins.then_inc(self.esem[e], 1)
        self.ninstr += 1
        self._commit((self.esem[e], self.ecnt[e]), R, W)

    def dma(self, q, out_ap, in_ap, R=(), W=(), **kw):
        deps = self._deps(R, W)
        slot = self.dsems[self.dnext]
        self.dnext = (self.dnext + 1) % len(self.dsems)
        if slot[1] > 0:
            deps[id(slot[0])] = (slot[0], slot[1])
        self._wait(q, deps)
        ins = self.eng[q].dma_start(out=out_ap, in_=in_ap, **kw)
        slot[1] += 16
        ins.then_inc(slot[0], 16)
        self.ninstr += 1
        self._commit((slot[0], slot[1]), R, W)

    def finish(self, bufs, e="sp"):
        deps = self._deps(bufs, ())
        self._wait(e, deps)

    def mm(self, out, lhsT, rhs, start, stop, R, W):
        self.op("pe", lambda h: h.matmul(out, lhsT=lhsT, rhs=rhs, start=start, stop=stop), R=R, W=W)

    def act(self, out, in_, func, R, W, e="act", **kw):
        self.op(e, lambda h: h.activation(out=out, in_=in_, func=func, **kw), R=R, W=W)

    def tt(self, e, out, in0, in1, op, R, W):
        self.op(e, lambda h: h.tensor_tensor(out=out, in0=in0, in1=in1, op=op), R=R, W=W)

    def ts(self, e, out, in0, s1, s2, op0, op1, R, W):
        if op1 is None:
            self.op(e, lambda h: h.tensor_scalar(out=out, in0=in0, scalar1=s1, scalar2=None, op0=op0), R=R, W=W)
        else:
            self.op(e, lambda h: h.tensor_scalar(out=out, in0=in0, scalar1=s1, scalar2=s2, op0=op0, op1=op1), R=R, W=W)

    def stt(self, out, in0, scalar, in1, op0, op1, R, W):
        self.op("dve", lambda h: h.scalar_tensor_tensor(out=out, in0=in0, scalar=scalar, in1=in1, op0=op0, op1=op1), R=R, W=W)

    def copy(self, e, out, in_, R, W):
        if e == "act":
            self.op(e, lambda h: h.copy(out=out, in_=in_), R=R, W=W)
        else:
            self.op(e, lambda h: h.tensor_copy(out=out, in_=in_), R=R, W=W)

    def cast(self, out, in_, R, W):
        self.ncast = getattr(self, "ncast", 0) + 1
        self.copy("dve" if self.ncast % 2 else "act", out, in_, R, W)

    def memset(self, e, ap, val, W):
        self.op(e, lambda h: h.memset(ap, val), W=W)


class Rot:
    def __init__(self, bufs):
        self.bufs = bufs
        self.i = 0

    def next(self):
        b = self.bufs[self.i]
        self.i = (self.i + 1) % len(self.bufs)
        return b


TB = 512


class NormCtx:
    def __init__(self, k, es, vec, rw, post_bank=None):
        self.k = k
        self.big = k.sb([128, KC, TB], F32, es=es)
        self.xin = k.rot(2, [128, 4, TB], F32, es=es)
        self.sq = k.rot(2, [128, 4, TB], BF16, es=es)
        self.tmp = k.rot(2, [128, TB], F32, es=es)
        self.xr = k.rot(2, [128, TB], F32, es=es)
        self.ob = k.rot(2, [128, TB], F32, es=es)
        self.rstd = k.sb([128, TB], F32, es=es)
        self.rt = k.sb([128, TB], F32, es=es)
        self.rstd_pre = k.sb([128, TB], F32, es=es)
        self.rt_pre = k.sb([128, TB], F32, es=es)
        self.ones = k.sb([128, 128], BF16, es=es)
        self.vt = k.sb([128, 5, KC], F32, es=es)
        self.sc = k.sb([128, KC], F32, es=es)
        self.gp = k.sb([128, KC], F32, es=es)
        self.epsb = k.sb([128, 1], F32, es=es)
        self.ps_pre = k.banks[0]
        self.ps_stat = post_bank if post_bank is not None else k.banks[0]
        k.memset("dve", self.ones[:], 1.0, W=[self.ones])
        k.memset("dve", self.epsb[:], EPS, W=[self.epsb])
        k.dma("sp", self.vt[:], vec.t[:, :, :], R=[vec], W=[self.vt])
        k.stt(self.sc[:], self.vt[:, 1, :], 1.0, self.vt[:, 3, :], ALU.add, ALU.mult, R=[self.vt], W=[self.sc])
        k.stt(self.gp[:], self.vt[:, 2, :], float(rw), self.vt[:, 4, :], ALU.mult, ALU.mult, R=[self.vt], W=[self.gp])

    def rstd_from(self, ps, dim, rt=None, rstd=None):
        k = self.k
        rt = rt or self.rt
        rstd = rstd or self.rstd
        k.act(rt[:], ps[:], AF.Sqrt, R=[ps, self.epsb], W=[rt], bias=self.epsb[:], scale=1.0 / dim)
        k.op("dve", lambda h: h.reciprocal(out=rstd[:], in_=rt[:]), R=[rt], W=[rstd])

    def prenorm(self, xT, xTv, t0, uT):
        k = self.k
        for q in range(4):
            xp = self.xin.next()
            k.dma("pool", xp[:], xTv[:, 4 * q:4 * q + 4, t0:t0 + TB], R=[xT], W=[xp])
            s = self.sq.next()
            k.act(s[:], xp[:], AF.Square, R=[xp], W=[s])
            for i in range(4):
                kc = 4 * q + i
                k.mm(self.ps_pre[:], self.ones[:], s[:, i, :], kc == 0, kc == KC - 1, R=[self.ones, s], W=[self.ps_pre])
        self.rstd_from(self.ps_pre, D, self.rt_pre, self.rstd_pre)
        for q in range(4):
            xp = self.xin.next()
            k.dma("pool", xp[:], xTv[:, 4 * q:4 * q + 4, t0:t0 + TB], R=[xT], W=[xp])
            for i in range(4):
                kc = 4 * q + i
                tm = self.tmp.next()
                k.stt(tm[:], xp[:, i, :], self.sc[:, kc:kc + 1], self.rstd_pre[:], ALU.mult, ALU.mult,
                      R=[xp, self.sc, self.rstd_pre], W=[tm])
                k.act(uT[:, kc, :], tm[:], AF.Identity, R=[tm, self.vt], W=[uT], bias=self.vt[:, 0, kc:kc + 1], scale=1.0)

    def take(self, py, dc):
        k = self.k
        k.copy("act", self.big[:, dc, :], py[:], R=[py], W=[self.big])
        s = self.sq.next()
        k.act(s[:, 0, :], py[:], AF.Square, R=[py], W=[s])
        k.mm(self.ps_stat[:], self.ones[:], s[:, 0, :], dc == 0, dc == KC - 1, R=[self.ones, s], W=[self.ps_stat])

    def postnorm(self, xT, xTv, oT, oTv, t0):
        k = self.k
        self.rstd_from(self.ps_stat, D)
        xs = {}
        for kc in range(KC):
            for k2 in range(kc, min(kc + 2, KC)):
                if k2 not in xs:
                    xs[k2] = self.xr.next()
                    k.dma("pool", xs[k2][:], xTv[:, k2, t0:t0 + TB], R=[xT], W=[xs[k2]])
            x_ = xs[kc]
            tm = self.tmp.next()
            k.stt(tm[:], self.big[:, kc, :], self.gp[:, kc:kc + 1], self.rstd[:], ALU.mult, ALU.mult,
                  R=[self.big, self.gp, self.rstd], W=[tm])
            o = self.ob.next()
            k.tt("pool", o[:], tm[:], x_[:], ALU.add, R=[tm, x_], W=[o])
            k.dma("pool", oTv[:, kc, t0:t0 + TB], o[:], R=[o], W=[oT])


def fview(b):
    return b.t.rearrange("(kc p) t -> p kc t", p=128)


def emit_ffn(k, xT, wg, wu, wd, vec, oT, T, rw):
    xTv, oTv = fview(xT), fview(oT)
    with ExitStack() as es:
        N = NormCtx(k, es, vec, rw, post_bank=k.banks[6])
        uTs = [k.sb([128, KC, TB], BF16, es=es) for _ in range(2)]
        aT = k.sb([128, NFF, TB], BF16, es=es)
        wgb = k.rot(3, [128, KC, 128], BF16, es=es)
        wub = k.rot(3, [128, KC, 128], BF16, es=es)
        wdb = k.rot(2, [128, NFF, 128], BF16, es=es)
        sgb = k.rot(2, [128, TB], F32, es=es)
        ps_g, ps_u, ps_y = Rot(k.banks[1:3]), Rot(k.banks[3:4]), Rot(k.banks[4:6])
        NP = T // TB
        N.prenorm(xT, xTv, 0, uTs[0])
        for p in range(NP):
            t0 = p * TB
            uT = uTs[p % 2]
            for j in range(NFF):
                wb = []
                for (wsrc, pool) in ((wg, wgb), (wu, wub)):
                    b = pool.next()
                    k.dma("sp", b[:, :, :].rearrange("p a b -> p (a b)"), wsrc.t[j], R=[wsrc], W=[b])
                    wb.append(b)
                pg, pu = ps_g.next(), ps_u.next()
                for kc in range(KC):
                    k.mm(pg[:], wb[0][:, kc, :], uT[:, kc, :], kc == 0, kc == KC - 1, R=[wb[0], uT], W=[pg])
                for kc in range(KC):
                    k.mm(pu[:], wb[1][:, kc, :], uT[:, kc, :], kc == 0, kc == KC - 1, R=[wb[1], uT], W=[pu])
                sg = sgb.next()
                k.act(sg[:], pg[:], AF.Silu, R=[pg], W=[sg])
                k.tt("dve", aT[:, j, :], sg[:], pu[:], ALU.mult, R=[sg, pu], W=[aT])
            if p + 1 < NP:
                N.prenorm(xT, xTv, t0 + TB, uTs[(p + 1) % 2])
            for dc in range(KC):
                b = wdb.next()
                k.dma("sp", b[:, :, :].rearrange("p a b -> p (a b)"), wd.t[dc], R=[wd], W=[b])
                py = ps_y.next()
                for j in range(NFF):
                    k.mm(py[:], b[:, j, :], aT[:, j, :], j == 0, j == NFF - 1, R=[b, aT], W=[py])
                N.take(py, dc)
            N.postnorm(xT, xTv, oT, oTv, t0)
        k.barrier()


def emit_convert(k, w, wb, piece):
    n, _, R_ = w.t.shape
    for c in range(n):
        for r0 in range(0, R_, piece):
            k.dma("pool", wb.t[c][:, r0:r0 + piece], w.t[c][:, r0:r0 + piece], R=[w], W=[wb])


def emit_merge(k, xT, yT, wgt, wbr, wout, vec, oT, T, y_tok=None):
    xTv, oTv = fview(xT), fview(oT)
    if y_tok is None:
        yTv = yT.t.rearrange("(c p) t -> p c t", p=128)
    with ExitStack() as es:
        N = NormCtx(k, es, vec, 1.0, post_bank=k.banks[4])
        uTs = [k.sb([128, KC, TB], BF16, es=es) for _ in range(2)]
        yb = k.sb([128, 16, TB], BF16, es=es)
        mT = k.sb([128, KC, TB], BF16, es=es)
        ystg = k.rot(2, [128, 2, TB], F32, es=es)
        if y_tok is not None:
            ytk_b = k.rot(2, [128, 2048], BF16, es=es)
            identb = k.sb([128, 128], BF16, es=es)
            k.memset("pool", identb[:], 1.0, W=[identb])
            k.op("pool", lambda h: h.affine_select(out=identb[:], in_=identb[:], pattern=[[-1, 128]], compare_op=ALU.is_equal,
                                                  fill=0.0, base=0, channel_multiplier=1), R=[identb], W=[identb])
        wgb = k.rot(4, [128, KC, 128], BF16, es=es)
        wbb = k.rot(2, [128, 16, 128], BF16, es=es)
        wob = k.rot(2, [128, KC, 128], BF16, es=es)
        sgb = k.rot(2, [128, TB], F32, es=es)
        mb = k.rot(2, [128, TB], F32, es=es)
        macc = k.rot(2, [128, TB], F32, es=es)
        ps_g, ps_b, ps_y = Rot(k.banks[1:3]), Rot(k.banks[3:4]), Rot(k.banks[5:7])
        N.prenorm(xT, xTv, 0, uTs[0])
        for p in range(T // TB):
            t0 = p * TB
            uT = uTs[p % 2]
            if y_tok is None:
                for c2 in range(8):
                    st = ystg.next()
                    k.dma("sp", st[:], yTv[:, 2 * c2:2 * c2 + 2, t0:t0 + TB], R=[yT], W=[st])
                    k.cast(yb[:, 2 * c2:2 * c2 + 2, :], st[:], R=[st], W=[yb])
            else:
                for sub in range(4):
                    tk = t0 + sub * 128
                    ytb = ytk_b.next()
                    for hf in range(2):
                        st = ystg.next()
                        k.dma("sp", st[:, :, :].rearrange("p a b -> p (a b)"), y_tok.t[tk:tk + 128, hf * 1024:(hf + 1) * 1024], R=[y_tok], W=[st])
                        k.cast(ytb[:, hf * 1024:(hf + 1) * 1024], st[:, :, :].rearrange("p a b -> p (a b)"), R=[st], W=[ytb])
                    for c4 in range(4):
                        pb = k.bankb
                        for i in range(4):
                            c = 4 * c4 + i
                            k.op("pe", lambda h: h.transpose(out=pb[:, i * 128:(i + 1) * 128], in_=ytb[:, c * 128:(c + 1) * 128],
                                                             identity=identb[:, :]), R=[ytb, identb], W=[pb])
                        k.copy("act", yb[:, 4 * c4:4 * c4 + 4, sub * 128:(sub + 1) * 128],
                               pb[:, 0:512].rearrange("p (c t) -> p c t", t=128), R=[pb], W=[yb])
            for dc in range(KC):
                wbt = wbb.next()
                k.dma("sp", wbt[:, :, :].rearrange("p a b -> p (a b)"), wbr.t[dc], R=[wbr], W=[wbt])
                acc = macc.next()
                for br in range(4):
                    wg_ = wgb.next()
                    k.dma("sp", wg_[:, :, :].rearrange("p a b -> p (a b)"), wgt.t[br * KC + dc], R=[wgt], W=[wg_])
                    pg, pb = ps_g.next(), ps_b.next()
                    for kc in range(KC):
                        k.mm(pg[:], wg_[:, kc, :], uT[:, kc, :], kc == 0, kc == KC - 1, R=[wg_, uT], W=[pg])
                    for kc in range(4):
                        k.mm(pb[:], wbt[:, br * 4 + kc, :], yb[:, br * 4 + kc, :], kc == 0, kc == 3, R=[wbt, yb], W=[pb])
                    sg = sgb.next()
                    k.act(sg[:], pg[:], AF.Sigmoid, R=[pg], W=[sg])
                    if br == 0:
                        k.tt("dve", acc[:], sg[:], pb[:], ALU.mult, R=[sg, pb], W=[acc])
                    else:
                        m_ = mb.next()
                        k.tt("dve", m_[:], sg[:], pb[:], ALU.mult, R=[sg, pb], W=[m_])
                        if br < 3:
                            k.tt("pool", acc[:], acc[:], m_[:], ALU.add, R=[acc, m_], W=[acc])
                        else:
                            k.tt("pool", mT[:, dc, :], acc[:], m_[:], ALU.add, R=[acc, m_], W=[mT])
            if p + 1 < T // TB:
                N.prenorm(xT, xTv, t0 + TB, uTs[(p + 1) % 2])
            for oc in range(KC):
                wo_ = wob.next()
                k.dma("sp", wo_[:, :, :].rearrange("p a b -> p (a b)"), wout.t[oc], R=[wout], W=[wo_])
                py = ps_y.next()
                for dc in range(KC):
                    k.mm(py[:], wo_[:, dc, :], mT[:, dc, :], dc == 0, dc == KC - 1, R=[wo_, mT], W=[py])
                N.take(py, oc)
            N.postnorm(xT, xTv, oT, oTv, t0)
        k.barrier()


def chunk_w(w, nc_):
    w = np.asarray(w, dtype=np.float32)
    K_, N_ = w.shape
    npad = (-N_) % nc_
    if npad:
        w = np.concatenate([w, np.zeros((K_, npad), np.float32)], axis=1)
    return np.ascontiguousarray(w.reshape(K_ // 128, 128, (N_ + npad) // nc_, nc_).transpose(2, 1, 0, 3))


def new_prog():
    nc = bass.Bass("TRN2", target_bir_lowering=False)
    es = ExitStack()
    k = K(nc, es)
    return nc, es, k


def acc_view(acc):
    return acc[:, :].rearrange("p (s c) -> p s c", c=128)


def attn_chunk(k, q_ap, qbufs, pairs, acc, W, scale, sbanks, ptr, q0, acc2=None, W2=0):
    state = {"first": True}

    def scores(pr):
        nk = pr["nk"]
        ps = sbanks.next()
        k.mm(ps[0:nk, :], pr["kT"], q_ap, True, pr.get("extra") is None, R=pr["bufs"] + qbufs, W=[ps])
        if pr.get("extra") is not None:
            el, er, eb = pr["extra"]
            k.mm(ps[0:nk, :], el, er, False, True, R=eb, W=[ps])
        pt = ptr.next()
        k.act(pt[0:nk, :], ps[0:nk, :], AF.Exp, R=[ps], W=[pt], scale=scale)
        if pr.get("mask") is not None:
            base, cm, step = pr["mask"]
            k.op("pool", lambda h: h.affine_select(out=pt[0:nk, :], in_=pt[0:nk, :], pattern=[[step, 512]],
                                                  compare_op=ALU.is_ge, fill=0.0, base=base, channel_multiplier=cm),
                 R=[pt], W=[pt])
        return pt

    def pv(pr, pt):
        nk = pr["nk"]
        for sub in range(4):
            if pr.get("kpos0") is not None and pr["kpos0"] > q0 + sub * 128 + 127:
                continue
            if sub in pr.get("skip", ()):
                continue
            c0 = sub * 128
            fst = state["first"]
            k.op("pe", lambda h: h.matmul(acc[:, c0:c0 + W], lhsT=pt[0:nk, c0:c0 + 128], rhs=pr["v"], start=fst, stop=True,
                                          skip_group_check=True), R=[pt] + pr["bufs"], W=[acc])
            if acc2 is not None:
                k.op("pe", lambda h: h.matmul(acc2[:, c0:c0 + W2], lhsT=pt[0:nk, c0:c0 + 128], rhs=pr["v2"], start=fst, stop=True,
                                              skip_group_check=True), R=[pt] + pr["bufs"], W=[acc2])
            state["first"] = False

    prev = None
    for pr in pairs:
        pt = scores(pr)
        if prev is not None:
            pv(*prev)
        prev = (pr, pt)
    if prev is not None:
        pv(*prev)


def rstd_calc(k, ps, nparts, dim, rt, rstd, epsb, eps=EPS):
    k.act(rt[0:nparts, :], ps[0:nparts, :], AF.Sqrt, R=[ps, epsb], W=[rt], bias=epsb[0:nparts, :], scale=1.0 / dim)
    k.op("dve", lambda h: h.reciprocal(out=rstd[0:nparts, :], in_=rt[0:nparts, :]), R=[rt], W=[rstd])


def emit_mla(k, I, y, S):
    nkb, nqc = S // 128, S // 512
    SC = 96 ** -0.5
    with ExitStack() as es:
        QT = k.sb([96, 4, S], BF16, es=es)
        KT = k.sb([96, 4, S], BF16, es=es)
        Va = k.sb([128, 4, nkb, 65], BF16, es=es)
        ysb = k.sb([128, nkb, 256], F32, es=es)
        ones = k.sb([128, 128], BF16, es=es)
        epsb = k.sb([128, 1], F32, es=es)
        k.memset("dve", ones[:], 1.0, W=[ones])
        k.memset("dve", epsb[:], EPS, W=[epsb])
        k.memset("pool", Va[:, :, :, 64:65], 1.0, W=[Va])
        with ExitStack() as e1:
            wuq = k.sb([128, 3, 384], BF16, es=e1)
            wuqs = k.sb([128, 3, 384], BF16, es=e1)
            wk = k.sb([128, 256], BF16, es=e1)
            wv = k.sb([128, 256], BF16, es=e1)
            qn = k.sb([128, 3], F32, es=e1)
            kvn = k.sb([128, 1], F32, es=e1)
            wst = k.rot(2, [128, 3, 384], F32, es=e1)
            for dst, src in ((wuq, I["wuq"]), (wuqs, I["wuqs"])):
                st = wst.next()
                k.dma("sp", st[:], src.t.rearrange("(kc p) f -> p kc f", p=128), R=[src], W=[st])
                k.copy("pool", dst[:], st[:], R=[st], W=[dst])
            for dst, src in ((wk, I["wk"]), (wv, I["wv"])):
                st = wst.next()
                k.dma("sp", st[:, 0, 0:256], src.t[:, :], R=[src], W=[st])
                k.copy("pool", dst[:], st[:, 0, 0:256], R=[st], W=[dst])
            k.dma("sp", qn[:], I["qn"].t[:, :], R=[I["qn"]], W=[qn])
            k.dma("sp", kvn[:], I["kvn"].t[:, :], R=[I["kvn"]], W=[kvn])
            cqb = k.rot(2, [128, 3, TB], F32, es=e1)
            sq = k.rot(2, [128, 3, TB], BF16, es=e1)
            cqn = k.rot(2, [128, 3, TB], BF16, es=e1)
            ckb = k.rot(2, [128, TB], F32, es=e1)
            ckn = k.rot(2, [128, TB], BF16, es=e1)
            ctab = k.rot(2, [96, TB], F32, es=e1)
            stab = k.rot(2, [96, TB], F32, es=e1)
            krb = k.rot(2, [96, TB], F32, es=e1)
            krsb = k.rot(2, [96, TB], F32, es=e1)
            t1r = k.rot(3, [96, TB], F32, es=e1)
            t2r = k.rot(3, [96, TB], F32, es=e1)
            rt = k.sb([128, TB], F32, es=e1)
            rstd = k.sb([128, TB], F32, es=e1)
            rstd2 = k.sb([128, TB], F32, es=e1)
            psr = Rot(k.banks[1:7])
            cqv = I["cqT"].t.rearrange("(kc p) t -> p kc t", p=128)
            for blk in range(S // TB):
                t0 = blk * TB
                cq, ck, ct, stb, kr, krs = cqb.next(), ckb.next(), ctab.next(), stab.next(), krb.next(), krsb.next()
                k.dma("sp", cq[:], cqv[:, :, t0:t0 + TB], R=[I["cqT"]], W=[cq])
                k.dma("sp", ck[:], I["ckvT"].t[:, t0:t0 + TB], R=[I["ckvT"]], W=[ck])
                k.dma("sp", ct[:], I["c96"].t[:, t0:t0 + TB], R=[I["c96"]], W=[ct])
                k.dma("sp", stb[:], I["s96"].t[:, t0:t0 + TB], R=[I["s96"]], W=[stb])
                if "krp" in I:
                    k.dma("sp", kr[:], I["krp"].t[:, t0:t0 + TB], R=[I["krp"]], W=[kr])
                    k.dma("sp", krs[:], I["krsp"].t[:, t0:t0 + TB], R=[I["krsp"]], W=[krs])
                else:
                    k.dma("sp", kr[64:96, :], I["krT"].t[:, t0:t0 + TB], R=[I["krT"]], W=[kr])
                    k.dma("sp", krs[64:80, :], I["krT"].t[16:32, t0:t0 + TB], R=[I["krT"]], W=[krs])
                    k.dma("sp", krs[80:96, :], I["krT"].t[0:16, t0:t0 + TB], R=[I["krT"]], W=[krs])
                s_ = sq.next()
                k.act(s_[:], cq[:], AF.Square, R=[cq], W=[s_])
                ps0 = k.banks[0]
                for kc in range(3):
                    k.mm(ps0[:], ones[:], s_[:, kc, :], kc == 0, kc == 2, R=[ones, s_], W=[ps0])
                rstd_calc(k, ps0, 128, 384, rt, rstd, epsb)
                cn = cqn.next()
                for kc in range(3):
                    k.stt(cn[:, kc, :], cq[:, kc, :], qn[:, kc:kc + 1], rstd[:], ALU.mult, ALU.mult, R=[cq, qn, rstd], W=[cn])
                for h in range(4):
                    p1, p2 = psr.next(), psr.next()
                    for kc in range(3):
                        k.mm(p1[0:96, :], wuq[:, kc, h * 96:(h + 1) * 96], cn[:, kc, :], kc == 0, kc == 2, R=[wuq, cn], W=[p1])
                    for kc in range(3):
                        k.mm(p2[0:96, :], wuqs[:, kc, h * 96:(h + 1) * 96], cn[:, kc, :], kc == 0, kc == 2, R=[wuqs, cn], W=[p2])
                    t1, t2 = t1r.next(), t2r.next()
                    k.tt("dve", t1[:], p1[0:96, :], ct[:], ALU.mult, R=[p1, ct], W=[t1])
                    k.tt("dve", t2[:], p2[0:96, :], stb[:], ALU.mult, R=[p2, stb], W=[t2])
                    k.tt("pool", QT[:, h, t0:t0 + TB], t1[:], t2[:], ALU.add, R=[t1, t2], W=[QT])
                s_ = sq.next()
                k.act(s_[:, 0, :], ck[:], AF.Square, R=[ck], W=[s_])
                k.mm(ps0[:], ones[:], s_[:, 0, :], True, True, R=[ones, s_], W=[ps0])
                rstd_calc(k, ps0, 128, 128, rt, rstd2, epsb)
                kn = ckn.next()
                k.stt(kn[:], ck[:], kvn[:, 0:1], rstd2[:], ALU.mult, ALU.mult, R=[ck, kvn, rstd2], W=[kn])
                for h in range(4):
                    p1 = psr.next()
                    k.mm(p1[0:64, :], wk[:, h * 64:(h + 1) * 64], kn[:], True, True, R=[wk, kn], W=[p1])
                    k.copy("act", KT[0:64, h, t0:t0 + TB], p1[0:64, :], R=[p1], W=[KT])
                t1, t2 = t1r.next(), t2r.next()
                k.tt("dve", t1[64:96, :], kr[64:96, :], ct[64:96, :], ALU.mult, R=[kr, ct], W=[t1])
                k.tt("pool", t2[64:96, :], krs[64:96, :], stb[64:96, :], ALU.mult, R=[krs, stb], W=[t2])
                for h in range(4):
                    k.tt("pool", KT[64:96, h, t0:t0 + TB], t1[64:96, :], t2[64:96, :], ALU.add, R=[t1, t2], W=[KT])
                for sub in range(4):
                    p1 = psr.next()
                    k.mm(p1[:, 0:256], kn[:, sub * 128:(sub + 1) * 128], wv[:], True, True, R=[kn, wv], W=[p1])
                    k.copy("act", Va[:, :, 4 * blk + sub, 0:64], p1[:, 0:256].rearrange("p (h d) -> p h d", d=64), R=[p1], W=[Va])
            k.barrier()
        with ExitStack() as e2:
            ptr = k.rot(3, [128, 512], BF16, es=e2)
            rden = k.rot(2, [128, 4], F32, es=e2)
            sbanks, abanks = Rot(k.banks[0:4]), Rot(k.banks[4:7])
            for h in range(4):
                for qc in range(nqc):
                    q0 = qc * 512
                    acc = abanks.next()
                    pairs = []
                    for kb in range(4 * qc + 4):
                        pairs.append(dict(kT=KT[:, h, kb * 128:(kb + 1) * 128], nk=128, v=Va[:, h, kb, :], bufs=[KT, Va],
                                          mask=(q0 - kb * 128, -1, 1) if kb >= 4 * qc else None, kpos0=kb * 128))
                    attn_chunk(k, QT[:, h, q0:q0 + 512], [QT], pairs, acc, 65, SC, sbanks, ptr, q0)
                    av = acc_view(acc)
                    rd = rden.next()
                    k.op("dve", lambda hh: hh.reciprocal(out=rd[:], in_=av[:, :, 64]), R=[acc], W=[rd])
                    for sub in range(4):
                        o_ap = ysb[:, 4 * qc + sub, h * 64:(h + 1) * 64]
                        if sub % 2 == 0:
                            k.act(o_ap, av[:, sub, 0:64], AF.Copy, R=[acc, rd], W=[ysb], scale=rd[:, sub:sub + 1])
                        else:
                            k.ts("dve", o_ap, av[:, sub, 0:64], rd[:, sub:sub + 1], None, ALU.mult, None, R=[acc, rd], W=[ysb])
            yv = y.t.rearrange("(kb p) f -> p kb f", p=128)
            for q in range(4):
                n4 = nkb // 4
                k.dma("sp", yv[:, q * n4:(q + 1) * n4, :], ysb[:, q * n4:(q + 1) * n4, :], R=[ysb], W=[y])
            k.barrier()


def rope_tables(S, d, rows_before=0):
    inv = 10000.0 ** (-np.arange(0, d, 2, dtype=np.float32) / d)
    ang = np.arange(S, dtype=np.float32)[:, None] * inv[None, :]
    cos, sin = np.cos(ang).T.astype(np.float32), np.sin(ang).T.astype(np.float32)
    c = np.concatenate([np.ones((rows_before, S), np.float32), cos, cos], 0)
    s = np.concatenate([np.zeros((rows_before, S), np.float32), -sin, sin], 0)
    return np.ascontiguousarray(c), np.ascontiguousarray(s)


def mla_inputs(pm, w_uq, w_ukv, q_norm, kv_norm, hh, S):
    cq, ckv, kr = pm[:, :384], pm[:, 384:512], pm[:, 512:544]
    krp = np.zeros((96, S), np.float32)
    krsp = np.zeros((96, S), np.float32)
    krp[64:96] = kr.T
    krsp[64:80], krsp[80:96] = kr.T[16:32], kr.T[0:16]
    wq = w_uq.reshape(384, 8, 96)[:, 4 * hh:4 * hh + 4]
    wqs = wq.copy()
    wqs[:, :, 64:80], wqs[:, :, 80:96] = wq[:, :, 80:96], wq[:, :, 64:80]
    wkv = w_ukv.reshape(128, 8, 128)[:, 4 * hh:4 * hh + 4]
    c96, s96 = rope_tables(S, 32, 64)
    A = np.ascontiguousarray
    return dict(cqT=A(cq.T), ckvT=A(ckv.T), krp=krp, krsp=krsp, wuq=A(wq.reshape(384, 384)), wuqs=A(wqs.reshape(384, 384)),
                wk=A(wkv[:, :, :64].reshape(128, 256)), wv=A(wkv[:, :, 64:].reshape(128, 256)),
                qn=A(q_norm.reshape(3, 128).T), kvn=A(kv_norm.reshape(128, 1)), c96=c96, s96=s96)


MLA_SHAPES = lambda S: dict(cqT=[384, S], ckvT=[128, S], krp=[96, S], krsp=[96, S], wuq=[384, 384], wuqs=[384, 384],
                            wk=[128, 256], wv=[128, 256], qn=[128, 3], kvn=[128, 1], c96=[96, S], s96=[96, S])


def emit_nsa(k, I, y, S):
    nkb, nqc = S // 128, S // 512
    n_cmp = S // 16 - 1
    ncb = (n_cmp + 127) // 128
    n_sel = S // 64
    SC = 0.125
    BIG = 30000.0
    with ExitStack() as es:
        QT = k.sb([64, 4, S], BF16, es=es)
        KsT = k.sb([64, S], BF16, es=es)
        KwT = k.sb([64, S], BF16, es=es)
        Vsa = k.sb([128, nkb, 65], BF16, es=es)
        Vwa = k.sb([128, nkb, 65], BF16, es=es)
        KcmpT = k.sb([64, ncb * 128], BF16, es=es)
        Vca = k.sb([128, ncb, 65], BF16, es=es)
        Ov = k.sb([128, ncb, n_sel], BF16, es=es)
        selb = k.sb([128, nkb, n_sel], F32, es=es)
        gsig = k.sb([128, nkb, 12], F32, es=es)
        Ebig = k.sb([n_sel, S], BF16, es=es)
        ident = k.sb([128, 128], BF16, es=es)
        k.memset("pool", Vsa[:, :, 64:65], 1.0, W=[Vsa])
        k.memset("pool", Vwa[:, :, 64:65], 1.0, W=[Vwa])
        k.memset("pool", Vca[:], 0.0, W=[Vca])
        k.memset("pool", Vca[:, :, 64:65], 1.0, W=[Vca])
        k.memset("pool", KcmpT[:], 0.0, W=[KcmpT])
        k.memset("pool", ident[:], 1.0, W=[ident])
        k.op("pool", lambda h: h.affine_select(out=ident[:], in_=ident[:], pattern=[[-1, 128]], compare_op=ALU.is_equal,
                                              fill=0.0, base=0, channel_multiplier=1), R=[ident], W=[ident])
        with ExitStack() as e1:
            KcT = k.sb([64, S], BF16, es=e1)
            VcT = k.sb([64, S], BF16, es=e1)
            xb, xsb = k.rot(3, [64, TB], F32, es=e1), k.rot(3, [64, TB], F32, es=e1)
            cb, sb_ = k.rot(2, [64, TB], F32, es=e1), k.rot(2, [64, TB], F32, es=e1)
            t1r, t2r = k.rot(2, [64, TB], F32, es=e1), k.rot(2, [64, TB], F32, es=e1)
            st8 = k.rot(2, [128, nkb // 4, 64], F32, es=e1)
            for blk in range(S // TB):
                t0 = blk * TB
                c_, s_ = cb.next(), sb_.next()
                k.dma("sp", c_[:], I["cos"].t[:, t0:t0 + TB], R=[I["cos"]], W=[c_])
                k.dma("sp", s_[:], I["sin"].t[:, t0:t0 + TB], R=[I["sin"]], W=[s_])
                pre = "qsT" in I
                items = [(I["qT"].t[g], I["qsT"].t[g] if pre else None, QT[:, g, t0:t0 + TB], QT) for g in range(4)]
                items += [(I["kT"].t[i], I["ksT"].t[i] if pre else None, d[:, t0:t0 + TB], d) for i, d in ((0, KcT), (1, KsT), (2, KwT))]
                for src, srcs, dst, dbuf in items:
                    x_, xs_ = xb.next(), xsb.next()
                    k.dma("sp", x_[:], src[:, t0:t0 + TB], R=[I["qT"]], W=[x_])
                    if srcs is not None:
                        k.dma("sp", xs_[:], srcs[:, t0:t0 + TB], R=[I["qT"]], W=[xs_])
                    else:
                        k.dma("sp", xs_[0:32, :], src[32:64, t0:t0 + TB], R=[I["qT"]], W=[xs_])
                        k.dma("sp", xs_[32:64, :], src[0:32, t0:t0 + TB], R=[I["qT"]], W=[xs_])
                    t1, t2 = t1r.next(), t2r.next()
                    k.tt("dve", t1[:], x_[:], c_[:], ALU.mult, R=[x_, c_], W=[t1])
                    k.tt("pool", t2[:], xs_[:], s_[:], ALU.mult, R=[xs_, s_], W=[t2])
                    k.tt("dve", dst, t1[:], t2[:], ALU.add, R=[t1, t2], W=[dbuf])
                x_ = xb.next()
                k.dma("sp", x_[:], I["vcT"].t[:, t0:t0 + TB], R=[I["vcT"]], W=[x_])
                k.copy("pool", VcT[:, t0:t0 + TB], x_[:], R=[x_], W=[VcT])
            for i, dst in ((0, Vsa), (1, Vwa)):
                vv = I["v_tok"].t[i].rearrange("(kb p) d -> p kb d", p=128)
                for q in range(4):
                    st = st8.next()
                    n4 = nkb // 4
                    k.dma("sp", st[:], vv[:, q * n4:(q + 1) * n4, :], R=[I["v_tok"]], W=[st])
                    k.copy("pool", dst[:, q * n4:(q + 1) * n4, 0:64], st[:], R=[st], W=[dst])
            glt = k.sb([128, nkb, 12], F32, es=e1)
            k.dma("sp", glt[:], I["gl"].t.rearrange("(kb p) c -> p kb c", p=128), R=[I["gl"]], W=[glt])
            k.act(gsig[:], glt[:], AF.Sigmoid, R=[glt], W=[gsig])
            k.dma("sp", selb[:], I["selb"].t.rearrange("(kb p) c -> p kb c", p=128), R=[I["selb"]], W=[selb])
            est = k.rot(2, [n_sel, 1024], F32, es=e1)
            for q in range(S // 1024):
                e_ = est.next()
                k.dma("sp", e_[:], I["ebig"].t[:, q * 1024:(q + 1) * 1024], R=[I["ebig"]], W=[e_])
                k.copy("pool", Ebig[:, q * 1024:(q + 1) * 1024], e_[:], R=[e_], W=[Ebig])
            ost = k.sb([128, ncb, n_sel], F32, es=e1)
            k.dma("sp", ost[:], I["ov"].t.rearrange("(c p) j -> p c j", p=128), R=[I["ov"]], W=[ost])
            k.copy("pool", Ov[:], ost[:], R=[ost], W=[Ov])
            w1b = k.sb([64, 32, 256], BF16, es=e1)
            w1s = k.rot(2, [64, 4, 256], F32, es=e1)
            w2s = k.sb([128, 2, 64], F32, es=e1)
            w2b = k.sb([128, 2, 64], BF16, es=e1)
            pss = k.sb([64, 34], F32, es=e1)
            posb = k.sb([64, 34], BF16, es=e1)
            biasb = k.sb([128, 2], F32, es=e1)
            hs = k.rot(2, [128, n_cmp], F32, es=e1)
            tq = k.rot(2, [128, n_cmp], F32, es=e1)
            gg = [k.sb([128, ncb * 128], BF16, es=e1) for _ in range(2)]
            for which, src in ((0, KcT), (1, VcT)):
                for q in range(8):
                    st = w1s.next()
                    k.dma("sp", st[:], I["w1"].t[which, :, 4 * q:4 * q + 4, :], R=[I["w1"]], W=[st])
                    k.copy("pool", w1b[:, 4 * q:4 * q + 4, :], st[:], R=[st], W=[w1b])
                k.dma("sp", w2s[:], I["w2"].t[which].rearrange("(c p) d -> p c d", p=128), R=[I["w2"]], W=[w2s])
                k.copy("pool", w2b[:], w2s[:], R=[w2s], W=[w2b])
                k.memset("dve", pss[:], 0.0, W=[pss])
                k.dma("sp", pss[:, 0:32], I["posT"].t[which], R=[I["posT"]], W=[pss])
                k.copy("dve", posb[:], pss[:], R=[pss], W=[posb])
                for hc in range(2):
                    ph, pb = k.banks[1 + hc], k.banks[3 + hc]
                    for j in range(32):
                        k.mm(ph[:, 0:n_cmp], w1b[:, j, hc * 128:(hc + 1) * 128], src[:, j:j + 16 * (n_cmp - 1) + 1:16],
                             j == 0, j == 31, R=[w1b, src], W=[ph])
                    for j in range(32):
                        k.mm(pb[:, 0:2], w1b[:, j, hc * 128:(hc + 1) * 128], posb[:, j:j + 2], j == 0, j == 31, R=[w1b, posb], W=[pb])
                    k.copy("dve", biasb[:, hc:hc + 1], pb[:, 0:1], R=[pb], W=[biasb])
                    h_, t_ = hs.next(), tq.next()
                    k.act(h_[:], ph[:, 0:n_cmp], AF.Identity, R=[ph, biasb], W=[h_], bias=biasb[:, hc:hc + 1], scale=1.0)
                    k.tt("dve", t_[:], h_[:], h_[:], ALU.mult, R=[h_], W=[t_])
                    k.ts("dve", t_[:], t_[:], 0.044715, 1.0, ALU.mult, ALU.add, R=[t_], W=[t_])
                    k.tt("dve", t_[:], t_[:], h_[:], ALU.mult, R=[t_, h_], W=[t_])
                    k.act(t_[:], t_[:], AF.Sigmoid, R=[t_], W=[t_], scale=1.5957691216)
                    k.memset("pool", gg[hc][:], 0.0, W=[gg[hc]])
                    k.tt("dve", gg[hc][:, 0:n_cmp], h_[:], t_[:], ALU.mult, R=[h_, t_], W=[gg[hc]])
                if which == 0:
                    po = k.banks[5]
                    for hc in range(2):
                        k.mm(po[0:64, 0:n_cmp], w2b[:, hc, :], gg[hc][:, 0:n_cmp], hc == 0, hc == 1, R=[w2b, gg[hc]], W=[po])
                    k.copy("act", KcmpT[:, 0:n_cmp], po[0:64, 0:n_cmp], R=[po], W=[KcmpT])
                else:
                    for nb in range(ncb):
                        nk = min(128, n_cmp - nb * 128)
                        po = k.banks[5 + nb % 2]
                        for hc in range(2):
                            k.mm(po[0:nk, 0:64], gg[hc][:, nb * 128:nb * 128 + nk], w2b[:, hc, :], hc == 0, hc == 1, R=[w2b, gg[hc]], W=[po])
                        k.copy("act", Vca[0:nk, nb, 0:64], po[0:nk, 0:64], R=[po], W=[Vca])
            k.barrier()
        with ExitStack() as e2:
            ysb = k.sb([128, nkb, 256], F32, es=e2)
            imp = k.sb([128, nkb, n_sel], F32, es=e2)
            negT = k.sb([n_sel, S], BF16, es=e2)
            ptr = k.rot(3, [128, 512], BF16, es=e2)
            dn, rden, fc = k.rot(2, [128, 4], F32, es=e2), k.rot(2, [128, 4], F32, es=e2), k.rot(2, [128, 4], F32, es=e2)
            imod, iw = k.rot(2, [128, n_sel], F32, es=e2), k.rot(2, [128, n_sel], F32, es=e2)
            m8a, m8b = k.rot(2, [128, 8], F32, es=e2), k.rot(2, [128, 8], F32, es=e2)
            s01 = k.rot(2, [128, n_sel], F32, es=e2)
            ngm = k.rot(2, [128, n_sel], BF16, es=e2)
            sbanks, abanks, ibanks = Rot(k.banks[0:3]), Rot(k.banks[3:5]), Rot(k.banks[5:7])

            def epilogue(acc, g, qc, br, first_branch):
                av = acc_view(acc)
                d_, rd, f_ = dn.next(), rden.next(), fc.next()
                k.ts("dve", d_[:], av[:, :, 64], 1e-30, None, ALU.max, None, R=[acc], W=[d_])
                k.op("dve", lambda hh: hh.reciprocal(out=rd[:], in_=d_[:]), R=[d_], W=[rd])
                k.tt("dve", f_[:], rd[:], gsig[:, 4 * qc:4 * qc + 4, g * 3 + br], ALU.mult, R=[rd, gsig], W=[f_])
                for sub in range(4):
                    o_ap = ysb[:, 4 * qc + sub, g * 64:(g + 1) * 64]
                    if first_branch:
                        k.act(o_ap, av[:, sub, 0:64], AF.Copy, R=[acc, f_], W=[ysb], scale=f_[:, sub:sub + 1])
                    else:
                        k.stt(o_ap, av[:, sub, 0:64], f_[:, sub:sub + 1], o_ap, ALU.mult, ALU.add, R=[acc, f_, ysb], W=[ysb])
                return rd

            for g in range(4):
                for qc in range(nqc):
                    q0 = qc * 512
                    acc, ia = abanks.next(), ibanks.next()
                    pairs = []
                    for nb in range(ncb):
                        if 16 * 128 * nb + 31 > q0 + 511:
                            continue
                        nk = min(128, n_cmp - nb * 128)
                        pairs.append(dict(kT=KcmpT[:, nb * 128:nb * 128 + nk], nk=nk, v=Vca[0:nk, nb, :], v2=Ov[0:nk, nb, :],
                                          bufs=[KcmpT, Vca, Ov], mask=(q0 - 2048 * nb - 31, -16, 1)))
                    attn_chunk(k, QT[:, g, q0:q0 + 512], [QT], pairs, acc, 65, SC, sbanks, ptr, q0, acc2=ia, W2=n_sel)
                    rd = epilogue(acc, g, qc, 0, True)
                    iv = acc_view(ia)
                    for sub in range(4):
                        i_ap = imp[:, 4 * qc + sub, :]
                        if g == 0:
                            k.ts("dve", i_ap, iv[:, sub, 0:n_sel], rd[:, sub:sub + 1], None, ALU.mult, None, R=[ia, rd], W=[imp])
                        else:
                            k.stt(i_ap, iv[:, sub, 0:n_sel], rd[:, sub:sub + 1], i_ap, ALU.mult, ALU.add, R=[ia, rd, imp], W=[imp])
            for qt in range(nkb):
                im, w_, a8, b8, s_, n_ = imod.next(), iw.next(), m8a.next(), m8b.next(), s01.next(), ngm.next()
                k.tt("dve", im[:], imp[:, qt, :], selb[:, qt, :], ALU.add, R=[imp, selb], W=[im])
                k.op("dve", lambda h: h.max(out=a8[:], in_=im[:]), R=[im], W=[a8])
                k.op("dve", lambda h: h.match_replace(out=w_[:], in_to_replace=a8[:], in_values=im[:], imm_value=-3.0e38),
                     R=[a8, im], W=[w_])
                k.op("dve", lambda h: h.max(out=b8[:], in_=w_[:]), R=[w_], W=[b8])
                k.ts("dve", s_[:], im[:], b8[:, 7:8], None, ALU.is_ge, None, R=[im, b8], W=[s_])
                k.ts("dve", n_[:], s_[:], -1.0, BIG, ALU.add, ALU.mult, R=[s_], W=[n_])
                pb = k.bankb
                k.op("pe", lambda h: h.transpose(out=pb[0:n_sel, 0:128], in_=n_[:], identity=ident[:]), R=[n_, ident], W=[pb])
                k.copy("act", negT[:, qt * 128:(qt + 1) * 128], pb[0:n_sel, 0:128], R=[pb], W=[negT])
            for g in range(4):
                for qc in range(nqc):
                    q0 = qc * 512
                    acc = abanks.next()
                    pairs = []
                    for kb in range(4 * qc + 4):
                        pairs.append(dict(kT=KsT[:, kb * 128:(kb + 1) * 128], nk=128, v=Vsa[:, kb, :], bufs=[KsT, Vsa],
                                          extra=(Ebig[:, kb * 128:(kb + 1) * 128], negT[:, q0:q0 + 512], [Ebig, negT]),
                                          mask=(q0 - kb * 128, -1, 1) if kb >= 4 * qc else None, kpos0=kb * 128))
                    attn_chunk(k, QT[:, g, q0:q0 + 512], [QT], pairs, acc, 65, SC, sbanks, ptr, q0)
                    epilogue(acc, g, qc, 1, False)
            for g in range(4):
                for qc in range(nqc):
                    q0 = qc * 512
                    acc = abanks.next()
                    pairs = []
                    for kb in range(max(0, 4 * qc - 4), 4 * qc + 4):
                        if kb < 4 * qc:
                            i = kb - (4 * qc - 4)
                            pairs.append(dict(kT=KwT[:, kb * 128:(kb + 1) * 128], nk=128, v=Vwa[:, kb, :], bufs=[KwT, Vwa],
                                              mask=(kb * 128 - q0 + 511, 1, -1), skip=set(range(i + 1, 4))))
                        else:
                            pairs.append(dict(kT=KwT[:, kb * 128:(kb + 1) * 128], nk=128, v=Vwa[:, kb, :], bufs=[KwT, Vwa],
                                              mask=(q0 - kb * 128, -1, 1), kpos0=kb * 128))
                    attn_chunk(k, QT[:, g, q0:q0 + 512], [QT], pairs, acc, 65, SC, sbanks, ptr, q0)
                    epilogue(acc, g, qc, 2, False)
            yv = y.t.rearrange("(kb p) f -> p kb f", p=128)
            for q in range(4):
                n4 = nkb // 4
                k.dma("sp", yv[:, q * n4:(q + 1) * n4, :], ysb[:, q * n4:(q + 1) * n4, :], R=[ysb], W=[y])
            k.barrier()


def nsa_consts(S):
    n_cmp, n_sel = S // 16 - 1, S // 64
    ncb = (n_cmp + 127) // 128
    cmp_idx = np.arange(n_cmp)[:, None] * 16 + np.arange(32)[None, :]
    blk = np.arange(n_sel)
    ov = np.zeros((ncb * 128, n_sel), np.float32)
    ov[:n_cmp] = ((cmp_idx[:, :1] < (blk[None, :] + 1) * 64) & (cmp_idx[:, -1:] >= blk[None, :] * 64)).astype(np.float32)
    cur = (np.arange(S) // 64)[:, None]
    valid = blk[None, :] <= cur
    forced = valid & ((blk[None, :] == 0) | (blk[None, :] >= cur - 1))
    selb = np.where(forced, 1e30, np.where(valid, 0.0, -1e30)).astype(np.float32)
    ebig = (np.arange(S)[None, :] // 64 == blk[:, None]).astype(np.float32)
    return ov, selb, np.ascontiguousarray(ebig)


def nsa_inputs(pn, cmp_pos, cmp_w1, cmp_w2, hh, S):
    A = np.ascontiguousarray
    sw = lambda t: np.concatenate([t[..., 32:, :], t[..., :32, :]], axis=-2)
    q = pn[:, hh * 256:(hh + 1) * 256].reshape(S, 4, 64).transpose(1, 2, 0)
    kk = np.stack([pn[:, c + hh * 64:c + hh * 64 + 64].T for c in (512, 768, 1024)])
    vc = pn[:, 640 + hh * 64:640 + hh * 64 + 64].T
    v_tok = np.stack([pn[:, c + hh * 64:c + hh * 64 + 64] for c in (896, 1152)])
    gl = pn[:, 1280 + hh * 12:1280 + hh * 12 + 12]
    cos, sin = rope_tables(S, 64)
    ov, selb, ebig = nsa_consts(S)
    w1 = cmp_w1.reshape(2, 32, 64, 256).transpose(0, 2, 1, 3)
    return dict(qT=A(q), qsT=A(sw(q)), kT=A(kk), ksT=A(sw(kk)), vcT=A(vc), v_tok=A(v_tok), gl=A(gl), cos=cos, sin=sin,
                w1=A(w1), w2=A(cmp_w2), posT=A(cmp_pos.transpose(0, 2, 1)), ov=ov, selb=selb, ebig=ebig)


def NSA_SHAPES(S):
    n_cmp, n_sel = S // 16 - 1, S // 64
    ncb = (n_cmp + 127) // 128
    return dict(qT=[4, 64, S], qsT=[4, 64, S], kT=[3, 64, S], ksT=[3, 64, S], vcT=[64, S], v_tok=[2, S, 64], gl=[S, 12],
                cos=[64, S], sin=[64, S], w1=[2, 64, 32, 256], w2=[2, 256, 64], posT=[2, 64, 32], ov=[ncb * 128, n_sel],
                selb=[S, n_sel], ebig=[n_sel, S])


CH = 64


def tri_mask(k, es, strict, n=8):
    m = k.sb([64, n, 64], BF16, es=es)
    k.memset("pool", m[:], 1.0, W=[m])
    k.op("pool", lambda h: h.affine_select(out=m[:], in_=m[:], pattern=[[0, n], [1, 64]], compare_op=ALU.is_ge, fill=0.0,
                                          base=-1 if strict else 0, channel_multiplier=-1), R=[m], W=[m])
    return m


def chunk_start_mask(k, es, n):
    m = k.sb([64, n], F32, es=es)
    k.memset("pool", m[:], 1.0, W=[m])
    k.memset("pool", m[:, :].rearrange("p (c t) -> p c t", t=CH)[:, :, 0:1], 0.0, W=[m])
    return m


def emit_gla(k, I, y, S):
    NCH = S // CH
    with ExitStack() as es:
        ident = k.sb([128, 128], BF16, es=es)
        k.memset("pool", ident[:], 1.0, W=[ident])
        k.op("pool", lambda h: h.affine_select(out=ident[:], in_=ident[:], pattern=[[-1, 128]], compare_op=ALU.is_equal,
                                              fill=0.0, base=0, channel_multiplier=1), R=[ident], W=[ident])
        tri = tri_mask(k, es, False)
        cmask = chunk_start_mask(k, es, TB)
        Vb = k.sb([64, NCH, 256], BF16, es=es)
        aw2 = k.sb([16, 128], F32, es=es)
        ab = k.sb([64, 2], F32, es=es)
        nab = k.sb([64, 2], F32, es=es)
        ng = k.sb([64, 256], F32, es=es)
        epsb = k.sb([64, 1], F32, es=es)
        k.memset("dve", epsb[:], EPS, W=[epsb])
        k.dma("sp", aw2[:], I["aw2"].t[:, :], R=[I["aw2"]], W=[aw2])
        k.dma("sp", ab[:], I["ab"].t[:, :], R=[I["ab"]], W=[ab])
        k.dma("sp", ng[:], I["ng"].t[:, :], R=[I["ng"]], W=[ng])
        k.ts("dve", nab[:], ab[:], -1.0, None, ALU.mult, None, R=[ab], W=[nab])
        vst = k.rot(2, [64, 8, 256], F32, es=es)
        for c8 in range(NCH // 8):
            st = vst.next()
            k.dma("sp", st[:], I["v_ch"].t[:, 8 * c8:8 * c8 + 8, :], R=[I["v_ch"]], W=[st])
            k.copy("pool", Vb[:, 8 * c8:8 * c8 + 8, :], st[:], R=[st], W=[Vb])
        for h in range(2):
            with ExitStack() as e1:
                Qb = k.sb([64, S], BF16, es=e1)
                Kt = k.sb([64, S], BF16, es=e1)
                Kh = k.sb([64, NCH, 64], BF16, es=e1)
                MT = k.sb([64, NCH, 64], BF16, es=e1)
                gC = k.sb([64, NCH], F32, es=e1)
                otok = k.sb([64, NCH, 128], F32, es=e1)
                H = k.sb([64, 128], F32, es=e1)
                Hb = k.sb([64, 128], BF16, es=e1)
                alo = k.rot(2, [16, TB], F32, es=e1)
                qb, kb_ = k.rot(2, [64, TB], F32, es=e1), k.rot(2, [64, TB], F32, es=e1)
                t_e, t_l, t_b = k.rot(2, [64, TB], F32, es=e1), k.rot(2, [64, TB], F32, es=e1), k.rot(2, [64, TB], F32, es=e1)
                t_eb, t_enb, t_k = k.rot(2, [64, TB], F32, es=e1), k.rot(2, [64, TB], F32, es=e1), k.rot(2, [64, TB], F32, es=e1)
                t_kh = k.rot(2, [64, TB], BF16, es=e1)
                pz = Rot(k.banks[0:2])
                for blk in range(S // TB):
                    t0 = blk * TB
                    a_, q_, k_ = alo.next(), qb.next(), kb_.next()
                    k.dma("sp", a_[:], I["aloT"].t[:, t0:t0 + TB], R=[I["aloT"]], W=[a_])
                    k.dma("sp", q_[:], I["qT"].t[h, :, t0:t0 + TB], R=[I["qT"]], W=[q_])
                    k.dma("sp", k_[:], I["kT"].t[h, :, t0:t0 + TB], R=[I["kT"]], W=[k_])
                    p_ = pz.next()
                    k.mm(p_[0:64, :], aw2[:, h * 64:(h + 1) * 64], a_[:], True, True, R=[aw2, a_], W=[p_])
                    e_, l_, b_ = t_e.next(), t_l.next(), t_b.next()
                    k.act(e_[:], p_[0:64, :], AF.Exp, R=[p_, nab], W=[e_], bias=nab[:, h:h + 1], scale=-1.0)
                    k.act(l_[:], e_[:], AF.Ln, R=[e_], W=[l_], bias=1.0, scale=1.0)
                    k.op("dve", lambda hh: hh.tensor_tensor_scan(out=b_[:], data0=cmask[:], data1=l_[:], initial=0.0,
                                                                 op0=ALU.mult, op1=ALU.add), R=[cmask, l_], W=[b_])
                    eb, enb, kt32 = t_eb.next(), t_enb.next(), t_k.next()
                    k.act(eb[:], b_[:], AF.Exp, R=[b_], W=[eb], scale=-1.0 / 16)
                    k.act(enb[:], b_[:], AF.Exp, R=[b_], W=[enb], scale=1.0 / 16)
                    k.stt(Qb[:, t0:t0 + TB], q_[:], 0.125, eb[:], ALU.mult, ALU.mult, R=[q_, eb], W=[Qb])
                    k.tt("dve", kt32[:], k_[:], enb[:], ALU.mult, R=[k_, enb], W=[kt32])
                    k.copy("pool", Kt[:, t0:t0 + TB], kt32[:], R=[kt32], W=[Kt])
                    ebv = eb[:, :].rearrange("p (c t) -> p c t", t=CH)
                    k.copy("dve", gC[:, 8 * blk:8 * blk + 8], ebv[:, :, CH - 1], R=[eb], W=[gC])
                    kh = t_kh.next()
                    k.tt("dve", kh[:, :].rearrange("p (c t) -> p c t", t=CH), kt32[:, :].rearrange("p (c t) -> p c t", t=CH),
                         ebv[:, :, CH - 1:CH].to_broadcast([64, 8, CH]), ALU.mult, R=[kt32, eb], W=[kh])
                    pb = k.bankb
                    for c in range(8):
                        k.op("pe", lambda hh: hh.transpose(out=pb[0:64, c * 64:(c + 1) * 64], in_=kh[:, c * 64:(c + 1) * 64],
                                                           identity=ident[0:64, 0:64]), R=[kh, ident], W=[pb])
                    k.copy("act", Kh[:, 8 * blk:8 * blk + 8, :], pb[0:64, 0:512].rearrange("p (c t) -> p c t", t=64), R=[pb], W=[Kh])
                pm = Rot(k.banks[2:4])
                for c8 in range(NCH // 8):
                    p_ = pm.next()
                    for c in range(8):
                        cc = 8 * c8 + c
                        k.mm(p_[0:64, c * 64:(c + 1) * 64], Kt[:, cc * 64:(cc + 1) * 64], Qb[:, cc * 64:(cc + 1) * 64], True, True,
                             R=[Kt, Qb], W=[p_])
                    k.tt("dve", MT[:, 8 * c8:8 * c8 + 8, :], p_[0:64, :].rearrange("p (c t) -> p c t", t=64), tri[:], ALU.mult,
                         R=[p_, tri], W=[MT])
                po_r, ph_r = Rot(k.banks[4:6]), Rot([k.banks[6], k.banks[0]])
                k.memset("dve", H[:], 0.0, W=[H])
                for c in range(NCH):
                    po, ph = po_r.next(), ph_r.next()
                    vch = Vb[:, c, h * 128:(h + 1) * 128]
                    k.mm(po[0:64, 0:128], MT[:, c, :], vch, True, c == 0, R=[MT, Vb], W=[po])
                    if c > 0:
                        k.mm(po[0:64, 0:128], Qb[:, c * 64:(c + 1) * 64], Hb[:], False, True, R=[Qb, Hb], W=[po])
                    k.copy("act", otok[:, c, :], po[0:64, 0:128], R=[po], W=[otok])
                    if c < NCH - 1:
                        k.mm(ph[0:64, 0:128], Kh[:, c, :], vch, True, True, R=[Kh, Vb], W=[ph])
                        k.stt(H[:], H[:], gC[:, c:c + 1], ph[0:64, 0:128], ALU.mult, ALU.add, R=[H, gC, ph], W=[H])
                        k.copy("dve", Hb[:], H[:], R=[H], W=[Hb])
                sqp = k.rot(2, [64, 16, 128], F32, es=e1)
                rst = k.rot(2, [64, 16, 128], F32, es=e1)
                ms, rs_ = k.rot(2, [64, 16], F32, es=e1), k.rot(2, [64, 16], F32, es=e1)
                yv = y.t.rearrange("(c p) f -> p c f", p=CH)
                for c16 in range(NCH // 16):
                    cs = slice(16 * c16, 16 * c16 + 16)
                    sq, rr, m_, r_ = sqp.next(), rst.next(), ms.next(), rs_.next()
                    k.dma("sp", rr[:], I["r_ch"].t[:, cs, h * 128:(h + 1) * 128], R=[I["r_ch"]], W=[rr])
                    k.tt("dve", sq[:], otok[:, cs, :], otok[:, cs, :], ALU.mult, R=[otok], W=[sq])
                    k.op("dve", lambda hh: hh.reduce_sum(out=m_[:], in_=sq[:], axis=AX.X), R=[sq], W=[m_])
                    k.act(m_[:], m_[:], AF.Sqrt, R=[m_, epsb], W=[m_], bias=epsb[:], scale=1.0 / 128)
                    k.op("dve", lambda hh: hh.reciprocal(out=r_[:], in_=m_[:]), R=[m_], W=[r_])
                    k.tt("dve", sq[:], otok[:, cs, :], r_[:, :].unsqueeze(2).to_broadcast([64, 16, 128]), ALU.mult, R=[otok, r_], W=[sq])
                    k.tt("pool", sq[:], sq[:], ng[:, h * 128:(h + 1) * 128].unsqueeze(1).to_broadcast([64, 16, 128]), ALU.mult,
                         R=[sq, ng], W=[sq])
                    k.act(rr[:], rr[:], AF.Silu, R=[rr], W=[rr])
                    k.tt("dve", sq[:], sq[:], rr[:], ALU.mult, R=[sq, rr], W=[sq])
                    k.dma("pool", yv[:, cs, h * 128:(h + 1) * 128], sq[:], R=[sq], W=[y])
                k.barrier()


def gla_inputs(pg, alpha_w2, alpha_b, norm_g, hh, S):
    A = np.ascontiguousarray
    q = pg[:, hh * 128:(hh + 1) * 128].reshape(S, 2, 64).transpose(1, 2, 0)
    kk = pg[:, 256 + hh * 128:256 + (hh + 1) * 128].reshape(S, 2, 64).transpose(1, 2, 0)
    v = pg[:, 512 + hh * 256:512 + (hh + 1) * 256].reshape(S // CH, CH, 256).transpose(1, 0, 2)
    r = pg[:, 1040 + hh * 256:1040 + (hh + 1) * 256].reshape(S // CH, CH, 256).transpose(1, 0, 2)
    alo = pg[:, 1024:1040].T
    return dict(qT=A(q), kT=A(kk), v_ch=A(v), r_ch=A(r), aloT=A(alo), aw2=A(alpha_w2[:, hh * 128:(hh + 1) * 128]),
                ab=A(alpha_b[hh * 128:(hh + 1) * 128].reshape(2, 64).T), ng=A(np.broadcast_to(norm_g[hh * 256:(hh + 1) * 256], (64, 256))))


GLA_SHAPES = lambda S: dict(qT=[2, 64, S], kT=[2, 64, S], v_ch=[64, S // CH, 256], r_ch=[64, S // CH, 256], aloT=[16, S],
                            aw2=[16, 128], ab=[64, 2], ng=[64, 256])


class _Stop(Exception):
    pass


def emit_rwkv(k, I, y, S, dbg=99):
    try:
        _emit_rwkv(k, I, y, S, dbg)
    except _Stop:
        k.barrier()


def _emit_rwkv(k, I, y, S, dbg):
    def ck(n):
        if dbg <= n:
            raise _Stop()
    NCH = S // CH
    C0 = 0.6065306597126334
    with ExitStack() as es:
        identb = k.sb([128, 128], BF16, es=es)
        identf = k.sb([64, 64], F32, es=es)
        for idt in (identb, identf):
            n = idt.t.shape[0]
            k.memset("pool", idt[:], 1.0, W=[idt])
            k.op("pool", lambda h: h.affine_select(out=idt[:], in_=idt[:], pattern=[[-1, n]], compare_op=ALU.is_equal,
                                                  fill=0.0, base=0, channel_multiplier=1), R=[idt], W=[idt])
        tri_i = tri_mask(k, es, False)
        tri_s = tri_mask(k, es, True)
        tri_l = k.sb([64, 8, 64], BF16, es=es)
        k.memset("pool", tri_l[:], 1.0, W=[tri_l])
        k.op("pool", lambda h: h.affine_select(out=tri_l[:], in_=tri_l[:], pattern=[[0, 8], [-1, 64]], compare_op=ALU.is_ge,
                                              fill=0.0, base=-1, channel_multiplier=1), R=[tri_l], W=[tri_l])
        cmask = chunk_start_mask(k, es, TB)
        ones64 = k.sb([64, 64], BF16, es=es)
        k.memset("dve", ones64[:], 1.0, W=[ones64])
        gnb = k.sb([64, 1], F32, es=es)
        k.memset("dve", gnb[:], 64e-5, W=[gnb])
        P = {}
        for nm, shp in (("mu_r", [64, 4]), ("mu_k", [64, 4]), ("mu_v", [64, 4]), ("mu_w", [64, 1]), ("mu_a", [64, 1]), ("mu_g", [128, 1]),
                        ("w0", [64, 4]), ("a0", [64, 4]), ("k_k", [64, 4]), ("k_a", [64, 4]), ("r_k", [64, 4]),
                        ("ww2", [64, 256]), ("aw2", [64, 256]), ("gw2", [128, 256]), ("ln_w", [64, 256]), ("ln_b", [64, 256])):
            P[nm] = k.sb(shp, F32, es=es)
            k.dma("sp", P[nm][:], I[nm].t[:, :], R=[I[nm]], W=[P[nm]])
        omka = k.sb([64, 4], F32, es=es)
        k.ts("dve", omka[:], P["k_a"][:], -1.0, 1.0, ALU.mult, ALU.add, R=[P["k_a"]], W=[omka])
        gw2b = k.sb([128, 256], BF16, es=es)
        k.copy("dve", gw2b[:], P["gw2"][:], R=[P["gw2"]], W=[gw2b])
        Hs = [k.sb([64, 64], F32, es=es) for _ in range(4)]
        Hbs = [k.sb([64, 64], BF16, es=es) for _ in range(4)]
        for h in range(4):
            k.memset("dve", Hs[h][:], 0.0, W=[Hs[h]])

        def f32r(n, p=64, w=TB):
            return k.rot(n, [p, w], F32, es=es)

        ldw, lda, ldg = k.rot(2, [64, TB + 1], F32, es=es), k.rot(2, [64, TB + 1], F32, es=es), k.rot(2, [128, TB + 1], F32, es=es)
        ldr, ldk, ldv = k.rot(2, [64, TB + 1], F32, es=es), k.rot(2, [64, TB + 1], F32, es=es), k.rot(2, [64, TB + 1], F32, es=es)
        dtmp = k.rot(2, [128, TB], F32, es=es)
        tw_r, as_r = f32r(2), f32r(2)
        sgl_r = k.rot(2, [128, TB], BF16, es=es)
        rs_r, ks_r, vs_r = f32r(2), f32r(2), f32r(2)
        sgw_r, a_r, b_r, eb_r, enb_r, eb1_r = f32r(1), f32r(2), f32r(1), f32r(2), f32r(1), f32r(1)
        kkr_r, nrm_r, kk_r, kp_r, be_r, kt32_r, bt32_r = f32r(1), f32r(1), f32r(2), f32r(2), f32r(1), f32r(1), f32r(1)
        sqk_r = k.rot(2, [64, TB], BF16, es=es)

        def b16r(n):
            return k.rot(n, [64, TB], BF16, es=es)

        Rb_r, Ab_r, Kt_r, Bt_r, khf_r, bhf_r, rkx_r = b16r(4), b16r(4), b16r(2), b16r(2), b16r(2), b16r(2), b16r(2)

        def m16r(n):
            return k.rot(n, [64, 8, 64], BF16, es=es)

        Kh_r, Bh_r, Vc_r = m16r(4), m16r(4), m16r(4)
        vtok_r = k.rot(4, [64, 8, 64], F32, es=es)
        X_r, XT_r, P_r, PT_r = m16r(2), m16r(2), m16r(3), m16r(3)
        Mak_r, Mrk_r, Mrb_r, Ri_r, R16_r = m16r(4), m16r(4), m16r(4), m16r(4), m16r(2)
        Rf_r = k.rot(2, [64, 8, 64], F32, es=es)
        gC_r = k.rot(4, [64, 8], F32, es=es)
        rk_r = k.rot(4, [64, 8], F32, es=es)
        Zb_r, Un_r = k.rot(8, [64, 64], BF16, es=es), k.rot(8, [64, 64], BF16, es=es)
        ytok_r = k.rot(4, [64, 8, 64], F32, es=es)
        gt_r, po1_r, po2_r = (k.rot(2, [64, 8, 64], F32, es=es) for _ in range(3))
        st1_r, st2_r = k.rot(2, [64, 8], F32, es=es), k.rot(2, [64, 8], F32, es=es)
        bk = Rot(k.banks[0:3])
        yv = y.t.rearrange("(c p) f -> p c f", p=CH)

        def v3(ap):
            return ap.rearrange("p (c t) -> p c t", t=CH)

        def shifted(ld, src_ap, srcbuf, mu_ap, mubuf, out, np_, t0):
            t = ld.next()
            if t0 == 0:
                k.memset("pool", t[0:np_, 0:1], 0.0, W=[t])
                k.dma("sp", t[0:np_, 1:TB + 1], src_ap[:, 0:TB], R=[srcbuf], W=[t])
            else:
                k.dma("sp", t[0:np_, :], src_ap[:, t0 - 1:t0 + TB], R=[srcbuf], W=[t])
            d = dtmp.next()
            k.tt("pool", d[0:np_, :], t[0:np_, 0:TB], t[0:np_, 1:TB + 1], ALU.subtract, R=[t], W=[d])
            k.stt(out[0:np_, :], d[0:np_, :], mu_ap, t[0:np_, 1:TB + 1], ALU.mult, ALU.add, R=[d, mubuf, t], W=[out])

        for blk in range(S // TB):
            t0 = blk * TB
            tw, als, sgl = tw_r.next(), as_r.next(), sgl_r.next()
            shifted(ldw, I["wloT"].t, I["wloT"], P["mu_w"][:, 0:1], P["mu_w"], tw, 64, t0)
            k.act(tw[:], tw[:], AF.Tanh, R=[tw], W=[tw])
            shifted(lda, I["aloT"].t, I["aloT"], P["mu_a"][:, 0:1], P["mu_a"], als, 64, t0)
            gl = dtmp.next()
            shifted(ldg, I["gloT"].t, I["gloT"], P["mu_g"][:, 0:1], P["mu_g"], gl, 128, t0)
            k.act(sgl[:], gl[:], AF.Sigmoid, R=[gl], W=[sgl])
            ck(1)
            HS = []
            for h in range(4):
                hc = slice(h, h + 1)
                rs, ks, vs = rs_r.next(), ks_r.next(), vs_r.next()
                shifted(ldr, I["rT"].t[h], I["rT"], P["mu_r"][:, hc], P["mu_r"], rs, 64, t0)
                shifted(ldk, I["kT"].t[h], I["kT"], P["mu_k"][:, hc], P["mu_k"], ks, 64, t0)
                shifted(ldv, I["vT"].t[h], I["vT"], P["mu_v"][:, hc], P["mu_v"], vs, 64, t0)
                pw, pa = bk.next(), bk.next()
                k.mm(pw[0:64, :], P["ww2"][:, h * 64:(h + 1) * 64], tw[:], True, True, R=[P["ww2"], tw], W=[pw])
                k.mm(pa[0:64, :], P["aw2"][:, h * 64:(h + 1) * 64], als[:], True, True, R=[P["aw2"], als], W=[pa])
                sgw, a_, b_, eb, enb, eb1 = sgw_r.next(), a_r.next(), b_r.next(), eb_r.next(), enb_r.next(), eb1_r.next()
                k.act(sgw[:], pw[0:64, :], AF.Sigmoid, R=[pw, P["w0"]], W=[sgw], bias=P["w0"][:, hc], scale=1.0)
                k.act(a_[:], pa[0:64, :], AF.Sigmoid, R=[pa, P["a0"]], W=[a_], bias=P["a0"][:, hc], scale=1.0)
                k.op("dve", lambda hh: hh.tensor_tensor_scan(out=b_[:], data0=cmask[:], data1=sgw[:], initial=0.0,
                                                             op0=ALU.mult, op1=ALU.add), R=[cmask, sgw], W=[b_])
                k.act(eb[:], b_[:], AF.Exp, R=[b_], W=[eb], scale=-C0)
                k.act(enb[:], b_[:], AF.Exp, R=[b_], W=[enb], scale=C0)
                k.tt("pool", eb1[:], b_[:], sgw[:], ALU.subtract, R=[b_, sgw], W=[eb1])
                k.act(eb1[:], eb1[:], AF.Exp, R=[eb1], W=[eb1], scale=-C0)
                kkr, sqk, nrm, kk, kp, be = kkr_r.next(), sqk_r.next(), nrm_r.next(), kk_r.next(), kp_r.next(), be_r.next()
                k.ts("dve", kkr[:], ks[:], P["k_k"][:, hc], None, ALU.mult, None, R=[ks, P["k_k"]], W=[kkr])
                k.act(sqk[:], kkr[:], AF.Square, R=[kkr], W=[sqk])
                pn = bk.next()
                k.mm(pn[0:64, :], ones64[:], sqk[:], True, True, R=[ones64, sqk], W=[pn])
                k.act(nrm[:], pn[0:64, :], AF.Sqrt, R=[pn], W=[nrm])
                k.ts("dve", nrm[:], nrm[:], 1e-12, None, ALU.max, None, R=[nrm], W=[nrm])
                k.op("dve", lambda hh: hh.reciprocal(out=nrm[:], in_=nrm[:]), R=[nrm], W=[nrm])
                k.tt("pool", kk[:], kkr[:], nrm[:], ALU.mult, R=[kkr, nrm], W=[kk])
                k.ts("dve", kp[:], a_[:], P["k_a"][:, hc], omka[:, hc], ALU.mult, ALU.add, R=[a_, P["k_a"], omka], W=[kp])
                k.tt("pool", kp[:], kp[:], ks[:], ALU.mult, R=[kp, ks], W=[kp])
                k.tt("pool", be[:], kk[:], a_[:], ALU.mult, R=[kk, a_], W=[be])
                Rb, Ab, Kt, Bt, kt32, bt32 = Rb_r.next(), Ab_r.next(), Kt_r.next(), Bt_r.next(), kt32_r.next(), bt32_r.next()
                k.tt("dve", Rb[:], rs[:], eb[:], ALU.mult, R=[rs, eb], W=[Rb])
                k.tt("pool", Ab[:], kk[:], eb1[:], ALU.mult, R=[kk, eb1], W=[Ab])
                k.tt("dve", kt32[:], kp[:], enb[:], ALU.mult, R=[kp, enb], W=[kt32])
                k.tt("pool", bt32[:], be[:], enb[:], ALU.mult, R=[be, enb], W=[bt32])
                k.copy("act", Kt[:], kt32[:], R=[kt32], W=[Kt])
                k.copy("act", Bt[:], bt32[:], R=[bt32], W=[Bt])
                gC = gC_r.next()
                ebv = v3(eb[:, :])
                k.copy("dve", gC[:], ebv[:, :, CH - 1], R=[eb], W=[gC])
                gbc = ebv[:, :, CH - 1:CH].to_broadcast([64, 8, CH])
                khf, bhf, rkx = khf_r.next(), bhf_r.next(), rkx_r.next()
                k.tt("dve", v3(khf[:, :]), v3(kt32[:, :]), gbc, ALU.mult, R=[kt32, eb], W=[khf])
                k.tt("pool", v3(bhf[:, :]), v3(bt32[:, :]), gbc, ALU.mult, R=[bt32, eb], W=[bhf])
                k.stt(rkx[:], rs[:], P["r_k"][:, hc], kp[:], ALU.mult, ALU.mult, R=[rs, P["r_k"], kp], W=[rkx])
                ck(2)
                Kh, Bh, Vc, vtok = Kh_r.next(), Bh_r.next(), Vc_r.next(), vtok_r.next()
                pbb = k.bankb
                for src_, dst_, eng_ in ((khf, Kh, "act"), (bhf, Bh, "dve")):
                    for c in range(8):
                        k.op("pe", lambda hh: hh.transpose(out=pbb[0:64, c * 64:(c + 1) * 64], in_=src_[:, c * 64:(c + 1) * 64],
                                                           identity=identb[0:64, 0:64]), R=[src_, identb], W=[pbb])
                    k.copy(eng_, dst_[:], v3(pbb[0:64, 0:512]), R=[pbb], W=[dst_])
                ck(3)
                pv = bk.next()
                for c in range(8):
                    k.mm(pv[0:64, c * 64:(c + 1) * 64], vs[:, c * 64:(c + 1) * 64], identf[:, :], True, True, R=[vs, identf], W=[pv])
                k.copy("act", vtok[:], v3(pv[0:64, :]), R=[pv], W=[vtok])
                k.copy("dve", Vc[:], v3(pv[0:64, :]), R=[pv], W=[Vc])
                ck(4)
                prk = bk.next()
                for c in range(8):
                    k.mm(prk[0:64, 2 * c:2 * c + 2], rkx[:, c * 64:(c + 1) * 64], ones64[:, 0:2], True, True, R=[rkx, ones64], W=[prk])
                rk = rk_r.next()
                k.copy("dve", rk[:], prk[0:64, 0:16].rearrange("p (c two) -> p c two", two=2)[:, :, 0], R=[prk], W=[rk])
                ck(5)
                X, XT, Mak, Mrk, Mrb = X_r.next(), XT_r.next(), Mak_r.next(), Mrk_r.next(), Mrb_r.next()
                for (lh, rh, dst, msk) in ((Bt, Ab, X, tri_s), (Ab, Bt, XT, tri_l), (Kt, Ab, Mak, tri_s), (Kt, Rb, Mrk, tri_i), (Bt, Rb, Mrb, tri_i)):
                    pm = bk.next()
                    for c in range(8):
                        k.mm(pm[0:64, c * 64:(c + 1) * 64], lh[:, c * 64:(c + 1) * 64], rh[:, c * 64:(c + 1) * 64], True, True,
                             R=[lh, rh], W=[pm])
                    k.tt("dve", dst[:], v3(pm[0:64, :]), msk[:], ALU.mult, R=[pm, msk], W=[dst])
                ck(6)
                Rf, R16 = Rf_r.next(), R16_r.next()
                idb = identb[0:64, 0:64].unsqueeze(1).to_broadcast([64, 8, 64])
                k.tt("dve", Rf[:], idb, X[:], ALU.subtract, R=[identb, X], W=[Rf])
                k.copy("pool", R16[:], Rf[:], R=[Rf], W=[R16])
                Pc, PTc = X, XT
                for it in range(5):
                    Pn, PTn = P_r.next(), PT_r.next()
                    p1, p2 = bk.next(), bk.next()
                    for c in range(8):
                        k.mm(p1[0:64, c * 64:(c + 1) * 64], PTc[:, c, :], Pc[:, c, :], True, True, R=[PTc, Pc], W=[p1])
                    k.copy("act", Pn[:], v3(p1[0:64, :]), R=[p1], W=[Pn])
                    for c in range(8):
                        k.mm(p2[0:64, c * 64:(c + 1) * 64], Pc[:, c, :], PTc[:, c, :], True, True, R=[PTc, Pc], W=[p2])
                    k.copy("act", PTn[:], v3(p2[0:64, :]), R=[p2], W=[PTn])
                    p3 = bk.next()
                    for c in range(8):
                        k.mm(p3[0:64, c * 64:(c + 1) * 64], PTn[:, c, :], R16[:, c, :], True, True, R=[PTn, R16], W=[p3])
                    k.tt("dve", Rf[:], Rf[:], v3(p3[0:64, :]), ALU.add, R=[Rf, p3], W=[Rf])
                    R16 = R16_r.next() if it < 4 else Ri_r.next()
                    k.copy("pool", R16[:], Rf[:], R=[Rf], W=[R16])
                    Pc, PTc = Pn, PTn
                Ri = R16
                ck(7)
                ytok = ytok_r.next()
                HS.append(dict(h=h, Mak=Mak, Mrk=Mrk, Mrb=Mrb, Ri=Ri, Ab=Ab, Rb=Rb, Kh=Kh, Bh=Bh, Vc=Vc, gC=gC, vtok=vtok, rk=rk, ytok=ytok))
            for c in range(8):
                first = (blk == 0 and c == 0)
                cs = slice(c * 64, (c + 1) * 64)
                Zs, Us = [Zb_r.next() for _ in HS], [Un_r.next() for _ in HS]
                for i, T_ in enumerate(HS):
                    pb_ = k.banks[3 + i]
                    k.mm(pb_[0:64, 0:64], T_["Mak"][:, c, :], T_["Vc"][:, c, :], True, first, R=[T_["Mak"], T_["Vc"]], W=[pb_])
                    if not first:
                        k.mm(pb_[0:64, 0:64], T_["Ab"][:, cs], Hbs[T_["h"]][:], False, True, R=[T_["Ab"], Hbs[T_["h"]]], W=[pb_])
                for i, T_ in enumerate(HS):
                    pb_ = k.banks[3 + i]
                    k.copy("act", Zs[i][:], pb_[0:64, 0:64], R=[pb_], W=[Zs[i]])
                for i, T_ in enumerate(HS):
                    pb_ = k.banks[3 + i]
                    k.mm(pb_[0:64, 64:128], T_["Ri"][:, c, :], Zs[i][:], True, True, R=[T_["Ri"], Zs[i]], W=[pb_])
                for i, T_ in enumerate(HS):
                    pb_ = k.banks[3 + i]
                    k.ts("dve", Us[i][:], pb_[0:64, 64:128], -1.0, None, ALU.mult, None, R=[pb_], W=[Us[i]])
                for i, T_ in enumerate(HS):
                    pb_ = k.banks[3 + i]
                    Hb = Hbs[T_["h"]]
                    k.mm(pb_[0:64, 128:192], T_["Mrk"][:, c, :], T_["Vc"][:, c, :], True, False, R=[T_["Mrk"], T_["Vc"]], W=[pb_])
                    if not first:
                        k.mm(pb_[0:64, 128:192], T_["Rb"][:, cs], Hb[:], False, False, R=[T_["Rb"], Hb], W=[pb_])
                    k.mm(pb_[0:64, 128:192], T_["Mrb"][:, c, :], Us[i][:], False, True, R=[T_["Mrb"], Us[i]], W=[pb_])
                    k.mm(pb_[0:64, 192:256], T_["Kh"][:, c, :], T_["Vc"][:, c, :], True, False, R=[T_["Kh"], T_["Vc"]], W=[pb_])
                    k.mm(pb_[0:64, 192:256], T_["Bh"][:, c, :], Us[i][:], False, True, R=[T_["Bh"], Us[i]], W=[pb_])
                for i, T_ in enumerate(HS):
                    pb_ = k.banks[3 + i]
                    H, Hb = Hs[T_["h"]], Hbs[T_["h"]]
                    k.stt(H[:], H[:], T_["gC"][:, c:c + 1], pb_[0:64, 192:256], ALU.mult, ALU.add, R=[H, T_["gC"], pb_], W=[H])
                    k.copy("dve", Hb[:], H[:], R=[H], W=[Hb])
                    k.copy("act", T_["ytok"][:, c, :], pb_[0:64, 128:192], R=[pb_], W=[T_["ytok"]])
            for T_ in HS:
                h, vtok, rk, ytok = T_["h"], T_["vtok"], T_["rk"], T_["ytok"]
                pg = bk.next()
                for c in range(8):
                    k.mm(pg[0:64, c * 64:(c + 1) * 64], sgl[:, c * 64:(c + 1) * 64], gw2b[:, h * 64:(h + 1) * 64], True, True,
                         R=[sgl, gw2b], W=[pg])
                gt = gt_r.next()
                k.copy("act", gt[:], v3(pg[0:64, :]), R=[pg], W=[gt])
                s1, s2, o1, o2 = st1_r.next(), st2_r.next(), po1_r.next(), po2_r.next()
                k.op("dve", lambda hh: hh.reduce_sum(out=s1[:], in_=ytok[:], axis=AX.X), R=[ytok], W=[s1])
                k.ts("dve", s1[:], s1[:], 1.0 / 64, None, ALU.mult, None, R=[s1], W=[s1])
                k.tt("pool", o1[:], ytok[:], s1[:, :].unsqueeze(2).to_broadcast([64, 8, 64]), ALU.subtract, R=[ytok, s1], W=[o1])
                k.tt("pool", o2[:], o1[:], o1[:], ALU.mult, R=[o1], W=[o2])
                k.op("dve", lambda hh: hh.reduce_sum(out=s2[:], in_=o2[:], axis=AX.X), R=[o2], W=[s2])
                k.act(s2[:], s2[:], AF.Sqrt, R=[s2, gnb], W=[s2], bias=gnb[:], scale=1.0 / 64)
                k.op("dve", lambda hh: hh.reciprocal(out=s2[:], in_=s2[:]), R=[s2], W=[s2])
                k.tt("dve", o1[:], o1[:], s2[:, :].unsqueeze(2).to_broadcast([64, 8, 64]), ALU.mult, R=[o1, s2], W=[o1])
                k.tt("pool", o1[:], o1[:], P["ln_w"][:, h * 64:(h + 1) * 64].unsqueeze(1).to_broadcast([64, 8, 64]), ALU.mult,
                     R=[o1, P["ln_w"]], W=[o1])
                k.tt("pool", o1[:], o1[:], P["ln_b"][:, h * 64:(h + 1) * 64].unsqueeze(1).to_broadcast([64, 8, 64]), ALU.add,
                     R=[o1, P["ln_b"]], W=[o1])
                k.tt("dve", o2[:], vtok[:], rk[:, :].unsqueeze(2).to_broadcast([64, 8, 64]), ALU.mult, R=[vtok, rk], W=[o2])
                k.tt("pool", o1[:], o1[:], o2[:], ALU.add, R=[o1, o2], W=[o1])
                k.tt("dve", o1[:], o1[:], gt[:], ALU.mult, R=[o1, gt], W=[o1])
                k.dma("pool", yv[:, 8 * blk:8 * blk + 8, h * 64:(h + 1) * 64], o1[:], R=[o1], W=[y])
        k.barrier()


def rwkv_inputs(pr, mu, w0, w_w2, a0, a_w2, g_w2, k_k, k_a, r_k, ln_w, ln_b, hh, S):
    A = np.ascontiguousarray
    hs = slice(hh * 256, (hh + 1) * 256)
    heads = lambda t: A(t.reshape(S, 4, 64).transpose(1, 2, 0))
    col = lambda v: A(v[hs].reshape(4, 64).T)
    bc = lambda v: A(np.broadcast_to(v[hs], (64, 256)))
    return dict(rT=heads(pr[:, 0:512][:, hs]), kT=heads(pr[:, 576:1088][:, hs]), vT=heads(pr[:, 1088:1600][:, hs]),
                wloT=A(pr[:, 512:576].T), aloT=A(pr[:, 1600:1664].T), gloT=A(pr[:, 1664:1792].T),
                mu_r=col(mu[0:512]), mu_k=col(mu[576:1088]), mu_v=col(mu[1088:1600]), mu_w=A(mu[512:576].reshape(64, 1)),
                mu_a=A(mu[1600:1664].reshape(64, 1)), mu_g=A(mu[1664:1792].reshape(128, 1)),
                w0=col(w0), a0=col(a0), k_k=col(k_k), k_a=col(k_a), r_k=col(r_k),
                ww2=A(w_w2[:, hs]), aw2=A(a_w2[:, hs]), gw2=A(g_w2[:, hs]), ln_w=bc(ln_w), ln_b=bc(ln_b))


RWKV_SHAPES = lambda S: dict(rT=[4, 64, S], kT=[4, 64, S], vT=[4, 64, S], wloT=[64, S], aloT=[64, S], gloT=[128, S],
                             mu_r=[64, 4], mu_k=[64, 4], mu_v=[64, 4], mu_w=[64, 1], mu_a=[64, 1], mu_g=[128, 1],
                             w0=[64, 4], a0=[64, 4], k_k=[64, 4], k_a=[64, 4], r_k=[64, 4],
                             ww2=[64, 256], aw2=[64, 256], gw2=[128, 256], ln_w=[64, 256], ln_b=[64, 256])


SEQ = 4096
NPROJ = 5192


def _ein(k, name, shape):
    return k.dram(name, shape, F32, kind="ExternalInput")


class View:
    def __init__(self, base, ap):
        self.t = ap
        self.trk = base.trk
        self.psum = False


NF, NT = 3888, 1304


def proj_col_order():
    F, T = [], []
    r = lambda a, n: list(range(a, a + n))
    for hh in range(2):
        F += r(hh * 256, 256) + r(512 + hh * 64, 64) + r(768 + hh * 64, 64) + r(1024 + hh * 64, 64) + r(640 + hh * 64, 64)
    b = 1304
    for hh in range(2):
        F += r(b + hh * 256, 256) + r(b + 576 + hh * 256, 256) + r(b + 1088 + hh * 256, 256)
    F += r(b + 512, 64) + r(b + 1600, 64) + r(b + 1664, 128)
    b = 3096
    for hh in range(2):
        F += r(b + hh * 128, 128) + r(b + 256 + hh * 128, 128)
    F += r(b + 1024, 16)
    F += r(4648, 544)
    for hh in range(2):
        T += r(896 + hh * 64, 64) + r(1152 + hh * 64, 64) + r(1280 + hh * 12, 12)
    for hh in range(2):
        T += r(3096 + 512 + hh * 256, 256) + r(3096 + 1040 + hh * 256, 256)
    assert len(F) == NF and len(T) == NT and len(set(F + T)) == NF + NT
    return np.array(F), np.array(T)


def emit_mod_fm(k, c2, w, badd, gpre, gpost, vecs, pre=None):
    if pre is not None:
        pre()
    wv = w.t.rearrange("(kc p) f -> p kc f", p=128)
    with ExitStack() as es:
        ct = k.sb([128, KC, 2], F32, es=es)
        sc = k.sb([128, KC, 2], F32, es=es)
        bt = k.sb([128, 18, KC], F32, es=es)
        gp1 = k.sb([128, 6, KC], F32, es=es)
        gp2 = k.sb([128, 6, KC], F32, es=es)
        mt = k.sb([128, 18, KC], F32, es=es)
        wt = k.rot(2, [128, KC, 512], F32, es=es)
        v5 = k.rot(2, [128, 5, KC], F32, es=es)
        k.dma("sp", ct[:], c2.t[:, :, :], R=[c2], W=[ct])
        k.dma("sp", bt[:], badd.t[:, :, :], R=[badd], W=[bt])
        k.dma("sp", gp1[:], gpre.t[:, :, :], R=[gpre], W=[gp1])
        k.dma("sp", gp2[:], gpost.t[:, :, :], R=[gpost], W=[gp2])
        k.act(sc[:], ct[:], AF.Silu, R=[ct], W=[sc])
        ps = Rot(k.banks[0:2])
        for grp in range(18):
            p_ = ps.next()
            for q in range(4):
                w_ = wt.next()
                for q2 in range(4):
                    k.dma("sp", w_[:, 4 * q2:4 * q2 + 4, :], wv[:, 4 * q2:4 * q2 + 4, grp * D + q * 512:grp * D + (q + 1) * 512], R=[w], W=[w_])
                for m in range(4):
                    ko = q * 4 + m
                    for kc in range(KC):
                        k.mm(p_[:, 2 * ko:2 * ko + 2], w_[:, kc, m * 128:(m + 1) * 128], sc[:, kc, :], kc == 0, kc == KC - 1, R=[w_, sc], W=[p_])
            k.tt("dve", mt[:, grp, :], p_[:, 0:2 * KC].rearrange("p (c two) -> p c two", two=2)[:, :, 0], bt[:, grp, :], ALU.add,
                 R=[p_, bt], W=[mt])
        for ls in range(6):
            v_ = v5.next()
            k.copy("dve", v_[:, 0:3, :], mt[:, 3 * ls:3 * ls + 3, :], R=[mt], W=[v_])
            k.copy("dve", v_[:, 3, :], gp1[:, ls, :], R=[gp1], W=[v_])
            k.copy("dve", v_[:, 4, :], gp2[:, ls, :], R=[gp2], W=[v_])
            k.dma("sp", vecs[ls].t[:, :, :], v_[:], R=[v_], W=[vecs[ls]])
        k.barrier()


def emit_proj2(k, xT, wF, wT, vec, pF, pTok, T):
    xTv = fview(xT)
    with ExitStack() as es:
        N = NormCtx(k, es, vec, 1.0)
        uTs = [k.sb([128, KC, TB], BF16, es=es) for _ in range(2)]
        wb = k.rot(4, [128, KC, 128], BF16, es=es)
        ob = k.rot(4, [128, TB], F32, es=es)
        wtb = k.rot(2, [128, KC, 512], BF16, es=es)
        ps = Rot(k.banks[1:7])
        nch = (NF + 127) // 128
        tgroups = [(0, 512), (512, 512), (1024, NT - 1024)]
        N.prenorm(xT, xTv, 0, uTs[0])
        for p in range(T // TB):
            t0 = p * TB
            uT = uTs[p % 2]
            if p + 1 < T // TB:
                N.prenorm(xT, xTv, t0 + TB, uTs[(p + 1) % 2])
            for c in range(nch):
                c0 = c * 128
                m = min(128, NF - c0)
                b = wb.next()
                k.dma("sp", b[:, :, :].rearrange("p a b -> p (a b)"), wF.t[c], R=[wF], W=[b])
                pp = ps.next()
                for kc in range(KC):
                    k.mm(pp[0:m, :], b[:, kc, 0:m], uT[:, kc, :], kc == 0, kc == KC - 1, R=[b, uT], W=[pp])
                o = ob.next()
                if c % 2 == 0:
                    k.copy("act", o[0:m, :], pp[0:m, :], R=[pp], W=[o])
                    k.dma("act", pF.t[c0:c0 + m, t0:t0 + TB], o[0:m, :], R=[o], W=[pF])
                else:
                    k.copy("dve", o[0:m, :], pp[0:m, :], R=[pp], W=[o])
                    k.dma("pool", pF.t[c0:c0 + m, t0:t0 + TB], o[0:m, :], R=[o], W=[pF])
            for gi, (g0, gn) in enumerate(tgroups):
                wt_ = wtb.next()
                k.dma("sp", wt_[:, :, :].rearrange("p a b -> p (a b)"), wT.t[gi], R=[wT], W=[wt_])
                for sub in range(4):
                    pp = ps.next()
                    for kc in range(KC):
                        k.mm(pp[:, 0:gn], uT[:, kc, sub * 128:(sub + 1) * 128], wt_[:, kc, 0:gn], kc == 0, kc == KC - 1, R=[uT, wt_], W=[pp])
                    o = ob.next()
                    tk = t0 + sub * 128
                    if sub % 2 == 0:
                        k.copy("act", o[:, 0:gn], pp[:, 0:gn], R=[pp], W=[o])
                        k.dma("act", pTok.t[tk:tk + 128, g0:g0 + gn], o[:, 0:gn], R=[o], W=[pTok])
                    else:
                        k.copy("dve", o[:, 0:gn], pp[:, 0:gn], R=[pp], W=[o])
                        k.dma("pool", pTok.t[tk:tk + 128, g0:g0 + gn], o[:, 0:gn], R=[o], W=[pTok])
        k.barrier()


_ACT_KEYS = {"nsa": ("qT", "qsT", "kT", "ksT", "vcT", "v_tok", "gl"), "rwkv": ("rT", "kT", "vT", "wloT", "aloT", "gloT"),
             "gla": ("qT", "kT", "v_ch", "r_ch", "aloT"), "mla": ("cqT", "ckvT", "krp", "krsp")}
_SHARED_TABLES = ("cos", "sin", "ov", "selb", "ebig", "c96", "s96")


def _param_shapes():
    out = {}
    for nm, shp in (("nsa", NSA_SHAPES(SEQ)), ("rwkv", RWKV_SHAPES(SEQ)), ("gla", GLA_SHAPES(SEQ)), ("mla", MLA_SHAPES(SEQ))):
        out[nm] = {n: s for n, s in shp.items() if n not in _ACT_KEYS[nm]}
    return out


def mixer_views(pF, pTok, hh):
    f, t = pF.t, pTok.t
    V = lambda ap, base: View(base, ap)
    b = hh * 512
    nsa = dict(qT=V(f[b:b + 256, :].rearrange("(g d) s -> g d s", d=64), pF),
               kT=V(f[b + 256:b + 448, :].rearrange("(i d) s -> i d s", d=64), pF),
               vcT=V(f[b + 448:b + 512, :], pF),
               v_tok=V(t[:, hh * 140:hh * 140 + 128].rearrange("s (i d) -> i s d", d=64), pTok),
               gl=V(t[:, hh * 140 + 128:hh * 140 + 140], pTok))
    b = 1024 + hh * 768
    rw = dict(rT=V(f[b:b + 256, :].rearrange("(g d) s -> g d s", d=64), pF),
              kT=V(f[b + 256:b + 512, :].rearrange("(g d) s -> g d s", d=64), pF),
              vT=V(f[b + 512:b + 768, :].rearrange("(g d) s -> g d s", d=64), pF),
              wloT=V(f[2560:2624, :], pF), aloT=V(f[2624:2688, :], pF), gloT=V(f[2688:2816, :], pF))
    b = 2816 + hh * 256
    tb = 280 + hh * 512
    gla = dict(qT=V(f[b:b + 128, :].rearrange("(g d) s -> g d s", d=64), pF),
               kT=V(f[b + 128:b + 256, :].rearrange("(g d) s -> g d s", d=64), pF),
               aloT=V(f[3328:3344, :], pF),
               v_ch=V(t[:, tb:tb + 256].rearrange("(c p) f -> p c f", p=CH), pTok),
               r_ch=V(t[:, tb + 256:tb + 512].rearrange("(c p) f -> p c f", p=CH), pTok))
    mla = dict(cqT=V(f[3344:3728, :], pF), ckvT=V(f[3728:3856, :], pF), krT=V(f[3856:3888, :], pF))
    return dict(nsa=nsa, rwkv=rw, gla=gla, mla=mla)


def build_fused(L=2):
    nc, es, k = new_prog()
    S = SEQ
    PS = _param_shapes()
    emits = dict(nsa=emit_nsa, rwkv=emit_rwkv, gla=emit_gla, mla=emit_mla)
    with es:
        xT = _ein(k, "xT", [D, S])
        c2, wmod = _ein(k, "c2", [128, KC, 2]), _ein(k, "wmod", [D, 18 * D])
        badd, gpre, gpost = _ein(k, "badd", [128, 18, KC]), _ein(k, "gpre", [128, 6, KC]), _ein(k, "gpost", [128, 6, KC])
        tables = {n: _ein(k, "tab_" + n, (NSA_SHAPES(S) | MLA_SHAPES(S))[n]) for n in _SHARED_TABLES}
        oT = k.dram("oT", [D, S], F32, kind="ExternalOutput")
        vecs = [k.dram("vec%d" % i, [128, 5, KC], F32) for i in range(3 * L)]
        xa, xb_ = k.dram("xa", [D, S], F32), k.dram("xb", [D, S], F32)
        pF, pTok, yTok = k.dram("pF", [NF, S], F32), k.dram("pTok", [S, NT], F32), k.dram("yTok", [S, D], F32)
        WSH = dict(wg1=(NFF, KC * 128, 2048), wu1=(NFF, KC * 128, 2048), wd1=(KC, NFF * 128, 1408), wF=(31, KC * 128, 2048), wT=(3, KC * 512, 2048),
                   wgt=(4 * KC, KC * 128, 2048), wbr=(KC, 16 * 128, 2048), wout=(KC, KC * 128, 2048),
                   wg2=(NFF, KC * 128, 2048), wu2=(NFF, KC * 128, 2048), wd2=(KC, NFF * 128, 1408))
        Wf = [{n: _ein(k, "l%d_%s" % (l, n), [c, 128, r]) for n, (c, r, _) in WSH.items()} for l in range(L)]
        Wb = [{n: k.dram("l%d_%s_bf" % (l, n), [c, 128, r], BF16) for n, (c, r, _) in WSH.items()} for l in range(L)]
        groups = []
        for l in range(L):
            groups += [(l, ("wg1", "wu1", "wd1")), (l, ("wF", "wT")), (l, ("wgt", "wbr", "wout")), (l, ("wg2", "wu2", "wd2"))]
        state = {"next": 0}

        def convert_ahead(upto):
            while state["next"] < min(upto, len(groups)):
                l_, names = groups[state["next"]]
                for n in names:
                    emit_convert(k, Wf[l_][n], Wb[l_][n], WSH[n][2])
                state["next"] += 1

        emit_mod_fm(k, c2, wmod, badd, gpre, gpost, vecs, pre=lambda: convert_ahead(2))
        cur = xT
        for l in range(L):
            W = Wb[l]
            g0 = 4 * l
            emit_ffn(k, cur, W["wg1"], W["wu1"], W["wd1"], vecs[3 * l], xa, S, 0.5)
            emit_proj2(k, xa, W["wF"], W["wT"], vecs[3 * l + 1], pF, pTok, S)
            for hh in range(2):
                views = mixer_views(pF, pTok, hh)
                for br, nm in enumerate(("nsa", "rwkv", "gla", "mla")):
                    if br % 2 == 0:
                        convert_ahead(g0 + 3 + hh * 2 + br // 2)
                    I = {}
                    for n, shp in PS[nm].items():
                        I[n] = tables[n] if n in _SHARED_TABLES else _ein(k, "l%d_h%d_%s_%s" % (l, hh, nm, n), shp)
                    I.update(views[nm])
                    yv = View(yTok, yTok.t[:, br * 512 + hh * 256:br * 512 + hh * 256 + 256])
                    emits[nm](k, I, yv, S)
            emit_merge(k, xa, None, W["wgt"], W["wbr"], W["wout"], vecs[3 * l + 1], xb_, S, y_tok=yTok)
            last = l == L - 1
            emit_ffn(k, xb_, W["wg2"], W["wu2"], W["wd2"], vecs[3 * l + 2], oT if last else xa, S, 0.5)
            cur = xa
        k.finish([oT])
    return nc, k.ninstr


def fused_inputs(P, b, L=2):
    A = lambda a: np.ascontiguousarray(np.asarray(a, dtype=np.float32))
    S = SEQ
    m = {}
    m["xT"] = A(np.asarray(P["x"][b]).T)
    cb = np.asarray(P["c"][b], dtype=np.float32)
    m["c2"] = A(np.repeat(cb.reshape(KC, 128).T[:, :, None], 2, axis=2))
    m["wmod"] = P["_wmod"]
    m["badd"] = P["_badd"]
    m["gpre"], m["gpost"] = P["_gpre"], P["_gpost"]
    for n in _SHARED_TABLES:
        m["tab_" + n] = P["_tab"][n]
    for l in range(L):
        for n, v in P["_lw"][l].items():
            m["l%d_%s" % (l, n)] = v
        for hh in range(2):
            for nm in ("nsa", "rwkv", "gla", "mla"):
                for n, v in P["_mp"][l][hh][nm].items():
                    if n not in _SHARED_TABLES:
                        m["l%d_h%d_%s_%s" % (l, hh, nm, n)] = v
    return m


def kernel(**P):
    A = lambda a: np.ascontiguousarray(np.asarray(a, dtype=np.float32))
    L, S = int(np.asarray(P["ada_w"]).shape[0]), SEQ
    Fo, To = proj_col_order()
    fm = lambda v: A(np.asarray(v, dtype=np.float32).reshape(-1, KC, 128).transpose(2, 0, 1))
    P = dict(P)
    P["_wmod"] = A(np.concatenate([np.asarray(P["ada_w"][l, s]) for l in range(L) for s in range(3)], axis=1))
    P["_badd"] = fm(np.asarray(P["ada_b"]).reshape(L * 3 * 3, D))
    P["_gpre"], P["_gpost"] = fm(np.asarray(P["pre_g"]).reshape(L * 3, D)), fm(np.asarray(P["post_g"]).reshape(L * 3, D))
    z = lambda n: np.zeros((S, n), np.float32)
    lw, mp, tab = [], [], {}
    for l in range(L):
        w_in = np.asarray(P["mix_w_in"][l])
        cw = lambda a: chunk_w(a, 128)
        lw.append(dict(wg1=cw(P["ffn_wg"][l, 0]), wu1=cw(P["ffn_wu"][l, 0]), wd1=cw(P["ffn_wd"][l, 0]),
                       wg2=cw(P["ffn_wg"][l, 1]), wu2=cw(P["ffn_wu"][l, 1]), wd2=cw(P["ffn_wd"][l, 1]),
                       wF=cw(w_in[:, Fo]), wT=chunk_w(w_in[:, To], 512), wgt=cw(w_in[:, NPROJ:]),
                       wbr=cw(np.asarray(P["mix_w_branch"][l]).reshape(D, D)), wout=cw(P["mix_w_out"][l])))
        lw[-1] = {n: np.ascontiguousarray(v.reshape(v.shape[0], 128, -1)) for n, v in lw[-1].items()}
        per_hh = []
        for hh in range(2):
            g = lambda name: np.asarray(P[name][l])
            d = dict(nsa=nsa_inputs(z(1304), g("nsa_cmp_pos"), g("nsa_cmp_w1"), g("nsa_cmp_w2"), hh, S),
                     rwkv=rwkv_inputs(z(1792), *[g(n) for n in ("rwkv_mu", "rwkv_w0", "rwkv_w_w2", "rwkv_a0", "rwkv_a_w2", "rwkv_g_w2",
                                                                "rwkv_k_k", "rwkv_k_a", "rwkv_r_k", "rwkv_ln_w", "rwkv_ln_b")], hh, S),
                     gla=gla_inputs(z(1552), g("gla_alpha_w2"), g("gla_alpha_b"), g("gla_norm_g"), hh, S),
                     mla=mla_inputs(z(544), g("mla_w_uq"), g("mla_w_ukv"), g("mla_q_norm"), g("mla_kv_norm"), hh, S))
            for nm in d:
                for n in list(d[nm]):
                    if n in _SHARED_TABLES:
                        tab[n] = A(d[nm][n])
                    if n in _ACT_KEYS[nm]:
                        del d[nm][n]
                    else:
                        d[nm][n] = A(d[nm][n])
            per_hh.append(d)
        mp.append(per_hh)
    P["_lw"], P["_mp"], P["_tab"] = lw, mp, tab
    nc, _ = build_fused(L)
    in_maps = [fused_inputs(P, c % 4, L) for c in range(NCORES)]
    res = run_bass_kernel_spmd(nc, in_maps, core_ids=list(range(NCORES))).results
    out = np.empty((4, S, D), np.float32)
    for b in range(4):
        out[b] = res[b]["oT"].T
    return out
```

```python
import numpy as np
from contextlib import ExitStack
import concourse.bass as bass
import concourse.mybir as mybir
from concourse.bass_utils import run_bass_kernel_spmd

F32 = mybir.dt.float32
BF16 = mybir.dt.bfloat16
AF = mybir.ActivationFunctionType
ALU = mybir.AluOpType
AX = mybir.AxisListType

D = 2048
KC = D // 128
DFF = 5632
NFF = DFF // 128
EPS = 1e-6
NCORES = 8


class Trk:
    __slots__ = ("w", "r")

    def __init__(self):
        self.w = None
        self.r = {}


class Buf:
    def __init__(self, t, trk=None, psum=False):
        self.t = t
        self.trk = trk or Trk()
        self.psum = psum

    def __getitem__(self, idx):
        return self.t[idx]


class K:
    def __init__(self, nc, es, n_dma_sems=40):
        self.nc = nc
        self.es = es
        self.eng = {"pe": nc.tensor, "dve": nc.vector, "act": nc.scalar, "pool": nc.gpsimd, "sp": nc.sync}
        self.esem = {k: es.enter_context(nc.semaphore("es_" + k)) for k in self.eng}
        self.ecnt = {k: 0 for k in self.eng}
        self.waited = {k: {} for k in self.eng}
        self.dsems = [[es.enter_context(nc.semaphore("ds%d" % i)), 0] for i in range(n_dma_sems)]
        self.dnext = 0
        self.uid = 0
        self.ninstr = 0
        self.banks = [Buf(es.enter_context(nc.psum_tensor("bank%d" % i, [128, 512], F32)), psum=True) for i in range(7)]
        self.bankb = Buf(es.enter_context(nc.psum_tensor("bankb", [128, 1024], BF16)), psum=True)

    def sb(self, shape, dt, name=None, es=None):
        self.uid += 1
        return Buf((es or self.es).enter_context(self.nc.sbuf_tensor("sb%d" % self.uid, list(shape), dt)))

    def rot(self, n, shape, dt, es=None):
        return Rot([self.sb(shape, dt, es=es) for _ in range(n)])

    def barrier(self):
        deps = {id(self.esem[e]): (self.esem[e], self.ecnt[e]) for e in self.eng if self.ecnt[e] > 0}
        for s, c in self.dsems:
            if c > 0:
                deps[id(s)] = (s, c)
        for e in self.eng:
            self._wait(e, dict(deps))

    def ps(self, shape, dt=F32, name=None):
        self.uid += 1
        return Buf(self.es.enter_context(self.nc.psum_tensor(name or "ps%d" % self.uid, list(shape), dt)))

    def dram(self, name, shape, dt, kind="Internal"):
        return Buf(self.nc.dram_tensor(name, list(shape), dt, kind=kind).ap())

    def _deps(self, R, W):
        deps = {}

        def add(d):
            if d is None:
                return
            s, v = d
            k = id(s)
            if k not in deps or deps[k][1] < v:
                deps[k] = (s, v)

        for b in R:
            add(b.trk.w)
        for b in W:
            add(b.trk.w)
            for d in b.trk.r.values():
                add(d)
        return deps

    def _wait(self, e, deps, skip_sem=None):
        h = self.eng[e]
        wd = self.waited[e]
        for k, (s, v) in deps.items():
            if skip_sem is not None and s is skip_sem:
                continue
            if wd.get(k, 0) < v:
                h.wait_ge(s, v)
                wd[k] = v

    def _commit(self, d, R, W):
        for b in W:
            b.trk.w = d
            b.trk.r = {}
        for b in R:
            b.trk.r[id(d[0])] = d

    def op(self, e, fn, R=(), W=()):
        if any(b.psum for b in R):
            W = list(W) + [b for b in R if b.psum]
            R = [b for b in R if not b.psum]
        deps = self._deps(R, W)
        self._wait(e, deps, skip_sem=self.esem["pe"] if e == "pe" else None)
        ins = fn(self.eng[e])
        self.ecnt[e] += 1
        ins.then_inc(self.esem[e], 1)
        self.ninstr += 1
        self._commit((self.esem[e], self.ecnt[e]), R, W)

    def dma(self, q, out_ap, in_ap, R=(), W=(), **kw):
        deps = self._deps(R, W)
        slot = self.dsems[self.dnext]
        self.dnext = (self.dnext + 1) % len(self.dsems)
        if slot[1] > 0:
            deps[id(slot[0])] = (slot[0], slot[1])
        self._wait(q, deps)
        ins = self.eng[q].dma_start(out=out_ap, in_=in_ap, **kw)
        slot[1] += 16
        ins.then_inc(slot[0], 16)
        self.ninstr += 1
        self._commit((slot[0], slot[1]), R, W)

    def finish(self, bufs, e="sp"):
        deps = self._deps(bufs, ())
        self._wait(e, deps)

    def mm(self, out, lhsT, rhs, start, stop, R, W):
        self.op("pe", lambda h: h.matmul(out, lhsT=lhsT, rhs=rhs, start=start, stop=stop), R=R, W=W)

    def act(self, out, in_, func, R, W, e="act", **kw):
        self.op(e, lambda h: h.activation(out=out, in_=in_, func=func, **kw), R=R, W=W)

    def tt(self, e, out, in0, in1, op, R, W):
        self.op(e, lambda h: h.tensor_tensor(out=out, in0=in0, in1=in1, op=op), R=R, W=W)

    def ts(self, e, out, in0, s1, s2, op0, op1, R, W):
        if op1 is None:
            self.op(e, lambda h: h.tensor_scalar(out=out, in0=in0, scalar1=s1, scalar2=None, op0=op0), R=R, W=W)
        else:
            self.op(e, lambda h: h.tensor_scalar(out=out, in0=in0, scalar1=s1, scalar2=s2, op0=op0, op1=op1), R=R, W=W)

    def stt(self, out, in0, scalar, in1, op0, op1, R, W):
        self.op("dve", lambda h: h.scalar_tensor_tensor(out=out, in0=in0, scalar=scalar, in1=in1, op0=op0, op1=op1), R=R, W=W)

    def copy(self, e, out, in_, R, W):
        if e == "act":
            self.op(e, lambda h: h.copy(out=out, in_=in_), R=R, W=W)
        else:
            self.op(e, lambda h: h.tensor_copy(out=out, in_=in_), R=R, W=W)

    def cast(self, out, in_, R, W):
        self.ncast = getattr(self, "ncast", 0) + 1
        self.copy("dve" if self.ncast % 2 else "act", out, in_, R, W)

    def memset(self, e, ap, val, W):
        self.op(e, lambda h: h.memset(ap, val), W=W)


class Rot:
    def __init__(self, bufs):
        self.bufs = bufs
        self.i = 0

    def next(self):
        b = self.bufs[self.i]
        self.i = (self.i + 1) % len(self.bufs)
        return b


TB = 512


class NormCtx:
    def __init__(self, k, es, vec, rw, post_bank=None):
        self.k = k
        self.big = k.sb([128, KC, TB], F32, es=es)
        self.xin = k.rot(2, [128, 4, TB], F32, es=es)
        self.sq = k.rot(2, [128, 4, TB], BF16, es=es)
        self.tmp = k.rot(2, [128, TB], F32, es=es)
        self.xr = k.rot(2, [128, TB], F32, es=es)
        self.ob = k.rot(2, [128, TB], F32, es=es)
        self.rstd = k.sb([128, TB], F32, es=es)
        self.rt = k.sb([128, TB], F32, es=es)
        self.rstd_pre = k.sb([128, TB], F32, es=es)
        self.rt_pre = k.sb([128, TB], F32, es=es)
        self.ones = k.sb([128, 128], BF16, es=es)
        self.vt = k.sb([128, 5, KC], F32, es=es)
        self.sc = k.sb([128, KC], F32, es=es)
        self.gp = k.sb([128, KC], F32, es=es)
        self.epsb = k.sb([128, 1], F32, es=es)
        self.ps_pre = k.banks[0]
        self.ps_stat = post_bank if post_bank is not None else k.banks[0]
        k.memset("dve", self.ones[:], 1.0, W=[self.ones])
        k.memset("dve", self.epsb[:], EPS, W=[self.epsb])
        k.dma("sp", self.vt[:], vec.t[:, :, :], R=[vec], W=[self.vt])
        k.stt(self.sc[:], self.vt[:, 1, :], 1.0, self.vt[:, 3, :], ALU.add, ALU.mult, R=[self.vt], W=[self.sc])
        k.stt(self.gp[:], self.vt[:, 2, :], float(rw), self.vt[:, 4, :], ALU.mult, ALU.mult, R=[self.vt], W=[self.gp])

    def rstd_from(self, ps, dim, rt=None, rstd=None):
        k = self.k
        rt = rt or self.rt
        rstd = rstd or self.rstd
        k.act(rt[:], ps[:], AF.Sqrt, R=[ps, self.epsb], W=[rt], bias=self.epsb[:], scale=1.0 / dim)
        k.op("dve", lambda h: h.reciprocal(out=rstd[:], in_=rt[:]), R=[rt], W=[rstd])

    def prenorm(self, xT, xTv, t0, uT):
        k = self.k
        for q in range(4):
            xp = self.xin.next()
            k.dma("pool", xp[:], xTv[:, 4 * q:4 * q + 4, t0:t0 + TB], R=[xT], W=[xp])
            s = self.sq.next()
            k.act(s[:], xp[:], AF.Square, R=[xp], W=[s])
            for i in range(4):
                kc = 4 * q + i
                k.mm(self.ps_pre[:], self.ones[:], s[:, i, :], kc == 0, kc == KC - 1, R=[self.ones, s], W=[self.ps_pre])
        self.rstd_from(self.ps_pre, D, self.rt_pre, self.rstd_pre)
        for q in range(4):
            xp = self.xin.next()
            k.dma("pool", xp[:], xTv[:, 4 * q:4 * q + 4, t0:t0 + TB], R=[xT], W=[xp])
            for i in range(4):
                kc = 4 * q + i
                tm = self.tmp.next()
                k.stt(tm[:], xp[:, i, :], self.sc[:, kc:kc + 1], self.rstd_pre[:], ALU.mult, ALU.mult,
                      R=[xp, self.sc, self.rstd_pre], W=[tm])
                k.act(uT[:, kc, :], tm[:], AF.Identity, R=[tm, self.vt], W=[uT], bias=self.vt[:, 0, kc:kc + 1], scale=1.0)

    def take(self, py, dc):
        k = self.k
        k.copy("act", self.big[:, dc, :], py[:], R=[py], W=[self.big])
        s = self.sq.next()
        k.act(s[:, 0, :], py[:], AF.Square, R=[py], W=[s])
        k.mm(self.ps_stat[:], self.ones[:], s[:, 0, :], dc == 0, dc == KC - 1, R=[self.ones, s], W=[self.ps_stat])

    def postnorm(self, xT, xTv, oT, oTv, t0):
        k = self.k
        self.rstd_from(self.ps_stat, D)
        xs = {}
        for kc in range(KC):
            for k2 in range(kc, min(kc + 2, KC)):
                if k2 not in xs:
                    xs[k2] = self.xr.next()
                    k.dma("pool", xs[k2][:], xTv[:, k2, t0:t0 + TB], R=[xT], W=[xs[k2]])
            x_ = xs[kc]
            tm = self.tmp.next()
            k.stt(tm[:], self.big[:, kc, :], self.gp[:, kc:kc + 1], self.rstd[:], ALU.mult, ALU.mult,
                  R=[self.big, self.gp, self.rstd], W=[tm])
            o = self.ob.next()
            k.tt("pool", o[:], tm[:], x_[:], ALU.add, R=[tm, x_], W=[o])
            k.dma("pool", oTv[:, kc, t0:t0 + TB], o[:], R=[o], W=[oT])


def fview(b):
    return b.t.rearrange("(kc p) t -> p kc t", p=128)


def emit_ffn(k, xT, wg, wu, wd, vec, oT, T, rw):
    xTv, oTv = fview(xT), fview(oT)
    with ExitStack() as es:
        N = NormCtx(k, es, vec, rw, post_bank=k.banks[6])
        uTs = [k.sb([128, KC, TB], BF16, es=es) for _ in range(2)]
        aT = k.sb([128, NFF, TB], BF16, es=es)
        wgb = k.rot(3, [128, KC, 128], BF16, es=es)
        wub = k.rot(3, [128, KC, 128], BF16, es=es)
        wdb = k.rot(2, [128, NFF, 128], BF16, es=es)
        sgb = k.rot(2, [128, TB], F32, es=es)
        ps_g, ps_u, ps_y = Rot(k.banks[1:3]), Rot(k.banks[3:4]), Rot(k.banks[4:6])
        NP = T // TB
        N.prenorm(xT, xTv, 0, uTs[0])
        for p in range(NP):
            t0 = p * TB
            uT = uTs[p % 2]
            for j in range(NFF):
                wb = []
                for (wsrc, pool) in ((wg, wgb), (wu, wub)):
                    b = pool.next()
                    k.dma("sp", b[:, :, :].rearrange("p a b -> p (a b)"), wsrc.t[j], R=[wsrc], W=[b])
                    wb.append(b)
                pg, pu = ps_g.next(), ps_u.next()
                for kc in range(KC):
                    k.mm(pg[:], wb[0][:, kc, :], uT[:, kc, :], kc == 0, kc == KC - 1, R=[wb[0], uT], W=[pg])
                for kc in range(KC):
                    k.mm(pu[:], wb[1][:, kc, :], uT[:, kc, :], kc == 0, kc == KC - 1, R=[wb[1], uT], W=[pu])
                sg = sgb.next()
                k.act(sg[:], pg[:], AF.Silu, R=[pg], W=[sg])
                k.tt("dve", aT[:, j, :], sg[:], pu[:], ALU.mult, R=[sg, pu], W=[aT])
            if p + 1 < NP:
                N.prenorm(xT, xTv, t0 + TB, uTs[(p + 1) % 2])
            for dc in range(KC):
                b = wdb.next()
                k.dma("sp", b[:, :, :].rearrange("p a b -> p (a b)"), wd.t[dc], R=[wd], W=[b])
                py = ps_y.next()
                for j in range(NFF):
                    k.mm(py[:], b[:, j, :], aT[:, j, :], j == 0, j == NFF - 1, R=[b, aT], W=[py])
                N.take(py, dc)
            N.postnorm(xT, xTv, oT, oTv, t0)
        k.barrier()


def emit_convert(k, w, wb, piece):
    n, _, R_ = w.t.shape
    for c in range(n):
        for r0 in range(0, R_, piece):
            k.dma("pool", wb.t[c][:, r0:r0 + piece], w.t[c][:, r0:r0 + piece], R=[w], W=[wb])


def emit_merge(k, xT, yT, wgt, wbr, wout, vec, oT, T, y_tok=None):
    xTv, oTv = fview(xT), fview(oT)
    if y_tok is None:
        yTv = yT.t.rearrange("(c p) t -> p c t", p=128)
    with ExitStack() as es:
        N = NormCtx(k, es, vec, 1.0, post_bank=k.banks[4])
        uTs = [k.sb([128, KC, TB], BF16, es=es) for _ in range(2)]
        yb = k.sb([128, 16, TB], BF16, es=es)
        mT = k.sb([128, KC, TB], BF16, es=es)
        ystg = k.rot(2, [128, 2, TB], F32, es=es)
        if y_tok is not None:
            ytk_b = k.rot(2, [128, 2048], BF16, es=es)
            identb = k.sb([128, 128], BF16, es=es)
            k.memset("pool", identb[:], 1.0, W=[identb])
            k.op("pool", lambda h: h.affine_select(out=identb[:], in_=identb[:], pattern=[[-1, 128]], compare_op=ALU.is_equal,
                                                  fill=0.0, base=0, channel_multiplier=1), R=[identb], W=[identb])
        wgb = k.rot(4, [128, KC, 128], BF16, es=es)
        wbb = k.rot(2, [128, 16, 128], BF16, es=es)
        wob = k.rot(2, [128, KC, 128], BF16, es=es)
        sgb = k.rot(2, [128, TB], F32, es=es)
        mb = k.rot(2, [128, TB], F32, es=es)
        macc = k.rot(2, [128, TB], F32, es=es)
        ps_g, ps_b, ps_y = Rot(k.banks[1:3]), Rot(k.banks[3:4]), Rot(k.banks[5:7])
        N.prenorm(xT, xTv, 0, uTs[0])
        for p in range(T // TB):
            t0 = p * TB
            uT = uTs[p % 2]
            if y_tok is None:
                for c2 in range(8):
                    st = ystg.next()
                    k.dma("sp", st[:], yTv[:, 2 * c2:2 * c2 + 2, t0:t0 + TB], R=[yT], W=[st])
                    k.cast(yb[:, 2 * c2:2 * c2 + 2, :], st[:], R=[st], W=[yb])
            else:
                for sub in range(4):
                    tk = t0 + sub * 128
                    ytb = ytk_b.next()
                    for hf in range(2):
                        st = ystg.next()
                        k.dma("sp", st[:, :, :].rearrange("p a b -> p (a b)"), y_tok.t[tk:tk + 128, hf * 1024:(hf + 1) * 1024], R=[y_tok], W=[st])
                        k.cast(ytb[:, hf * 1024:(hf + 1) * 1024], st[:, :, :].rearrange("p a b -> p (a b)"), R=[st], W=[ytb])
                    for c4 in range(4):
                        pb = k.bankb
                        for i in range(4):
                            c = 4 * c4 + i
                            k.op("pe", lambda h: h.transpose(out=pb[:, i * 128:(i + 1) * 128], in_=ytb[:, c * 128:(c + 1) * 128],
                                                             identity=identb[:, :]), R=[ytb, identb], W=[pb])
                        k.copy("act", yb[:, 4 * c4:4 * c4 + 4, sub * 128:(sub + 1) * 128],
                               pb[:, 0:512].rearrange("p (c t) -> p c t", t=128), R=[pb], W=[yb])
            for dc in range(KC):
                wbt = wbb.next()
                k.dma("sp", wbt[:, :, :].rearrange("p a b -> p (a b)"), wbr.t[dc], R=[wbr], W=[wbt])
                acc = macc.next()
                for br in range(4):
                    wg_ = wgb.next()
                    k.dma("sp", wg_[:, :, :].rearrange("p a b -> p (a b)"), wgt.t[br * KC + dc], R=[wgt], W=[wg_])
                    pg, pb = ps_g.next(), ps_b.next()
                    for kc in range(KC):
                        k.mm(pg[:], wg_[:, kc, :], uT[:, kc, :], kc == 0, kc == KC - 1, R=[wg_, uT], W=[pg])
                    for kc in range(4):
                        k.mm(pb[:], wbt[:, br * 4 + kc, :], yb[:, br * 4 + kc, :], kc == 0, kc == 3, R=[wbt, yb], W=[pb])
                    sg = sgb.next()
                    k.act(sg[:], pg[:], AF.Sigmoid, R=[pg], W=[sg])
                    if br == 0:
                        k.tt("dve", acc[:], sg[:], pb[:], ALU.mult, R=[sg, pb], W=[acc])
                    else:
                        m_ = mb.next()
                        k.tt("dve", m_[:], sg[:], pb[:], ALU.mult, R=[sg, pb], W=[m_])
                        if br < 3:
                            k.tt("pool", acc[:], acc[:], m_[:], ALU.add, R=[acc, m_], W=[acc])
                        else:
                            k.tt("pool", mT[:, dc, :], acc[:], m_[:], ALU.add, R=[acc, m_], W=[mT])
            if p + 1 < T // TB:
                N.prenorm(xT, xTv, t0 + TB, uTs[(p + 1) % 2])
            for oc in range(KC):
                wo_ = wob.next()
                k.dma("sp", wo_[:, :, :].rearrange("p a b -> p (a b)"), wout.t[oc], R=[wout], W=[wo_])
                py = ps_y.next()
                for dc in range(KC):
                    k.mm(py[:], wo_[:, dc, :], mT[:, dc, :], dc == 0, dc == KC - 1, R=[wo_, mT], W=[py])
                N.take(py, oc)
            N.postnorm(xT, xTv, oT, oTv, t0)
        k.barrier()


def chunk_w(w, nc_):
    w = np.asarray(w, dtype=np.float32)
    K_, N_ = w.shape
    npad = (-N_) % nc_
    if npad:
        w = np.concatenate([w, np.zeros((K_, npad), np.float32)], axis=1)
    return np.ascontiguousarray(w.reshape(K_ // 128, 128, (N_ + npad) // nc_, nc_).transpose(2, 1, 0, 3))


def new_prog():
    nc = bass.Bass("TRN2", target_bir_lowering=False)
    es = ExitStack()
    k = K(nc, es)
    return nc, es, k


AHEAD = 2


def acc_view(acc):
    return acc[:, :].rearrange("p (s c) -> p s c", c=128)


def attn_chunk(k, q_ap, qbufs, pairs, acc, W, scale, sbanks, ptr, q0, acc2=None, W2=0):
    state = {"first": True}

    def scores(pr):
        nk = pr["nk"]
        ps = sbanks.next()
        k.mm(ps[0:nk, :], pr["kT"], q_ap, True, pr.get("extra") is None, R=pr["bufs"] + qbufs, W=[ps])
        if pr.get("extra") is not None:
            el, er, eb = pr["extra"]
            k.mm(ps[0:nk, :], el, er, False, True, R=eb, W=[ps])
        pt = ptr.next()
        k.act(pt[0:nk, :], ps[0:nk, :], AF.Exp, R=[ps], W=[pt], scale=scale)
        if pr.get("mask") is not None:
            base, cm, step = pr["mask"]
            k.op("pool", lambda h: h.affine_select(out=pt[0:nk, :], in_=pt[0:nk, :], pattern=[[step, 512]],
                                                  compare_op=ALU.is_ge, fill=0.0, base=base, channel_multiplier=cm),
                 R=[pt], W=[pt])
        return pt

    def pv(pr, pt):
        nk = pr["nk"]
        for sub in range(4):
            if pr.get("kpos0") is not None and pr["kpos0"] > q0 + sub * 128 + 127:
                continue
            if sub in pr.get("skip", ()):
                continue
            c0 = sub * 128
            fst = state["first"]
            k.op("pe", lambda h: h.matmul(acc[:, c0:c0 + W], lhsT=pt[0:nk, c0:c0 + 128], rhs=pr["v"], start=fst, stop=True,
                                          skip_group_check=True), R=[pt] + pr["bufs"], W=[acc])
            if acc2 is not None:
                k.op("pe", lambda h: h.matmul(acc2[:, c0:c0 + W2], lhsT=pt[0:nk, c0:c0 + 128], rhs=pr["v2"], start=fst, stop=True,
                                              skip_group_check=True), R=[pt] + pr["bufs"], W=[acc2])
            state["first"] = False

    pend = []
    for pr in pairs:
        pend.append((pr, scores(pr)))
        if len(pend) > AHEAD:
            pv(*pend.pop(0))
    while pend:
        pv(*pend.pop(0))


def rstd_calc(k, ps, nparts, dim, rt, rstd, epsb, eps=EPS):
    k.act(rt[0:nparts, :], ps[0:nparts, :], AF.Sqrt, R=[ps, epsb], W=[rt], bias=epsb[0:nparts, :], scale=1.0 / dim)
    k.op("dve", lambda h: h.reciprocal(out=rstd[0:nparts, :], in_=rt[0:nparts, :]), R=[rt], W=[rstd])


def emit_mla(k, I, y, S):
    nkb, nqc = S // 128, S // 512
    SC = 96 ** -0.5
    with ExitStack() as es:
        QT = k.sb([96, 4, S], BF16, es=es)
        KT = k.sb([96, 4, S], BF16, es=es)
        Va = k.sb([128, 4, nkb, 65], BF16, es=es)
        ysb = k.sb([128, nkb, 256], F32, es=es)
        ones = k.sb([128, 128], BF16, es=es)
        epsb = k.sb([128, 1], F32, es=es)
        k.memset("dve", ones[:], 1.0, W=[ones])
        k.memset("dve", epsb[:], EPS, W=[epsb])
        k.memset("pool", Va[:, :, :, 64:65], 1.0, W=[Va])
        with ExitStack() as e1:
            wuq = k.sb([128, 3, 384], BF16, es=e1)
            wuqs = k.sb([128, 3, 384], BF16, es=e1)
            wk = k.sb([128, 256], BF16, es=e1)
            wv = k.sb([128, 256], BF16, es=e1)
            qn = k.sb([128, 3], F32, es=e1)
            kvn = k.sb([128, 1], F32, es=e1)
            wst = k.rot(2, [128, 3, 384], F32, es=e1)
            for dst, src in ((wuq, I["wuq"]), (wuqs, I["wuqs"])):
                st = wst.next()
                k.dma("sp", st[:], src.t.rearrange("(kc p) f -> p kc f", p=128), R=[src], W=[st])
                k.copy("pool", dst[:], st[:], R=[st], W=[dst])
            for dst, src in ((wk, I["wk"]), (wv, I["wv"])):
                st = wst.next()
                k.dma("sp", st[:, 0, 0:256], src.t[:, :], R=[src], W=[st])
                k.copy("pool", dst[:], st[:, 0, 0:256], R=[st], W=[dst])
            k.dma("sp", qn[:], I["qn"].t[:, :], R=[I["qn"]], W=[qn])
            k.dma("sp", kvn[:], I["kvn"].t[:, :], R=[I["kvn"]], W=[kvn])
            cqb = k.rot(2, [128, 3, TB], F32, es=e1)
            sq = k.rot(2, [128, 3, TB], BF16, es=e1)
            cqn = k.rot(2, [128, 3, TB], BF16, es=e1)
            ckb = k.rot(2, [128, TB], F32, es=e1)
            ckn = k.rot(2, [128, TB], BF16, es=e1)
            ctab = k.rot(2, [96, TB], F32, es=e1)
            stab = k.rot(2, [96, TB], F32, es=e1)
            krb = k.rot(2, [96, TB], F32, es=e1)
            krsb = k.rot(2, [96, TB], F32, es=e1)
            t1r = k.rot(3, [96, TB], F32, es=e1)
            t2r = k.rot(3, [96, TB], F32, es=e1)
            rt = k.sb([128, TB], F32, es=e1)
            rstd = k.sb([128, TB], F32, es=e1)
            rstd2 = k.sb([128, TB], F32, es=e1)
            psr = Rot(k.banks[1:7])
            cqv = I["cqT"].t.rearrange("(kc p) t -> p kc t", p=128)
            for blk in range(S // TB):
                t0 = blk * TB
                cq, ck, ct, stb, kr, krs = cqb.next(), ckb.next(), ctab.next(), stab.next(), krb.next(), krsb.next()
                k.dma("sp", cq[:], cqv[:, :, t0:t0 + TB], R=[I["cqT"]], W=[cq])
                k.dma("sp", ck[:], I["ckvT"].t[:, t0:t0 + TB], R=[I["ckvT"]], W=[ck])
                k.dma("sp", ct[:], I["c96"].t[:, t0:t0 + TB], R=[I["c96"]], W=[ct])
                k.dma("sp", stb[:], I["s96"].t[:, t0:t0 + TB], R=[I["s96"]], W=[stb])
                if "krp" in I:
                    k.dma("sp", kr[:], I["krp"].t[:, t0:t0 + TB], R=[I["krp"]], W=[kr])
                    k.dma("sp", krs[:], I["krsp"].t[:, t0:t0 + TB], R=[I["krsp"]], W=[krs])
                else:
                    k.dma("sp", kr[64:96, :], I["krT"].t[:, t0:t0 + TB], R=[I["krT"]], W=[kr])
                    k.dma("sp", krs[64:80, :], I["krT"].t[16:32, t0:t0 + TB], R=[I["krT"]], W=[krs])
                    k.dma("sp", krs[80:96, :], I["krT"].t[0:16, t0:t0 + TB], R=[I["krT"]], W=[krs])
                s_ = sq.next()
                k.act(s_[:], cq[:], AF.Square, R=[cq], W=[s_])
                ps0 = k.banks[0]
                for kc in range(3):
                    k.mm(ps0[:], ones[:], s_[:, kc, :], kc == 0, kc == 2, R=[ones, s_], W=[ps0])
                rstd_calc(k, ps0, 128, 384, rt, rstd, epsb)
                cn = cqn.next()
                for kc in range(3):
                    k.stt(cn[:, kc, :], cq[:, kc, :], qn[:, kc:kc + 1], rstd[:], ALU.mult, ALU.mult, R=[cq, qn, rstd], W=[cn])
                for h in range(4):
                    p1, p2 = psr.next(), psr.next()
                    for kc in range(3):
                        k.mm(p1[0:96, :], wuq[:, kc, h * 96:(h + 1) * 96], cn[:, kc, :], kc == 0, kc == 2, R=[wuq, cn], W=[p1])
                    for kc in range(3):
                        k.mm(p2[0:96, :], wuqs[:, kc, h * 96:(h + 1) * 96], cn[:, kc, :], kc == 0, kc == 2, R=[wuqs, cn], W=[p2])
                    t1, t2 = t1r.next(), t2r.next()
                    k.tt("dve", t1[:], p1[0:96, :], ct[:], ALU.mult, R=[p1, ct], W=[t1])
                    k.tt("dve", t2[:], p2[0:96, :], stb[:], ALU.mult, R=[p2, stb], W=[t2])
                    k.tt("pool", QT[:, h, t0:t0 + TB], t1[:], t2[:], ALU.add, R=[t1, t2], W=[QT])
                s_ = sq.next()
                k.act(s_[:, 0, :], ck[:], AF.Square, R=[ck], W=[s_])
                k.mm(ps0[:], ones[:], s_[:, 0, :], True, True, R=[ones, s_], W=[ps0])
                rstd_calc(k, ps0, 128, 128, rt, rstd2, epsb)
                kn = ckn.next()
                k.stt(kn[:], ck[:], kvn[:, 0:1], rstd2[:], ALU.mult, ALU.mult, R=[ck, kvn, rstd2], W=[kn])
                for h in range(4):
                    p1 = psr.next()
                    k.mm(p1[0:64, :], wk[:, h * 64:(h + 1) * 64], kn[:], True, True, R=[wk, kn], W=[p1])
                    k.copy("act", KT[0:64, h, t0:t0 + TB], p1[0:64, :], R=[p1], W=[KT])
                t1, t2 = t1r.next(), t2r.next()
                k.tt("dve", t1[64:96, :], kr[64:96, :], ct[64:96, :], ALU.mult, R=[kr, ct], W=[t1])
                k.tt("pool", t2[64:96, :], krs[64:96, :], stb[64:96, :], ALU.mult, R=[krs, stb], W=[t2])
                for h in range(4):
                    k.tt("pool", KT[64:96, h, t0:t0 + TB], t1[64:96, :], t2[64:96, :], ALU.add, R=[t1, t2], W=[KT])
                for sub in range(4):
                    p1 = psr.next()
                    k.mm(p1[:, 0:256], kn[:, sub * 128:(sub + 1) * 128], wv[:], True, True, R=[kn, wv], W=[p1])
                    k.copy("act", Va[:, :, 4 * blk + sub, 0:64], p1[:, 0:256].rearrange("p (h d) -> p h d", d=64), R=[p1], W=[Va])
            k.barrier()
        with ExitStack() as e2:
            ptr = k.rot(4, [128, 512], BF16, es=e2)
            rden = k.rot(2, [128, 4], F32, es=e2)
            sbanks, abanks = Rot(k.banks[0:4]), Rot(k.banks[4:7])
            for h in range(4):
                for qc in range(nqc):
                    q0 = qc * 512
                    acc = abanks.next()
                    pairs = []
                    for kb in range(4 * qc + 4):
                        pairs.append(dict(kT=KT[:, h, kb * 128:(kb + 1) * 128], nk=128, v=Va[:, h, kb, :], bufs=[KT, Va],
                                          mask=(q0 - kb * 128, -1, 1) if kb >= 4 * qc else None, kpos0=kb * 128))
                    attn_chunk(k, QT[:, h, q0:q0 + 512], [QT], pairs, acc, 65, SC, sbanks, ptr, q0)
                    av = acc_view(acc)
                    rd = rden.next()
                    k.op("dve", lambda hh: hh.reciprocal(out=rd[:], in_=av[:, :, 64]), R=[acc], W=[rd])
                    for sub in range(4):
                        o_ap = ysb[:, 4 * qc + sub, h * 64:(h + 1) * 64]
                        if sub % 2 == 0:
                            k.act(o_ap, av[:, sub, 0:64], AF.Copy, R=[acc, rd], W=[ysb], scale=rd[:, sub:sub + 1])
                        else:
                            k.ts("dve", o_ap, av[:, sub, 0:64], rd[:, sub:sub + 1], None, ALU.mult, None, R=[acc, rd], W=[ysb])
            yv = y.t.rearrange("(kb p) f -> p kb f", p=128)
            for q in range(4):
                n4 = nkb // 4
                k.dma("sp", yv[:, q * n4:(q + 1) * n4, :], ysb[:, q * n4:(q + 1) * n4, :], R=[ysb], W=[y])
            k.barrier()


def rope_tables(S, d, rows_before=0):
    inv = 10000.0 ** (-np.arange(0, d, 2, dtype=np.float32) / d)
    ang = np.arange(S, dtype=np.float32)[:, None] * inv[None, :]
    cos, sin = np.cos(ang).T.astype(np.float32), np.sin(ang).T.astype(np.float32)
    c = np.concatenate([np.ones((rows_before, S), np.float32), cos, cos], 0)
    s = np.concatenate([np.zeros((rows_before, S), np.float32), -sin, sin], 0)
    return np.ascontiguousarray(c), np.ascontiguousarray(s)


def mla_inputs(pm, w_uq, w_ukv, q_norm, kv_norm, hh, S):
    cq, ckv, kr = pm[:, :384], pm[:, 384:512], pm[:, 512:544]
    krp = np.zeros((96, S), np.float32)
    krsp = np.zeros((96, S), np.float32)
    krp[64:96] = kr.T
    krsp[64:80], krsp[80:96] = kr.T[16:32], kr.T[0:16]
    wq = w_uq.reshape(384, 8, 96)[:, 4 * hh:4 * hh + 4]
    wqs = wq.copy()
    wqs[:, :, 64:80], wqs[:, :, 80:96] = wq[:, :, 80:96], wq[:, :, 64:80]
    wkv = w_ukv.reshape(128, 8, 128)[:, 4 * hh:4 * hh + 4]
    c96, s96 = rope_tables(S, 32, 64)
    A = np.ascontiguousarray
    return dict(cqT=A(cq.T), ckvT=A(ckv.T), krp=krp, krsp=krsp, wuq=A(wq.reshape(384, 384)), wuqs=A(wqs.reshape(384, 384)),
                wk=A(wkv[:, :, :64].reshape(128, 256)), wv=A(wkv[:, :, 64:].reshape(128, 256)),
                qn=A(q_norm.reshape(3, 128).T), kvn=A(kv_norm.reshape(128, 1)), c96=c96, s96=s96)


MLA_SHAPES = lambda S: dict(cqT=[384, S], ckvT=[128, S], krp=[96, S], krsp=[96, S], wuq=[384, 384], wuqs=[384, 384],
                            wk=[128, 256], wv=[128, 256], qn=[128, 3], kvn=[128, 1], c96=[96, S], s96=[96, S])


def emit_nsa(k, I, y, S):
    nkb, nqc = S // 128, S // 512
    n_cmp = S // 16 - 1
    ncb = (n_cmp + 127) // 128
    n_sel = S // 64
    SC = 0.125
    BIG = 30000.0
    with ExitStack() as es:
        QT = k.sb([64, 4, S], BF16, es=es)
        KsT = k.sb([64, S], BF16, es=es)
        KwT = k.sb([64, S], BF16, es=es)
        Vsa = k.sb([128, nkb, 65], BF16, es=es)
        Vwa = k.sb([128, nkb, 65], BF16, es=es)
        KcmpT = k.sb([64, ncb * 128], BF16, es=es)
        Vca = k.sb([128, ncb, 65], BF16, es=es)
        Ov = k.sb([128, ncb, n_sel], BF16, es=es)
        selb = k.sb([128, nkb, n_sel], F32, es=es)
        gsig = k.sb([128, nkb, 12], F32, es=es)
        Ebig = k.sb([n_sel, S], BF16, es=es)
        ident = k.sb([128, 128], BF16, es=es)
        k.memset("pool", Vsa[:, :, 64:65], 1.0, W=[Vsa])
        k.memset("pool", Vwa[:, :, 64:65], 1.0, W=[Vwa])
        k.memset("pool", Vca[:], 0.0, W=[Vca])
        k.memset("pool", Vca[:, :, 64:65], 1.0, W=[Vca])
        k.memset("pool", KcmpT[:], 0.0, W=[KcmpT])
        k.memset("pool", ident[:], 1.0, W=[ident])
        k.op("pool", lambda h: h.affine_select(out=ident[:], in_=ident[:], pattern=[[-1, 128]], compare_op=ALU.is_equal,
                                              fill=0.0, base=0, channel_multiplier=1), R=[ident], W=[ident])
        with ExitStack() as e1:
            KcT = k.sb([64, S], BF16, es=e1)
            VcT = k.sb([64, S], BF16, es=e1)
            xb, xsb = k.rot(3, [64, TB], F32, es=e1), k.rot(3, [64, TB], F32, es=e1)
            cb, sb_ = k.rot(2, [64, TB], F32, es=e1), k.rot(2, [64, TB], F32, es=e1)
            t1r, t2r = k.rot(2, [64, TB], F32, es=e1), k.rot(2, [64, TB], F32, es=e1)
            st8 = k.rot(2, [128, nkb // 4, 64], F32, es=e1)
            for blk in range(S // TB):
                t0 = blk * TB
                c_, s_ = cb.next(), sb_.next()
                k.dma("sp", c_[:], I["cos"].t[:, t0:t0 + TB], R=[I["cos"]], W=[c_])
                k.dma("sp", s_[:], I["sin"].t[:, t0:t0 + TB], R=[I["sin"]], W=[s_])
                pre = "qsT" in I
                items = [(I["qT"].t[g], I["qsT"].t[g] if pre else None, QT[:, g, t0:t0 + TB], QT) for g in range(4)]
                items += [(I["kT"].t[i], I["ksT"].t[i] if pre else None, d[:, t0:t0 + TB], d) for i, d in ((0, KcT), (1, KsT), (2, KwT))]
                for src, srcs, dst, dbuf in items:
                    x_, xs_ = xb.next(), xsb.next()
                    k.dma("sp", x_[:], src[:, t0:t0 + TB], R=[I["qT"]], W=[x_])
                    if srcs is not None:
                        k.dma("sp", xs_[:], srcs[:, t0:t0 + TB], R=[I["qT"]], W=[xs_])
                    else:
                        k.dma("sp", xs_[0:32, :], src[32:64, t0:t0 + TB], R=[I["qT"]], W=[xs_])
                        k.dma("sp", xs_[32:64, :], src[0:32, t0:t0 + TB], R=[I["qT"]], W=[xs_])
                    t1, t2 = t1r.next(), t2r.next()
                    k.tt("dve", t1[:], x_[:], c_[:], ALU.mult, R=[x_, c_], W=[t1])
                    k.tt("pool", t2[:], xs_[:], s_[:], ALU.mult, R=[xs_, s_], W=[t2])
                    k.tt("dve", dst, t1[:], t2[:], ALU.add, R=[t1, t2], W=[dbuf])
                x_ = xb.next()
                k.dma("sp", x_[:], I["vcT"].t[:, t0:t0 + TB], R=[I["vcT"]], W=[x_])
                k.copy("pool", VcT[:, t0:t0 + TB], x_[:], R=[x_], W=[VcT])
            for i, dst in ((0, Vsa), (1, Vwa)):
                vv = I["v_tok"].t[i].rearrange("(kb p) d -> p kb d", p=128)
                for q in range(4):
                    st = st8.next()
                    n4 = nkb // 4
                    k.dma("sp", st[:], vv[:, q * n4:(q + 1) * n4, :], R=[I["v_tok"]], W=[st])
                    k.copy("pool", dst[:, q * n4:(q + 1) * n4, 0:64], st[:], R=[st], W=[dst])
            glt = k.sb([128, nkb, 12], F32, es=e1)
            k.dma("sp", glt[:], I["gl"].t.rearrange("(kb p) c -> p kb c", p=128), R=[I["gl"]], W=[glt])
            k.act(gsig[:], glt[:], AF.Sigmoid, R=[glt], W=[gsig])
            k.dma("sp", selb[:], I["selb"].t.rearrange("(kb p) c -> p kb c", p=128), R=[I["selb"]], W=[selb])
            est = k.rot(2, [n_sel, 1024], F32, es=e1)
            for q in range(S // 1024):
                e_ = est.next()
                k.dma("sp", e_[:], I["ebig"].t[:, q * 1024:(q + 1) * 1024], R=[I["ebig"]], W=[e_])
                k.copy("pool", Ebig[:, q * 1024:(q + 1) * 1024], e_[:], R=[e_], W=[Ebig])
            ost = k.sb([128, ncb, n_sel], F32, es=e1)
            k.dma("sp", ost[:], I["ov"].t.rearrange("(c p) j -> p c j", p=128), R=[I["ov"]], W=[ost])
            k.copy("pool", Ov[:], ost[:], R=[ost], W=[Ov])
            w1b = k.sb([64, 32, 256], BF16, es=e1)
            w1s = k.rot(2, [64, 4, 256], F32, es=e1)
            w2s = k.sb([128, 2, 64], F32, es=e1)
            w2b = k.sb([128, 2, 64], BF16, es=e1)
            pss = k.sb([64, 34], F32, es=e1)
            posb = k.sb([64, 34], BF16, es=e1)
            biasb = k.sb([128, 2], F32, es=e1)
            hs = k.rot(2, [128, n_cmp], F32, es=e1)
            tq = k.rot(2, [128, n_cmp], F32, es=e1)
            gg = [k.sb([128, ncb * 128], BF16, es=e1) for _ in range(2)]
            for which, src in ((0, KcT), (1, VcT)):
                for q in range(8):
                    st = w1s.next()
                    k.dma("sp", st[:], I["w1"].t[which, :, 4 * q:4 * q + 4, :], R=[I["w1"]], W=[st])
                    k.copy("pool", w1b[:, 4 * q:4 * q + 4, :], st[:], R=[st], W=[w1b])
                k.dma("sp", w2s[:], I["w2"].t[which].rearrange("(c p) d -> p c d", p=128), R=[I["w2"]], W=[w2s])
                k.copy("pool", w2b[:], w2s[:], R=[w2s], W=[w2b])
                k.memset("dve", pss[:], 0.0, W=[pss])
                k.dma("sp", pss[:, 0:32], I["posT"].t[which], R=[I["posT"]], W=[pss])
                k.copy("dve", posb[:], pss[:], R=[pss], W=[posb])
                for hc in range(2):
                    ph, pb = k.banks[1 + hc], k.banks[3 + hc]
                    for j in range(32):
                        k.mm(ph[:, 0:n_cmp], w1b[:, j, hc * 128:(hc + 1) * 128], src[:, j:j + 16 * (n_cmp - 1) + 1:16],
                             j == 0, j == 31, R=[w1b, src], W=[ph])
                    for j in range(32):
                        k.mm(pb[:, 0:2], w1b[:, j, hc * 128:(hc + 1) * 128], posb[:, j:j + 2], j == 0, j == 31, R=[w1b, posb], W=[pb])
                    k.copy("dve", biasb[:, hc:hc + 1], pb[:, 0:1], R=[pb], W=[biasb])
                    h_, t_ = hs.next(), tq.next()
                    k.act(h_[:], ph[:, 0:n_cmp], AF.Identity, R=[ph, biasb], W=[h_], bias=biasb[:, hc:hc + 1], scale=1.0)
                    k.tt("dve", t_[:], h_[:], h_[:], ALU.mult, R=[h_], W=[t_])
                    k.ts("dve", t_[:], t_[:], 0.044715, 1.0, ALU.mult, ALU.add, R=[t_], W=[t_])
                    k.tt("dve", t_[:], t_[:], h_[:], ALU.mult, R=[t_, h_], W=[t_])
                    k.act(t_[:], t_[:], AF.Sigmoid, R=[t_], W=[t_], scale=1.5957691216)
                    k.memset("pool", gg[hc][:], 0.0, W=[gg[hc]])
                    k.tt("dve", gg[hc][:, 0:n_cmp], h_[:], t_[:], ALU.mult, R=[h_, t_], W=[gg[hc]])
                if which == 0:
                    po = k.banks[5]
                    for hc in range(2):
                        k.mm(po[0:64, 0:n_cmp], w2b[:, hc, :], gg[hc][:, 0:n_cmp], hc == 0, hc == 1, R=[w2b, gg[hc]], W=[po])
                    k.copy("act", KcmpT[:, 0:n_cmp], po[0:64, 0:n_cmp], R=[po], W=[KcmpT])
                else:
                    for nb in range(ncb):
                        nk = min(128, n_cmp - nb * 128)
                        po = k.banks[5 + nb % 2]
                        for hc in range(2):
                            k.mm(po[0:nk, 0:64], gg[hc][:, nb * 128:nb * 128 + nk], w2b[:, hc, :], hc == 0, hc == 1, R=[w2b, gg[hc]], W=[po])
                        k.copy("act", Vca[0:nk, nb, 0:64], po[0:nk, 0:64], R=[po], W=[Vca])
            k.barrier()
        with ExitStack() as e2:
            ysb = k.sb([128, nkb, 256], F32, es=e2)
            imp = k.sb([128, nkb, n_sel], F32, es=e2)
            negT = k.sb([n_sel, S], BF16, es=e2)
            ptr = k.rot(4, [128, 512], BF16, es=e2)
            dn, rden, fc = k.rot(2, [128, 4], F32, es=e2), k.rot(2, [128, 4], F32, es=e2), k.rot(2, [128, 4], F32, es=e2)
            imod, iw = k.rot(2, [128, n_sel], F32, es=e2), k.rot(2, [128, n_sel], F32, es=e2)
            m8a, m8b = k.rot(2, [128, 8], F32, es=e2), k.rot(2, [128, 8], F32, es=e2)
            s01 = k.rot(2, [128, n_sel], F32, es=e2)
            ngm = k.rot(2, [128, n_sel], BF16, es=e2)
            sbanks, abanks, ibanks = Rot(k.banks[0:3]), Rot(k.banks[3:5]), Rot(k.banks[5:7])

            def epilogue(acc, g, qc, br, first_branch):
                av = acc_view(acc)
                d_, rd, f_ = dn.next(), rden.next(), fc.next()
                k.ts("dve", d_[:], av[:, :, 64], 1e-30, None, ALU.max, None, R=[acc], W=[d_])
                k.op("dve", lambda hh: hh.reciprocal(out=rd[:], in_=d_[:]), R=[d_], W=[rd])
                k.tt("dve", f_[:], rd[:], gsig[:, 4 * qc:4 * qc + 4, g * 3 + br], ALU.mult, R=[rd, gsig], W=[f_])
                for sub in range(4):
                    o_ap = ysb[:, 4 * qc + sub, g * 64:(g + 1) * 64]
                    if first_branch:
                        k.act(o_ap, av[:, sub, 0:64], AF.Copy, R=[acc, f_], W=[ysb], scale=f_[:, sub:sub + 1])
                    else:
                        k.stt(o_ap, av[:, sub, 0:64], f_[:, sub:sub + 1], o_ap, ALU.mult, ALU.add, R=[acc, f_, ysb], W=[ysb])
                return rd

            for g in range(4):
                for qc in range(nqc):
                    q0 = qc * 512
                    acc, ia = abanks.next(), ibanks.next()
                    pairs = []
                    for nb in range(ncb):
                        if 16 * 128 * nb + 31 > q0 + 511:
                            continue
                        nk = min(128, n_cmp - nb * 128)
                        pairs.append(dict(kT=KcmpT[:, nb * 128:nb * 128 + nk], nk=nk, v=Vca[0:nk, nb, :], v2=Ov[0:nk, nb, :],
                                          bufs=[KcmpT, Vca, Ov], mask=(q0 - 2048 * nb - 31, -16, 1)))
                    attn_chunk(k, QT[:, g, q0:q0 + 512], [QT], pairs, acc, 65, SC, sbanks, ptr, q0, acc2=ia, W2=n_sel)
                    rd = epilogue(acc, g, qc, 0, True)
                    iv = acc_view(ia)
                    for sub in range(4):
                        i_ap = imp[:, 4 * qc + sub, :]
                        if g == 0:
                            k.ts("dve", i_ap, iv[:, sub, 0:n_sel], rd[:, sub:sub + 1], None, ALU.mult, None, R=[ia, rd], W=[imp])
                        else:
                            k.stt(i_ap, iv[:, sub, 0:n_sel], rd[:, sub:sub + 1], i_ap, ALU.mult, ALU.add, R=[ia, rd, imp], W=[imp])
            for qt in range(nkb):
                im, w_, a8, b8, s_, n_ = imod.next(), iw.next(), m8a.next(), m8b.next(), s01.next(), ngm.next()
                k.tt("dve", im[:], imp[:, qt, :], selb[:, qt, :], ALU.add, R=[imp, selb], W=[im])
                k.op("dve", lambda h: h.max(out=a8[:], in_=im[:]), R=[im], W=[a8])
                k.op("dve", lambda h: h.match_replace(out=w_[:], in_to_replace=a8[:], in_values=im[:], imm_value=-3.0e38),
                     R=[a8, im], W=[w_])
                k.op("dve", lambda h: h.max(out=b8[:], in_=w_[:]), R=[w_], W=[b8])
                k.ts("dve", s_[:], im[:], b8[:, 7:8], None, ALU.is_ge, None, R=[im, b8], W=[s_])
                k.ts("dve", n_[:], s_[:], -1.0, BIG, ALU.add, ALU.mult, R=[s_], W=[n_])
                pb = k.bankb
                k.op("pe", lambda h: h.transpose(out=pb[0:n_sel, 0:128], in_=n_[:], identity=ident[:]), R=[n_, ident], W=[pb])
                k.copy("act", negT[:, qt * 128:(qt + 1) * 128], pb[0:n_sel, 0:128], R=[pb], W=[negT])
            for g in range(4):
                for qc in range(nqc):
                    q0 = qc * 512
                    acc = abanks.next()
                    pairs = []
                    for kb in range(4 * qc + 4):
                        pairs.append(dict(kT=KsT[:, kb * 128:(kb + 1) * 128], nk=128, v=Vsa[:, kb, :], bufs=[KsT, Vsa],
                                          extra=(Ebig[:, kb * 128:(kb + 1) * 128], negT[:, q0:q0 + 512], [Ebig, negT]),
                                          mask=(q0 - kb * 128, -1, 1) if kb >= 4 * qc else None, kpos0=kb * 128))
                    attn_chunk(k, QT[:, g, q0:q0 + 512], [QT], pairs, acc, 65, SC, sbanks, ptr, q0)
                    epilogue(acc, g, qc, 1, False)
            for g in range(4):
                for qc in range(nqc):
                    q0 = qc * 512
                    acc = abanks.next()
                    pairs = []
                    for kb in range(max(0, 4 * qc - 4), 4 * qc + 4):
                        if kb < 4 * qc:
                            i = kb - (4 * qc - 4)
                            pairs.append(dict(kT=KwT[:, kb * 128:(kb + 1) * 128], nk=128, v=Vwa[:, kb, :], bufs=[KwT, Vwa],
                                              mask=(kb * 128 - q0 + 511, 1, -1), skip=set(range(i + 1, 4))))
                        else:
                            pairs.append(dict(kT=KwT[:, kb * 128:(kb + 1) * 128], nk=128, v=Vwa[:, kb, :], bufs=[KwT, Vwa],
                                              mask=(q0 - kb * 128, -1, 1), kpos0=kb * 128))
                    attn_chunk(k, QT[:, g, q0:q0 + 512], [QT], pairs, acc, 65, SC, sbanks, ptr, q0)
                    epilogue(acc, g, qc, 2, False)
            yv = y.t.rearrange("(kb p) f -> p kb f", p=128)
            for q in range(4):
                n4 = nkb // 4
                k.dma("sp", yv[:, q * n4:(q + 1) * n4, :], ysb[:, q * n4:(q + 1) * n4, :], R=[ysb], W=[y])
            k.barrier()


def nsa_consts(S):
    n_cmp, n_sel = S // 16 - 1, S // 64
    ncb = (n_cmp + 127) // 128
    cmp_idx = np.arange(n_cmp)[:, None] * 16 + np.arange(32)[None, :]
    blk = np.arange(n_sel)
    ov = np.zeros((ncb * 128, n_sel), np.float32)
    ov[:n_cmp] = ((cmp_idx[:, :1] < (blk[None, :] + 1) * 64) & (cmp_idx[:, -1:] >= blk[None, :] * 64)).astype(np.float32)
    cur = (np.arange(S) // 64)[:, None]
    valid = blk[None, :] <= cur
    forced = valid & ((blk[None, :] == 0) | (blk[None, :] >= cur - 1))
    selb = np.where(forced, 1e30, np.where(valid, 0.0, -1e30)).astype(np.float32)
    ebig = (np.arange(S)[None, :] // 64 == blk[:, None]).astype(np.float32)
    return ov, selb, np.ascontiguousarray(ebig)


def nsa_inputs(pn, cmp_pos, cmp_w1, cmp_w2, hh, S):
    A = np.ascontiguousarray
    sw = lambda t: np.concatenate([t[..., 32:, :], t[..., :32, :]], axis=-2)
    q = pn[:, hh * 256:(hh + 1) * 256].reshape(S, 4, 64).transpose(1, 2, 0)
    kk = np.stack([pn[:, c + hh * 64:c + hh * 64 + 64].T for c in (512, 768, 1024)])
    vc = pn[:, 640 + hh * 64:640 + hh * 64 + 64].T
    v_tok = np.stack([pn[:, c + hh * 64:c + hh * 64 + 64] for c in (896, 1152)])
    gl = pn[:, 1280 + hh * 12:1280 + hh * 12 + 12]
    cos, sin = rope_tables(S, 64)
    ov, selb, ebig = nsa_consts(S)
    w1 = cmp_w1.reshape(2, 32, 64, 256).transpose(0, 2, 1, 3)
    return dict(qT=A(q), qsT=A(sw(q)), kT=A(kk), ksT=A(sw(kk)), vcT=A(vc), v_tok=A(v_tok), gl=A(gl), cos=cos, sin=sin,
                w1=A(w1), w2=A(cmp_w2), posT=A(cmp_pos.transpose(0, 2, 1)), ov=ov, selb=selb, ebig=ebig)


def NSA_SHAPES(S):
    n_cmp, n_sel = S // 16 - 1, S // 64
    ncb = (n_cmp + 127) // 128
    return dict(qT=[4, 64, S], qsT=[4, 64, S], kT=[3, 64, S], ksT=[3, 64, S], vcT=[64, S], v_tok=[2, S, 64], gl=[S, 12],
                cos=[64, S], sin=[64, S], w1=[2, 64, 32, 256], w2=[2, 256, 64], posT=[2, 64, 32], ov=[ncb * 128, n_sel],
                selb=[S, n_sel], ebig=[n_sel, S])


CH = 64


def tri_mask(k, es, strict, n=8):
    m = k.sb([64, n, 64], BF16, es=es)
    k.memset("pool", m[:], 1.0, W=[m])
    k.op("pool", lambda h: h.affine_select(out=m[:], in_=m[:], pattern=[[0, n], [1, 64]], compare_op=ALU.is_ge, fill=0.0,
                                          base=-1 if strict else 0, channel_multiplier=-1), R=[m], W=[m])
    return m


def chunk_start_mask(k, es, n):
    m = k.sb([64, n], F32, es=es)
    k.memset("pool", m[:], 1.0, W=[m])
    k.memset("pool", m[:, :].rearrange("p (c t) -> p c t", t=CH)[:, :, 0:1], 0.0, W=[m])
    return m


def emit_gla(k, I, y, S):
    NCH = S // CH
    with ExitStack() as es:
        ident = k.sb([128, 128], BF16, es=es)
        k.memset("pool", ident[:], 1.0, W=[ident])
        k.op("pool", lambda h: h.affine_select(out=ident[:], in_=ident[:], pattern=[[-1, 128]], compare_op=ALU.is_equal,
                                              fill=0.0, base=0, channel_multiplier=1), R=[ident], W=[ident])
        tri = tri_mask(k, es, False)
        cmask = chunk_start_mask(k, es, TB)
        Vb = k.sb([64, NCH, 256], BF16, es=es)
        aw2 = k.sb([16, 128], F32, es=es)
        ab = k.sb([64, 2], F32, es=es)
        nab = k.sb([64, 2], F32, es=es)
        ng = k.sb([64, 256], F32, es=es)
        epsb = k.sb([64, 1], F32, es=es)
        k.memset("dve", epsb[:], EPS, W=[epsb])
        k.dma("sp", aw2[:], I["aw2"].t[:, :], R=[I["aw2"]], W=[aw2])
        k.dma("sp", ab[:], I["ab"].t[:, :], R=[I["ab"]], W=[ab])
        k.dma("sp", ng[:], I["ng"].t[:, :], R=[I["ng"]], W=[ng])
        k.ts("dve", nab[:], ab[:], -1.0, None, ALU.mult, None, R=[ab], W=[nab])
        vst = k.rot(2, [64, 8, 256], F32, es=es)
        for c8 in range(NCH // 8):
            st = vst.next()
            k.dma("sp", st[:], I["v_ch"].t[:, 8 * c8:8 * c8 + 8, :], R=[I["v_ch"]], W=[st])
            k.copy("pool", Vb[:, 8 * c8:8 * c8 + 8, :], st[:], R=[st], W=[Vb])
        for h in range(2):
            with ExitStack() as e1:
                Qb = k.sb([64, S], BF16, es=e1)
                Kt = k.sb([64, S], BF16, es=e1)
                Kh = k.sb([64, NCH, 64], BF16, es=e1)
                MT = k.sb([64, NCH, 64], BF16, es=e1)
                gC = k.sb([64, NCH], F32, es=e1)
                otok = k.sb([64, NCH, 128], F32, es=e1)
                H = k.sb([64, 128], F32, es=e1)
                Hb = k.sb([64, 128], BF16, es=e1)
                alo = k.rot(2, [16, TB], F32, es=e1)
                qb, kb_ = k.rot(2, [64, TB], F32, es=e1), k.rot(2, [64, TB], F32, es=e1)
                t_e, t_l, t_b = k.rot(2, [64, TB], F32, es=e1), k.rot(2, [64, TB], F32, es=e1), k.rot(2, [64, TB], F32, es=e1)
                t_eb, t_enb, t_k = k.rot(2, [64, TB], F32, es=e1), k.rot(2, [64, TB], F32, es=e1), k.rot(2, [64, TB], F32, es=e1)
                t_kh = k.rot(2, [64, TB], BF16, es=e1)
                pz = Rot(k.banks[0:2])
                for blk in range(S // TB):
                    t0 = blk * TB
                    a_, q_, k_ = alo.next(), qb.next(), kb_.next()
                    k.dma("sp", a_[:], I["aloT"].t[:, t0:t0 + TB], R=[I["aloT"]], W=[a_])
                    k.dma("sp", q_[:], I["qT"].t[h, :, t0:t0 + TB], R=[I["qT"]], W=[q_])
                    k.dma("sp", k_[:], I["kT"].t[h, :, t0:t0 + TB], R=[I["kT"]], W=[k_])
                    p_ = pz.next()
                    k.mm(p_[0:64, :], aw2[:, h * 64:(h + 1) * 64], a_[:], True, True, R=[aw2, a_], W=[p_])
                    e_, l_, b_ = t_e.next(), t_l.next(), t_b.next()
                    k.act(e_[:], p_[0:64, :], AF.Exp, R=[p_, nab], W=[e_], bias=nab[:, h:h + 1], scale=-1.0)
                    k.act(l_[:], e_[:], AF.Ln, R=[e_], W=[l_], bias=1.0, scale=1.0)
                    k.op("dve", lambda hh: hh.tensor_tensor_scan(out=b_[:], data0=cmask[:], data1=l_[:], initial=0.0,
                                                                 op0=ALU.mult, op1=ALU.add), R=[cmask, l_], W=[b_])
                    eb, enb, kt32 = t_eb.next(), t_enb.next(), t_k.next()
                    k.act(eb[:], b_[:], AF.Exp, R=[b_], W=[eb], scale=-1.0 / 16)
                    k.act(enb[:], b_[:], AF.Exp, R=[b_], W=[enb], scale=1.0 / 16)
                    k.stt(Qb[:, t0:t0 + TB], q_[:], 0.125, eb[:], ALU.mult, ALU.mult, R=[q_, eb], W=[Qb])
                    k.tt("dve", kt32[:], k_[:], enb[:], ALU.mult, R=[k_, enb], W=[kt32])
                    k.copy("pool", Kt[:, t0:t0 + TB], kt32[:], R=[kt32], W=[Kt])
                    ebv = eb[:, :].rearrange("p (c t) -> p c t", t=CH)
                    k.copy("dve", gC[:, 8 * blk:8 * blk + 8], ebv[:, :, CH - 1], R=[eb], W=[gC])
                    kh = t_kh.next()
                    k.tt("dve", kh[:, :].rearrange("p (c t) -> p c t", t=CH), kt32[:, :].rearrange("p (c t) -> p c t", t=CH),
                         ebv[:, :, CH - 1:CH].to_broadcast([64, 8, CH]), ALU.mult, R=[kt32, eb], W=[kh])
                    pb = k.bankb
                    for c in range(8):
                        k.op("pe", lambda hh: hh.transpose(out=pb[0:64, c * 64:(c + 1) * 64], in_=kh[:, c * 64:(c + 1) * 64],
                                                           identity=ident[0:64, 0:64]), R=[kh, ident], W=[pb])
                    k.copy("act", Kh[:, 8 * blk:8 * blk + 8, :], pb[0:64, 0:512].rearrange("p (c t) -> p c t", t=64), R=[pb], W=[Kh])
                pm = Rot(k.banks[2:4])
                for c8 in range(NCH // 8):
                    p_ = pm.next()
                    for c in range(8):
                        cc = 8 * c8 + c
                        k.mm(p_[0:64, c * 64:(c + 1) * 64], Kt[:, cc * 64:(cc + 1) * 64], Qb[:, cc * 64:(cc + 1) * 64], True, True,
                             R=[Kt, Qb], W=[p_])
                    k.tt("dve", MT[:, 8 * c8:8 * c8 + 8, :], p_[0:64, :].rearrange("p (c t) -> p c t", t=64), tri[:], ALU.mult,
                         R=[p_, tri], W=[MT])
                po_r, ph_r = Rot(k.banks[4:6]), Rot([k.banks[6], k.banks[0]])
                k.memset("dve", H[:], 0.0, W=[H])
                for c in range(NCH):
                    po, ph = po_r.next(), ph_r.next()
                    vch = Vb[:, c, h * 128:(h + 1) * 128]
                    k.mm(po[0:64, 0:128], MT[:, c, :], vch, True, c == 0, R=[MT, Vb], W=[po])
                    if c > 0:
                        k.mm(po[0:64, 0:128], Qb[:, c * 64:(c + 1) * 64], Hb[:], False, True, R=[Qb, Hb], W=[po])
                    k.copy("act", otok[:, c, :], po[0:64, 0:128], R=[po], W=[otok])
                    if c < NCH - 1:
                        k.mm(ph[0:64, 0:128], Kh[:, c, :], vch, True, True, R=[Kh, Vb], W=[ph])
                        k.stt(H[:], H[:], gC[:, c:c + 1], ph[0:64, 0:128], ALU.mult, ALU.add, R=[H, gC, ph], W=[H])
                        k.copy("dve", Hb[:], H[:], R=[H], W=[Hb])
                sqp = k.rot(2, [64, 16, 128], F32, es=e1)
                rst = k.rot(2, [64, 16, 128], F32, es=e1)
                ms, rs_ = k.rot(2, [64, 16], F32, es=e1), k.rot(2, [64, 16], F32, es=e1)
                yv = y.t.rearrange("(c p) f -> p c f", p=CH)
                for c16 in range(NCH // 16):
                    cs = slice(16 * c16, 16 * c16 + 16)
                    sq, rr, m_, r_ = sqp.next(), rst.next(), ms.next(), rs_.next()
                    k.dma("sp", rr[:], I["r_ch"].t[:, cs, h * 128:(h + 1) * 128], R=[I["r_ch"]], W=[rr])
                    k.tt("dve", sq[:], otok[:, cs, :], otok[:, cs, :], ALU.mult, R=[otok], W=[sq])
                    k.op("dve", lambda hh: hh.reduce_sum(out=m_[:], in_=sq[:], axis=AX.X), R=[sq], W=[m_])
                    k.act(m_[:], m_[:], AF.Sqrt, R=[m_, epsb], W=[m_], bias=epsb[:], scale=1.0 / 128)
                    k.op("dve", lambda hh: hh.reciprocal(out=r_[:], in_=m_[:]), R=[m_], W=[r_])
                    k.tt("dve", sq[:], otok[:, cs, :], r_[:, :].unsqueeze(2).to_broadcast([64, 16, 128]), ALU.mult, R=[otok, r_], W=[sq])
                    k.tt("pool", sq[:], sq[:], ng[:, h * 128:(h + 1) * 128].unsqueeze(1).to_broadcast([64, 16, 128]), ALU.mult,
                         R=[sq, ng], W=[sq])
                    k.act(rr[:], rr[:], AF.Silu, R=[rr], W=[rr])
                    k.tt("dve", sq[:], sq[:], rr[:], ALU.mult, R=[sq, rr], W=[sq])
                    k.dma("pool", yv[:, cs, h * 128:(h + 1) * 128], sq[:], R=[sq], W=[y])
                k.barrier()


def gla_inputs(pg, alpha_w2, alpha_b, norm_g, hh, S):
    A = np.ascontiguousarray
    q = pg[:, hh * 128:(hh + 1) * 128].reshape(S, 2, 64).transpose(1, 2, 0)
    kk = pg[:, 256 + hh * 128:256 + (hh + 1) * 128].reshape(S, 2, 64).transpose(1, 2, 0)
    v = pg[:, 512 + hh * 256:512 + (hh + 1) * 256].reshape(S // CH, CH, 256).transpose(1, 0, 2)
    r = pg[:, 1040 + hh * 256:1040 + (hh + 1) * 256].reshape(S // CH, CH, 256).transpose(1, 0, 2)
    alo = pg[:, 1024:1040].T
    return dict(qT=A(q), kT=A(kk), v_ch=A(v), r_ch=A(r), aloT=A(alo), aw2=A(alpha_w2[:, hh * 128:(hh + 1) * 128]),
                ab=A(alpha_b[hh * 128:(hh + 1) * 128].reshape(2, 64).T), ng=A(np.broadcast_to(norm_g[hh * 256:(hh + 1) * 256], (64, 256))))


GLA_SHAPES = lambda S: dict(qT=[2, 64, S], kT=[2, 64, S], v_ch=[64, S // CH, 256], r_ch=[64, S // CH, 256], aloT=[16, S],
                            aw2=[16, 128], ab=[64, 2], ng=[64, 256])


class _Stop(Exception):
    pass


def emit_rwkv(k, I, y, S, dbg=99):
    try:
        _emit_rwkv(k, I, y, S, dbg)
    except _Stop:
        k.barrier()


def _emit_rwkv(k, I, y, S, dbg):
    def ck(n):
        if dbg <= n:
            raise _Stop()
    NCH = S // CH
    C0 = 0.6065306597126334
    with ExitStack() as es:
        identb = k.sb([128, 128], BF16, es=es)
        identf = k.sb([64, 64], F32, es=es)
        for idt in (identb, identf):
            n = idt.t.shape[0]
            k.memset("pool", idt[:], 1.0, W=[idt])
            k.op("pool", lambda h: h.affine_select(out=idt[:], in_=idt[:], pattern=[[-1, n]], compare_op=ALU.is_equal,
                                                  fill=0.0, base=0, channel_multiplier=1), R=[idt], W=[idt])
        tri_i = tri_mask(k, es, False)
        tri_s = tri_mask(k, es, True)
        tri_l = k.sb([64, 8, 64], BF16, es=es)
        k.memset("pool", tri_l[:], 1.0, W=[tri_l])
        k.op("pool", lambda h: h.affine_select(out=tri_l[:], in_=tri_l[:], pattern=[[0, 8], [-1, 64]], compare_op=ALU.is_ge,
                                              fill=0.0, base=-1, channel_multiplier=1), R=[tri_l], W=[tri_l])
        cmask = chunk_start_mask(k, es, TB)
        ones64 = k.sb([64, 64], BF16, es=es)
        k.memset("dve", ones64[:], 1.0, W=[ones64])
        gnb = k.sb([64, 1], F32, es=es)
        k.memset("dve", gnb[:], 64e-5, W=[gnb])
        P = {}
        for nm, shp in (("mu_r", [64, 4]), ("mu_k", [64, 4]), ("mu_v", [64, 4]), ("mu_w", [64, 1]), ("mu_a", [64, 1]), ("mu_g", [128, 1]),
                        ("w0", [64, 4]), ("a0", [64, 4]), ("k_k", [64, 4]), ("k_a", [64, 4]), ("r_k", [64, 4]),
                        ("ww2", [64, 256]), ("aw2", [64, 256]), ("gw2", [128, 256]), ("ln_w", [64, 256]), ("ln_b", [64, 256])):
            P[nm] = k.sb(shp, F32, es=es)
            k.dma("sp", P[nm][:], I[nm].t[:, :], R=[I[nm]], W=[P[nm]])
        omka = k.sb([64, 4], F32, es=es)
        k.ts("dve", omka[:], P["k_a"][:], -1.0, 1.0, ALU.mult, ALU.add, R=[P["k_a"]], W=[omka])
        gw2b = k.sb([128, 256], BF16, es=es)
        k.copy("dve", gw2b[:], P["gw2"][:], R=[P["gw2"]], W=[gw2b])
        Hs = [k.sb([64, 64], F32, es=es) for _ in range(4)]
        Hbs = [k.sb([64, 64], BF16, es=es) for _ in range(4)]
        for h in range(4):
            k.memset("dve", Hs[h][:], 0.0, W=[Hs[h]])

        def f32r(n, p=64, w=TB):
            return k.rot(n, [p, w], F32, es=es)

        ldw, lda, ldg = k.rot(2, [64, TB + 1], F32, es=es), k.rot(2, [64, TB + 1], F32, es=es), k.rot(2, [128, TB + 1], F32, es=es)
        ldr, ldk, ldv = k.rot(2, [64, TB + 1], F32, es=es), k.rot(2, [64, TB + 1], F32, es=es), k.rot(2, [64, TB + 1], F32, es=es)
        dtmp = k.rot(2, [128, TB], F32, es=es)
        tw_r, as_r = f32r(2), f32r(2)
        sgl_r = k.rot(2, [128, TB], BF16, es=es)
        rs_r, ks_r, vs_r = f32r(2), f32r(2), f32r(2)
        sgw_r, a_r, b_r, eb_r, enb_r, eb1_r = f32r(1), f32r(2), f32r(1), f32r(2), f32r(1), f32r(1)
        kkr_r, nrm_r, kk_r, kp_r, be_r, kt32_r, bt32_r = f32r(1), f32r(1), f32r(2), f32r(2), f32r(1), f32r(1), f32r(1)
        sqk_r = k.rot(2, [64, TB], BF16, es=es)

        def b16r(n):
            return k.rot(n, [64, TB], BF16, es=es)

        Rb_r, Ab_r, Kt_r, Bt_r, khf_r, bhf_r, rkx_r = b16r(4), b16r(4), b16r(2), b16r(2), b16r(2), b16r(2), b16r(2)

        def m16r(n):
            return k.rot(n, [64, 8, 64], BF16, es=es)

        Kh_r, Bh_r, Vc_r = m16r(4), m16r(4), m16r(4)
        vtok_r = k.rot(4, [64, 8, 64], F32, es=es)
        X_r, XT_r, P_r, PT_r = m16r(2), m16r(2), m16r(3), m16r(3)
        Mak_r, Mrk_r, Mrb_r, Ri_r, R16_r = m16r(4), m16r(4), m16r(4), m16r(4), m16r(2)
        Rf_r = k.rot(2, [64, 8, 64], F32, es=es)
        gC_r = k.rot(4, [64, 8], F32, es=es)
        rk_r = k.rot(4, [64, 8], F32, es=es)
        Zb_r, Un_r = k.rot(8, [64, 64], BF16, es=es), k.rot(8, [64, 64], BF16, es=es)
        ytok_r = k.rot(4, [64, 8, 64], F32, es=es)
        gt_r, po1_r, po2_r = (k.rot(2, [64, 8, 64], F32, es=es) for _ in range(3))
        st1_r, st2_r = k.rot(2, [64, 8], F32, es=es), k.rot(2, [64, 8], F32, es=es)
        bk = Rot(k.banks[0:3])
        yv = y.t.rearrange("(c p) f -> p c f", p=CH)

        def v3(ap):
            return ap.rearrange("p (c t) -> p c t", t=CH)

        def shifted(ld, src_ap, srcbuf, mu_ap, mubuf, out, np_, t0):
            t = ld.next()
            if t0 == 0:
                k.memset("pool", t[0:np_, 0:1], 0.0, W=[t])
                k.dma("sp", t[0:np_, 1:TB + 1], src_ap[:, 0:TB], R=[srcbuf], W=[t])
            else:
                k.dma("sp", t[0:np_, :], src_ap[:, t0 - 1:t0 + TB], R=[srcbuf], W=[t])
            d = dtmp.next()
            k.tt("pool", d[0:np_, :], t[0:np_, 0:TB], t[0:np_, 1:TB + 1], ALU.subtract, R=[t], W=[d])
            k.stt(out[0:np_, :], d[0:np_, :], mu_ap, t[0:np_, 1:TB + 1], ALU.mult, ALU.add, R=[d, mubuf, t], W=[out])

        for blk in range(S // TB):
            t0 = blk * TB
            tw, als, sgl = tw_r.next(), as_r.next(), sgl_r.next()
            shifted(ldw, I["wloT"].t, I["wloT"], P["mu_w"][:, 0:1], P["mu_w"], tw, 64, t0)
            k.act(tw[:], tw[:], AF.Tanh, R=[tw], W=[tw])
            shifted(lda, I["aloT"].t, I["aloT"], P["mu_a"][:, 0:1], P["mu_a"], als, 64, t0)
            gl = dtmp.next()
            shifted(ldg, I["gloT"].t, I["gloT"], P["mu_g"][:, 0:1], P["mu_g"], gl, 128, t0)
            k.act(sgl[:], gl[:], AF.Sigmoid, R=[gl], W=[sgl])
            ck(1)
            HS = []
            for h in range(4):
                hc = slice(h, h + 1)
                rs, ks, vs = rs_r.next(), ks_r.next(), vs_r.next()
                shifted(ldr, I["rT"].t[h], I["rT"], P["mu_r"][:, hc], P["mu_r"], rs, 64, t0)
                shifted(ldk, I["kT"].t[h], I["kT"], P["mu_k"][:, hc], P["mu_k"], ks, 64, t0)
                shifted(ldv, I["vT"].t[h], I["vT"], P["mu_v"][:, hc], P["mu_v"], vs, 64, t0)
                pw, pa = bk.next(), bk.next()
                k.mm(pw[0:64, :], P["ww2"][:, h * 64:(h + 1) * 64], tw[:], True, True, R=[P["ww2"], tw], W=[pw])
                k.mm(pa[0:64, :], P["aw2"][:, h * 64:(h + 1) * 64], als[:], True, True, R=[P["aw2"], als], W=[pa])
                sgw, a_, b_, eb, enb, eb1 = sgw_r.next(), a_r.next(), b_r.next(), eb_r.next(), enb_r.next(), eb1_r.next()
                k.act(sgw[:], pw[0:64, :], AF.Sigmoid, R=[pw, P["w0"]], W=[sgw], bias=P["w0"][:, hc], scale=1.0)
                k.act(a_[:], pa[0:64, :], AF.Sigmoid, R=[pa, P["a0"]], W=[a_], bias=P["a0"][:, hc], scale=1.0)
                k.op("dve", lambda hh: hh.tensor_tensor_scan(out=b_[:], data0=cmask[:], data1=sgw[:], initial=0.0,
                                                             op0=ALU.mult, op1=ALU.add), R=[cmask, sgw], W=[b_])
                k.act(eb[:], b_[:], AF.Exp, R=[b_], W=[eb], scale=-C0)
                k.act(enb[:], b_[:], AF.Exp, R=[b_], W=[enb], scale=C0)
                k.tt("pool", eb1[:], b_[:], sgw[:], ALU.subtract, R=[b_, sgw], W=[eb1])
                k.act(eb1[:], eb1[:], AF.Exp, R=[eb1], W=[eb1], scale=-C0)
                kkr, sqk, nrm, kk, kp, be = kkr_r.next(), sqk_r.next(), nrm_r.next(), kk_r.next(), kp_r.next(), be_r.next()
                k.ts("dve", kkr[:], ks[:], P["k_k"][:, hc], None, ALU.mult, None, R=[ks, P["k_k"]], W=[kkr])
                k.act(sqk[:], kkr[:], AF.Square, R=[kkr], W=[sqk])
                pn = bk.next()
                k.mm(pn[0:64, :], ones64[:], sqk[:], True, True, R=[ones64, sqk], W=[pn])
                k.act(nrm[:], pn[0:64, :], AF.Sqrt, R=[pn], W=[nrm])
                k.ts("dve", nrm[:], nrm[:], 1e-12, None, ALU.max, None, R=[nrm], W=[nrm])
                k.op("dve", lambda hh: hh.reciprocal(out=nrm[:], in_=nrm[:]), R=[nrm], W=[nrm])
                k.tt("pool", kk[:], kkr[:], nrm[:], ALU.mult, R=[kkr, nrm], W=[kk])
                k.ts("dve", kp[:], a_[:], P["k_a"][:, hc], omka[:, hc], ALU.mult, ALU.add, R=[a_, P["k_a"], omka], W=[kp])
                k.tt("pool", kp[:], kp[:], ks[:], ALU.mult, R=[kp, ks], W=[kp])
                k.tt("pool", be[:], kk[:], a_[:], ALU.mult, R=[kk, a_], W=[be])
                Rb, Ab, Kt, Bt, kt32, bt32 = Rb_r.next(), Ab_r.next(), Kt_r.next(), Bt_r.next(), kt32_r.next(), bt32_r.next()
                k.tt("dve", Rb[:], rs[:], eb[:], ALU.mult, R=[rs, eb], W=[Rb])
                k.tt("pool", Ab[:], kk[:], eb1[:], ALU.mult, R=[kk, eb1], W=[Ab])
                k.tt("dve", kt32[:], kp[:], enb[:], ALU.mult, R=[kp, enb], W=[kt32])
                k.tt("pool", bt32[:], be[:], enb[:], ALU.mult, R=[be, enb], W=[bt32])
                k.copy("act", Kt[:], kt32[:], R=[kt32], W=[Kt])
                k.copy("act", Bt[:], bt32[:], R=[bt32], W=[Bt])
                gC = gC_r.next()
                ebv = v3(eb[:, :])
                k.copy("dve", gC[:], ebv[:, :, CH - 1], R=[eb], W=[gC])
                gbc = ebv[:, :, CH - 1:CH].to_broadcast([64, 8, CH])
                khf, bhf, rkx = khf_r.next(), bhf_r.next(), rkx_r.next()
                k.tt("dve", v3(khf[:, :]), v3(kt32[:, :]), gbc, ALU.mult, R=[kt32, eb], W=[khf])
                k.tt("pool", v3(bhf[:, :]), v3(bt32[:, :]), gbc, ALU.mult, R=[bt32, eb], W=[bhf])
                k.stt(rkx[:], rs[:], P["r_k"][:, hc], kp[:], ALU.mult, ALU.mult, R=[rs, P["r_k"], kp], W=[rkx])
                ck(2)
                Kh, Bh, Vc, vtok = Kh_r.next(), Bh_r.next(), Vc_r.next(), vtok_r.next()
                pbb = k.bankb
                for src_, dst_, eng_ in ((khf, Kh, "act"), (bhf, Bh, "dve")):
                    for c in range(8):
                        k.op("pe", lambda hh: hh.transpose(out=pbb[0:64, c * 64:(c + 1) * 64], in_=src_[:, c * 64:(c + 1) * 64],
                                                           identity=identb[0:64, 0:64]), R=[src_, identb], W=[pbb])
                    k.copy(eng_, dst_[:], v3(pbb[0:64, 0:512]), R=[pbb], W=[dst_])
                ck(3)
                pv = bk.next()
                for c in range(8):
                    k.mm(pv[0:64, c * 64:(c + 1) * 64], vs[:, c * 64:(c + 1) * 64], identf[:, :], True, True, R=[vs, identf], W=[pv])
                k.copy("act", vtok[:], v3(pv[0:64, :]), R=[pv], W=[vtok])
                k.copy("dve", Vc[:], v3(pv[0:64, :]), R=[pv], W=[Vc])
                ck(4)
                prk = bk.next()
                for c in range(8):
                    k.mm(prk[0:64, 2 * c:2 * c + 2], rkx[:, c * 64:(c + 1) * 64], ones64[:, 0:2], True, True, R=[rkx, ones64], W=[prk])
                rk = rk_r.next()
                k.copy("dve", rk[:], prk[0:64, 0:16].rearrange("p (c two) -> p c two", two=2)[:, :, 0], R=[prk], W=[rk])
                ck(5)
                X, XT, Mak, Mrk, Mrb = X_r.next(), XT_r.next(), Mak_r.next(), Mrk_r.next(), Mrb_r.next()
                for (lh, rh, dst, msk) in ((Bt, Ab, X, tri_s), (Ab, Bt, XT, tri_l), (Kt, Ab, Mak, tri_s), (Kt, Rb, Mrk, tri_i), (Bt, Rb, Mrb, tri_i)):
                    pm = bk.next()
                    for c in range(8):
                        k.mm(pm[0:64, c * 64:(c + 1) * 64], lh[:, c * 64:(c + 1) * 64], rh[:, c * 64:(c + 1) * 64], True, True,
                             R=[lh, rh], W=[pm])
                    k.tt("dve", dst[:], v3(pm[0:64, :]), msk[:], ALU.mult, R=[pm, msk], W=[dst])
                ck(6)
                Rf, R16 = Rf_r.next(), R16_r.next()
                idb = identb[0:64, 0:64].unsqueeze(1).to_broadcast([64, 8, 64])
                k.tt("dve", Rf[:], idb, X[:], ALU.subtract, R=[identb, X], W=[Rf])
                k.copy("pool", R16[:], Rf[:], R=[Rf], W=[R16])
                Pc, PTc = X, XT
                for it in range(5):
                    Pn, PTn = P_r.next(), PT_r.next()
                    p1, p2 = bk.next(), bk.next()
                    for c in range(8):
                        k.mm(p1[0:64, c * 64:(c + 1) * 64], PTc[:, c, :], Pc[:, c, :], True, True, R=[PTc, Pc], W=[p1])
                    k.copy("act", Pn[:], v3(p1[0:64, :]), R=[p1], W=[Pn])
                    for c in range(8):
                        k.mm(p2[0:64, c * 64:(c + 1) * 64], Pc[:, c, :], PTc[:, c, :], True, True, R=[PTc, Pc], W=[p2])
                    k.copy("act", PTn[:], v3(p2[0:64, :]), R=[p2], W=[PTn])
                    p3 = bk.next()
                    for c in range(8):
                        k.mm(p3[0:64, c * 64:(c + 1) * 64], PTn[:, c, :], R16[:, c, :], True, True, R=[PTn, R16], W=[p3])
                    k.tt("dve", Rf[:], Rf[:], v3(p3[0:64, :]), ALU.add, R=[Rf, p3], W=[Rf])
                    R16 = R16_r.next() if it < 4 else Ri_r.next()
                    k.copy("pool", R16[:], Rf[:], R=[Rf], W=[R16])
                    Pc, PTc = Pn, PTn
                Ri = R16
                ck(7)
                ytok = ytok_r.next()
                HS.append(dict(h=h, Mak=Mak, Mrk=Mrk, Mrb=Mrb, Ri=Ri, Ab=Ab, Rb=Rb, Kh=Kh, Bh=Bh, Vc=Vc, gC=gC, vtok=vtok, rk=rk, ytok=ytok))
            for c in range(8):
                first = (blk == 0 and c == 0)
                cs = slice(c * 64, (c + 1) * 64)
                Zs, Us = [Zb_r.next() for _ in HS], [Un_r.next() for _ in HS]
                for i, T_ in enumerate(HS):
                    pb_ = k.banks[3 + i]
                    k.mm(pb_[0:64, 0:64], T_["Mak"][:, c, :], T_["Vc"][:, c, :], True, first, R=[T_["Mak"], T_["Vc"]], W=[pb_])
                    if not first:
                        k.mm(pb_[0:64, 0:64], T_["Ab"][:, cs], Hbs[T_["h"]][:], False, True, R=[T_["Ab"], Hbs[T_["h"]]], W=[pb_])
                for i, T_ in enumerate(HS):
                    pb_ = k.banks[3 + i]
                    k.copy("act", Zs[i][:], pb_[0:64, 0:64], R=[pb_], W=[Zs[i]])
                for i, T_ in enumerate(HS):
                    pb_ = k.banks[3 + i]
                    k.mm(pb_[0:64, 64:128], T_["Ri"][:, c, :], Zs[i][:], True, True, R=[T_["Ri"], Zs[i]], W=[pb_])
                for i, T_ in enumerate(HS):
                    pb_ = k.banks[3 + i]
                    k.ts("dve", Us[i][:], pb_[0:64, 64:128], -1.0, None, ALU.mult, None, R=[pb_], W=[Us[i]])
                for i, T_ in enumerate(HS):
                    pb_ = k.banks[3 + i]
                    Hb = Hbs[T_["h"]]
                    k.mm(pb_[0:64, 128:192], T_["Mrk"][:, c, :], T_["Vc"][:, c, :], True, False, R=[T_["Mrk"], T_["Vc"]], W=[pb_])
                    if not first:
                        k.mm(pb_[0:64, 128:192], T_["Rb"][:, cs], Hb[:], False, False, R=[T_["Rb"], Hb], W=[pb_])
                    k.mm(pb_[0:64, 128:192], T_["Mrb"][:, c, :], Us[i][:], False, True, R=[T_["Mrb"], Us[i]], W=[pb_])
                    k.mm(pb_[0:64, 192:256], T_["Kh"][:, c, :], T_["Vc"][:, c, :], True, False, R=[T_["Kh"], T_["Vc"]], W=[pb_])
                    k.mm(pb_[0:64, 192:256], T_["Bh"][:, c, :], Us[i][:], False, True, R=[T_["Bh"], Us[i]], W=[pb_])
                for i, T_ in enumerate(HS):
                    pb_ = k.banks[3 + i]
                    H, Hb = Hs[T_["h"]], Hbs[T_["h"]]
                    k.stt(H[:], H[:], T_["gC"][:, c:c + 1], pb_[0:64, 192:256], ALU.mult, ALU.add, R=[H, T_["gC"], pb_], W=[H])
                    k.copy("dve", Hb[:], H[:], R=[H], W=[Hb])
                    k.copy("act", T_["ytok"][:, c, :], pb_[0:64, 128:192], R=[pb_], W=[T_["ytok"]])
            for T_ in HS:
                h, vtok, rk, ytok = T_["h"], T_["vtok"], T_["rk"], T_["ytok"]
                pg = bk.next()
                for c in range(8):
                    k.mm(pg[0:64, c * 64:(c + 1) * 64], sgl[:, c * 64:(c + 1) * 64], gw2b[:, h * 64:(h + 1) * 64], True, True,
                         R=[sgl, gw2b], W=[pg])
                gt = gt_r.next()
                k.copy("act", gt[:], v3(pg[0:64, :]), R=[pg], W=[gt])
                s1, s2, o1, o2 = st1_r.next(), st2_r.next(), po1_r.next(), po2_r.next()
                k.op("dve", lambda hh: hh.reduce_sum(out=s1[:], in_=ytok[:], axis=AX.X), R=[ytok], W=[s1])
                k.ts("dve", s1[:], s1[:], 1.0 / 64, None, ALU.mult, None, R=[s1], W=[s1])
                k.tt("pool", o1[:], ytok[:], s1[:, :].unsqueeze(2).to_broadcast([64, 8, 64]), ALU.subtract, R=[ytok, s1], W=[o1])
                k.tt("pool", o2[:], o1[:], o1[:], ALU.mult, R=[o1], W=[o2])
                k.op("dve", lambda hh: hh.reduce_sum(out=s2[:], in_=o2[:], axis=AX.X), R=[o2], W=[s2])
                k.act(s2[:], s2[:], AF.Sqrt, R=[s2, gnb], W=[s2], bias=gnb[:], scale=1.0 / 64)
                k.op("dve", lambda hh: hh.reciprocal(out=s2[:], in_=s2[:]), R=[s2], W=[s2])
                k.tt("dve", o1[:], o1[:], s2[:, :].unsqueeze(2).to_broadcast([64, 8, 64]), ALU.mult, R=[o1, s2], W=[o1])
                k.tt("pool", o1[:], o1[:], P["ln_w"][:, h * 64:(h + 1) * 64].unsqueeze(1).to_broadcast([64, 8, 64]), ALU.mult,
                     R=[o1, P["ln_w"]], W=[o1])
                k.tt("pool", o1[:], o1[:], P["ln_b"][:, h * 64:(h + 1) * 64].unsqueeze(1).to_broadcast([64, 8, 64]), ALU.add,
                     R=[o1, P["ln_b"]], W=[o1])
                k.tt("dve", o2[:], vtok[:], rk[:, :].unsqueeze(2).to_broadcast([64, 8, 64]), ALU.mult, R=[vtok, rk], W=[o2])
                k.tt("pool", o1[:], o1[:], o2[:], ALU.add, R=[o1, o2], W=[o1])
                k.tt("dve", o1[:], o1[:], gt[:], ALU.mult, R=[o1, gt], W=[o1])
                k.dma("pool", yv[:, 8 * blk:8 * blk + 8, h * 64:(h + 1) * 64], o1[:], R=[o1], W=[y])
        k.barrier()


def rwkv_inputs(pr, mu, w0, w_w2, a0, a_w2, g_w2, k_k, k_a, r_k, ln_w, ln_b, hh, S):
    A = np.ascontiguousarray
    hs = slice(hh * 256, (hh + 1) * 256)
    heads = lambda t: A(t.reshape(S, 4, 64).transpose(1, 2, 0))
    col = lambda v: A(v[hs].reshape(4, 64).T)
    bc = lambda v: A(np.broadcast_to(v[hs], (64, 256)))
    return dict(rT=heads(pr[:, 0:512][:, hs]), kT=heads(pr[:, 576:1088][:, hs]), vT=heads(pr[:, 1088:1600][:, hs]),
                wloT=A(pr[:, 512:576].T), aloT=A(pr[:, 1600:1664].T), gloT=A(pr[:, 1664:1792].T),
                mu_r=col(mu[0:512]), mu_k=col(mu[576:1088]), mu_v=col(mu[1088:1600]), mu_w=A(mu[512:576].reshape(64, 1)),
                mu_a=A(mu[1600:1664].reshape(64, 1)), mu_g=A(mu[1664:1792].reshape(128, 1)),
                w0=col(w0), a0=col(a0), k_k=col(k_k), k_a=col(k_a), r_k=col(r_k),
                ww2=A(w_w2[:, hs]), aw2=A(a_w2[:, hs]), gw2=A(g_w2[:, hs]), ln_w=bc(ln_w), ln_b=bc(ln_b))


RWKV_SHAPES = lambda S: dict(rT=[4, 64, S], kT=[4, 64, S], vT=[4, 64, S], wloT=[64, S], aloT=[64, S], gloT=[128, S],
                             mu_r=[64, 4], mu_k=[64, 4], mu_v=[64, 4], mu_w=[64, 1], mu_a=[64, 1], mu_g=[128, 1],
                             w0=[64, 4], a0=[64, 4], k_k=[64, 4], k_a=[64, 4], r_k=[64, 4],
                             ww2=[64, 256], aw2=[64, 256], gw2=[128, 256], ln_w=[64, 256], ln_b=[64, 256])


SEQ = 4096
NPROJ = 5192


def _ein(k, name, shape):
    return k.dram(name, shape, F32, kind="ExternalInput")


class View:
    def __init__(self, base, ap):
        self.t = ap
        self.trk = base.trk
        self.psum = False


NF, NT = 3888, 1304


def proj_col_order():
    F, T = [], []
    r = lambda a, n: list(range(a, a + n))
    for hh in range(2):
        F += r(hh * 256, 256) + r(512 + hh * 64, 64) + r(768 + hh * 64, 64) + r(1024 + hh * 64, 64) + r(640 + hh * 64, 64)
    b = 1304
    for hh in range(2):
        F += r(b + hh * 256, 256) + r(b + 576 + hh * 256, 256) + r(b + 1088 + hh * 256, 256)
    F += r(b + 512, 64) + r(b + 1600, 64) + r(b + 1664, 128)
    b = 3096
    for hh in range(2):
        F += r(b + hh * 128, 128) + r(b + 256 + hh * 128, 128)
    F += r(b + 1024, 16)
    F += r(4648, 544)
    for hh in range(2):
        T += r(896 + hh * 64, 64) + r(1152 + hh * 64, 64) + r(1280 + hh * 12, 12)
    for hh in range(2):
        T += r(3096 + 512 + hh * 256, 256) + r(3096 + 1040 + hh * 256, 256)
    assert len(F) == NF and len(T) == NT and len(set(F + T)) == NF + NT
    return np.array(F), np.array(T)


def emit_mod_fm(k, c2, w, badd, gpre, gpost, vecs, pre=None):
    if pre is not None:
        pre()
    wv = w.t.rearrange("(kc p) f -> p kc f", p=128)
    with ExitStack() as es:
        ct = k.sb([128, KC, 2], F32, es=es)
        sc = k.sb([128, KC, 2], F32, es=es)
        bt = k.sb([128, 18, KC], F32, es=es)
        gp1 = k.sb([128, 6, KC], F32, es=es)
        gp2 = k.sb([128, 6, KC], F32, es=es)
        mt = k.sb([128, 18, KC], F32, es=es)
        wt = k.rot(2, [128, KC, 512], F32, es=es)
        v5 = k.rot(2, [128, 5, KC], F32, es=es)
        k.dma("sp", ct[:], c2.t[:, :, :], R=[c2], W=[ct])
        k.dma("sp", bt[:], badd.t[:, :, :], R=[badd], W=[bt])
        k.dma("sp", gp1[:], gpre.t[:, :, :], R=[gpre], W=[gp1])
        k.dma("sp", gp2[:], gpost.t[:, :, :], R=[gpost], W=[gp2])
        k.act(sc[:], ct[:], AF.Silu, R=[ct], W=[sc])
        ps = Rot(k.banks[0:2])
        for grp in range(18):
            p_ = ps.next()
            for q in range(4):
                w_ = wt.next()
                for q2 in range(4):
                    k.dma("sp", w_[:, 4 * q2:4 * q2 + 4, :], wv[:, 4 * q2:4 * q2 + 4, grp * D + q * 512:grp * D + (q + 1) * 512], R=[w], W=[w_])
                for m in range(4):
                    ko = q * 4 + m
                    for kc in range(KC):
                        k.mm(p_[:, 2 * ko:2 * ko + 2], w_[:, kc, m * 128:(m + 1) * 128], sc[:, kc, :], kc == 0, kc == KC - 1, R=[w_, sc], W=[p_])
            k.tt("dve", mt[:, grp, :], p_[:, 0:2 * KC].rearrange("p (c two) -> p c two", two=2)[:, :, 0], bt[:, grp, :], ALU.add,
                 R=[p_, bt], W=[mt])
        for ls in range(6):
            v_ = v5.next()
            k.copy("dve", v_[:, 0:3, :], mt[:, 3 * ls:3 * ls + 3, :], R=[mt], W=[v_])
            k.copy("dve", v_[:, 3, :], gp1[:, ls, :], R=[gp1], W=[v_])
            k.copy("dve", v_[:, 4, :], gp2[:, ls, :], R=[gp2], W=[v_])
            k.dma("sp", vecs[ls].t[:, :, :], v_[:], R=[v_], W=[vecs[ls]])
        k.barrier()


def emit_proj2(k, xT, wF, wT, vec, pF, pTok, T):
    xTv = fview(xT)
    with ExitStack() as es:
        N = NormCtx(k, es, vec, 1.0)
        uTs = [k.sb([128, KC, TB], BF16, es=es) for _ in range(2)]
        wb = k.rot(4, [128, KC, 128], BF16, es=es)
        ob = k.rot(4, [128, TB], F32, es=es)
        wtb = k.rot(2, [128, KC, 512], BF16, es=es)
        ps = Rot(k.banks[1:7])
        nch = (NF + 127) // 128
        tgroups = [(0, 512), (512, 512), (1024, NT - 1024)]
        N.prenorm(xT, xTv, 0, uTs[0])
        for p in range(T // TB):
            t0 = p * TB
            uT = uTs[p % 2]
            if p + 1 < T // TB:
                N.prenorm(xT, xTv, t0 + TB, uTs[(p + 1) % 2])
            for c in range(nch):
                c0 = c * 128
                m = min(128, NF - c0)
                b = wb.next()
                k.dma("sp", b[:, :, :].rearrange("p a b -> p (a b)"), wF.t[c], R=[wF], W=[b])
                pp = ps.next()
                for kc in range(KC):
                    k.mm(pp[0:m, :], b[:, kc, 0:m], uT[:, kc, :], kc == 0, kc == KC - 1, R=[b, uT], W=[pp])
                o = ob.next()
                if c % 2 == 0:
                    k.copy("act", o[0:m, :], pp[0:m, :], R=[pp], W=[o])
                    k.dma("act", pF.t[c0:c0 + m, t0:t0 + TB], o[0:m, :], R=[o], W=[pF])
                else:
                    k.copy("dve", o[0:m, :], pp[0:m, :], R=[pp], W=[o])
                    k.dma("pool", pF.t[c0:c0 + m, t0:t0 + TB], o[0:m, :], R=[o], W=[pF])
            for gi, (g0, gn) in enumerate(tgroups):
                wt_ = wtb.next()
                k.dma("sp", wt_[:, :, :].rearrange("p a b -> p (a b)"), wT.t[gi], R=[wT], W=[wt_])
                for sub in range(4):
                    pp = ps.next()
                    for kc in range(KC):
                        k.mm(pp[:, 0:gn], uT[:, kc, sub * 128:(sub + 1) * 128], wt_[:, kc, 0:gn], kc == 0, kc == KC - 1, R=[uT, wt_], W=[pp])
                    o = ob.next()
                    tk = t0 + sub * 128
                    if sub % 2 == 0:
                        k.copy("act", o[:, 0:gn], pp[:, 0:gn], R=[pp], W=[o])
                        k.dma("act", pTok.t[tk:tk + 128, g0:g0 + gn], o[:, 0:gn], R=[o], W=[pTok])
                    else:
                        k.copy("dve", o[:, 0:gn], pp[:, 0:gn], R=[pp], W=[o])
                        k.dma("pool", pTok.t[tk:tk + 128, g0:g0 + gn], o[:, 0:gn], R=[o], W=[pTok])
        k.barrier()


_ACT_KEYS = {"nsa": ("qT", "qsT", "kT", "ksT", "vcT", "v_tok", "gl"), "rwkv": ("rT", "kT", "vT", "wloT", "aloT", "gloT"),
             "gla": ("qT", "kT", "v_ch", "r_ch", "aloT"), "mla": ("cqT", "ckvT", "krp", "krsp")}
_SHARED_TABLES = ("cos", "sin", "ov", "selb", "ebig", "c96", "s96")


def _param_shapes():
    out = {}
    for nm, shp in (("nsa", NSA_SHAPES(SEQ)), ("rwkv", RWKV_SHAPES(SEQ)), ("gla", GLA_SHAPES(SEQ)), ("mla", MLA_SHAPES(SEQ))):
        out[nm] = {n: s for n, s in shp.items() if n not in _ACT_KEYS[nm]}
    return out


def mixer_views(pF, pTok, hh):
    f, t = pF.t, pTok.t
    V = lambda ap, base: View(base, ap)
    b = hh * 512
    nsa = dict(qT=V(f[b:b + 256, :].rearrange("(g d) s -> g d s", d=64), pF),
               kT=V(f[b + 256:b + 448, :].rearrange("(i d) s -> i d s", d=64), pF),
               vcT=V(f[b + 448:b + 512, :], pF),
               v_tok=V(t[:, hh * 140:hh * 140 + 128].rearrange("s (i d) -> i s d", d=64), pTok),
               gl=V(t[:, hh * 140 + 128:hh * 140 + 140], pTok))
    b = 1024 + hh * 768
    rw = dict(rT=V(f[b:b + 256, :].rearrange("(g d) s -> g d s", d=64), pF),
              kT=V(f[b + 256:b + 512, :].rearrange("(g d) s -> g d s", d=64), pF),
              vT=V(f[b + 512:b + 768, :].rearrange("(g d) s -> g d s", d=64), pF),
              wloT=V(f[2560:2624, :], pF), aloT=V(f[2624:2688, :], pF), gloT=V(f[2688:2816, :], pF))
    b = 2816 + hh * 256
    tb = 280 + hh * 512
    gla = dict(qT=V(f[b:b + 128, :].rearrange("(g d) s -> g d s", d=64), pF),
               kT=V(f[b + 128:b + 256, :].rearrange("(g d) s -> g d s", d=64), pF),
               aloT=V(f[3328:3344, :], pF),
               v_ch=V(t[:, tb:tb + 256].rearrange("(c p) f -> p c f", p=CH), pTok),
               r_ch=V(t[:, tb + 256:tb + 512].rearrange("(c p) f -> p c f", p=CH), pTok))
    mla = dict(cqT=V(f[3344:3728, :], pF), ckvT=V(f[3728:3856, :], pF), krT=V(f[3856:3888, :], pF))
    return dict(nsa=nsa, rwkv=rw, gla=gla, mla=mla)


def build_fused(L=2):
    nc, es, k = new_prog()
    S = SEQ
    PS = _param_shapes()
    emits = dict(nsa=emit_nsa, rwkv=emit_rwkv, gla=emit_gla, mla=emit_mla)
    with es:
        xT = _ein(k, "xT", [D, S])
        c2, wmod = _ein(k, "c2", [128, KC, 2]), _ein(k, "wmod", [D, 18 * D])
        badd, gpre, gpost = _ein(k, "badd", [128, 18, KC]), _ein(k, "gpre", [128, 6, KC]), _ein(k, "gpost", [128, 6, KC])
        tables = {n: _ein(k, "tab_" + n, (NSA_SHAPES(S) | MLA_SHAPES(S))[n]) for n in _SHARED_TABLES}
        oT = k.dram("oT", [D, S], F32, kind="ExternalOutput")
        vecs = [k.dram("vec%d" % i, [128, 5, KC], F32) for i in range(3 * L)]
        xa, xb_ = k.dram("xa", [D, S], F32), k.dram("xb", [D, S], F32)
        pF, pTok, yTok = k.dram("pF", [NF, S], F32), k.dram("pTok", [S, NT], F32), k.dram("yTok", [S, D], F32)
        WSH = dict(wg1=(NFF, KC * 128, 2048), wu1=(NFF, KC * 128, 2048), wd1=(KC, NFF * 128, 1408), wF=(31, KC * 128, 2048), wT=(3, KC * 512, 2048),
                   wgt=(4 * KC, KC * 128, 2048), wbr=(KC, 16 * 128, 2048), wout=(KC, KC * 128, 2048),
                   wg2=(NFF, KC * 128, 2048), wu2=(NFF, KC * 128, 2048), wd2=(KC, NFF * 128, 1408))
        Wf = [{n: _ein(k, "l%d_%s" % (l, n), [c, 128, r]) for n, (c, r, _) in WSH.items()} for l in range(L)]
        Wb = [{n: k.dram("l%d_%s_bf" % (l, n), [c, 128, r], BF16) for n, (c, r, _) in WSH.items()} for l in range(L)]
        groups = []
        for l in range(L):
            groups += [(l, ("wg1", "wu1", "wd1")), (l, ("wF", "wT")), (l, ("wgt", "wbr", "wout")), (l, ("wg2", "wu2", "wd2"))]
        state = {"next": 0}

        def convert_ahead(upto):
            while state["next"] < min(upto, len(groups)):
                l_, names = groups[state["next"]]
                for n in names:
                    emit_convert(k, Wf[l_][n], Wb[l_][n], WSH[n][2])
                state["next"] += 1

        emit_mod_fm(k, c2, wmod, badd, gpre, gpost, vecs, pre=lambda: convert_ahead(2))
        cur = xT
        for l in range(L):
            W = Wb[l]
            g0 = 4 * l
            emit_ffn(k, cur, W["wg1"], W["wu1"], W["wd1"], vecs[3 * l], xa, S, 0.5)
            emit_proj2(k, xa, W["wF"], W["wT"], vecs[3 * l + 1], pF, pTok, S)
            for hh in range(2):
                views = mixer_views(pF, pTok, hh)
                for br, nm in enumerate(("nsa", "rwkv", "gla", "mla")):
                    if br % 2 == 0:
                        convert_ahead(g0 + 3 + hh * 2 + br // 2)
                    I = {}
                    for n, shp in PS[nm].items():
                        I[n] = tables[n] if n in _SHARED_TABLES else _ein(k, "l%d_h%d_%s_%s" % (l, hh, nm, n), shp)
                    I.update(views[nm])
                    yv = View(yTok, yTok.t[:, br * 512 + hh * 256:br * 512 + hh * 256 + 256])
                    emits[nm](k, I, yv, S)
            emit_merge(k, xa, None, W["wgt"], W["wbr"], W["wout"], vecs[3 * l + 1], xb_, S, y_tok=yTok)
            last = l == L - 1
            emit_ffn(k, xb_, W["wg2"], W["wu2"], W["wd2"], vecs[3 * l + 2], oT if last else xa, S, 0.5)
            cur = xa
        k.finish([oT])
    return nc, k.ninstr


def fused_inputs(P, b, L=2):
    A = lambda a: np.ascontiguousarray(np.asarray(a, dtype=np.float32))
    S = SEQ
    m = {}
    m["xT"] = A(np.asarray(P["x"][b]).T)
    cb = np.asarray(P["c"][b], dtype=np.float32)
    m["c2"] = A(np.repeat(cb.reshape(KC, 128).T[:, :, None], 2, axis=2))
    m["wmod"] = P["_wmod"]
    m["badd"] = P["_badd"]
    m["gpre"], m["gpost"] = P["_gpre"], P["_gpost"]
    for n in _SHARED_TABLES:
        m["tab_" + n] = P["_tab"][n]
    for l in range(L):
        for n, v in P["_lw"][l].items():
            m["l%d_%s" % (l, n)] = v
        for hh in range(2):
            for nm in ("nsa", "rwkv", "gla", "mla"):
                for n, v in P["_mp"][l][hh][nm].items():
                    if n not in _SHARED_TABLES:
                        m["l%d_h%d_%s_%s" % (l, hh, nm, n)] = v
    return m


def kernel(**P):
    A = lambda a: np.ascontiguousarray(np.asarray(a, dtype=np.float32))
    L, S = int(np.asarray(P["ada_w"]).shape[0]), SEQ
    Fo, To = proj_col_order()
    fm = lambda v: A(np.asarray(v, dtype=np.float32).reshape(-1, KC, 128).transpose(2, 0, 1))
    P = dict(P)
    P["_wmod"] = A(np.concatenate([np.asarray(P["ada_w"][l, s]) for l in range(L) for s in range(3)], axis=1))
    P["_badd"] = fm(np.asarray(P["ada_b"]).reshape(L * 3 * 3, D))
    P["_gpre"], P["_gpost"] = fm(np.asarray(P["pre_g"]).reshape(L * 3, D)), fm(np.asarray(P["post_g"]).reshape(L * 3, D))
    z = lambda n: np.zeros((S, n), np.float32)
    lw, mp, tab = [], [], {}
    for l in range(L):
        w_in = np.asarray(P["mix_w_in"][l])
        cw = lambda a: chunk_w(a, 128)
        lw.append(dict(wg1=cw(P["ffn_wg"][l, 0]), wu1=cw(P["ffn_wu"][l, 0]), wd1=cw(P["ffn_wd"][l, 0]),
                       wg2=cw(P["ffn_wg"][l, 1]), wu2=cw(P["ffn_wu"][l, 1]), wd2=cw(P["ffn_wd"][l, 1]),
                       wF=cw(w_in[:, Fo]), wT=chunk_w(w_in[:, To], 512), wgt=cw(w_in[:, NPROJ:]),
                       wbr=cw(np.asarray(P["mix_w_branch"][l]).reshape(D, D)), wout=cw(P["mix_w_out"][l])))
        lw[-1] = {n: np.ascontiguousarray(v.reshape(v.shape[0], 128, -1)) for n, v in lw[-1].items()}
        per_hh = []
        for hh in range(2):
            g = lambda name: np.asarray(P[name][l])
            d = dict(nsa=nsa_inputs(z(1304), g("nsa_cmp_pos"), g("nsa_cmp_w1"), g("nsa_cmp_w2"), hh, S),
                     rwkv=rwkv_inputs(z(1792), *[g(n) for n in ("rwkv_mu", "rwkv_w0", "rwkv_w_w2", "rwkv_a0", "rwkv_a_w2", "rwkv_g_w2",
                                                                "rwkv_k_k", "rwkv_k_a", "rwkv_r_k", "rwkv_ln_w", "rwkv_ln_b")], hh, S),
                     gla=gla_inputs(z(1552), g("gla_alpha_w2"), g("gla_alpha_b"), g("gla_norm_g"), hh, S),
                     mla=mla_inputs(z(544), g("mla_w_uq"), g("mla_w_ukv"), g("mla_q_norm"), g("mla_kv_norm"), hh, S))
            for nm in d:
                for n in list(d[nm]):
                    if n in _SHARED_TABLES:
                        tab[n] = A(d[nm][n])
                    if n in _ACT_KEYS[nm]:
                        del d[nm][n]
                    else:
                        d[nm][n] = A(d[nm][n])
            per_hh.append(d)
        mp.append(per_hh)
    P["_lw"], P["_mp"], P["_tab"] = lw, mp, tab
    nc, _ = build_fused(L)
    in_maps = [fused_inputs(P, c % 4, L) for c in range(NCORES)]
    res = run_bass_kernel_spmd(nc, in_maps, core_ids=list(range(NCORES))).results
    out = np.empty((4, S, D), np.float32)
    for b in range(4):
        out[b] = res[b]["oT"].T
    return out
```

```python
import numpy as np
from contextlib import ExitStack
import concourse.bass as bass
import concourse.mybir as mybir
from concourse.bass_utils import run_bass_kernel_spmd

F32 = mybir.dt.float32
BF16 = mybir.dt.bfloat16
AF = mybir.ActivationFunctionType
ALU = mybir.AluOpType
AX = mybir.AxisListType

D = 2048
KC = D // 128
DFF = 5632
NFF = DFF // 128
EPS = 1e-6
NCORES = 8


class Trk:
    __slots__ = ("w", "r")

    def __init__(self):
        self.w = None
        self.r = {}


class Buf:
    def __init__(self, t, trk=None, psum=False):
        self.t = t
        self.trk = trk or Trk()
        self.psum = psum

    def __getitem__(self, idx):
        return self.t[idx]


class K:
    def __init__(self, nc, es, n_dma_sems=40):
        self.nc = nc
        self.es = es
        self.eng = {"pe": nc.tensor, "dve": nc.vector, "act": nc.scalar, "pool": nc.gpsimd, "sp": nc.sync}
        self.esem = {k: es.enter_context(nc.semaphore("es_" + k)) for k in self.eng}
        self.ecnt = {k: 0 for k in self.eng}
        self.waited = {k: {} for k in self.eng}
        self.dsems = [[es.enter_context(nc.semaphore("ds%d" % i)), 0] for i in range(n_dma_sems)]
        self.dnext = 0
        self.uid = 0
        self.ninstr = 0
        self.banks = [Buf(es.enter_context(nc.psum_tensor("bank%d" % i, [128, 512], F32)), psum=True) for i in range(7)]
        self.bankb = Buf(es.enter_context(nc.psum_tensor("bankb", [128, 1024], BF16)), psum=True)

    def sb(self, shape, dt, name=None, es=None):
        self.uid += 1
        return Buf((es or self.es).enter_context(self.nc.sbuf_tensor("sb%d" % self.uid, list(shape), dt)))

    def rot(self, n, shape, dt, es=None):
        return Rot([self.sb(shape, dt, es=es) for _ in range(n)])

    def barrier(self):
        deps = {id(self.esem[e]): (self.esem[e], self.ecnt[e]) for e in self.eng if self.ecnt[e] > 0}
        for s, c in self.dsems:
            if c > 0:
                deps[id(s)] = (s, c)
        for e in self.eng:
            self._wait(e, dict(deps))

    def ps(self, shape, dt=F32, name=None):
        self.uid += 1
        return Buf(self.es.enter_context(self.nc.psum_tensor(name or "ps%d" % self.uid, list(shape), dt)))

    def dram(self, name, shape, dt, kind="Internal"):
        return Buf(self.nc.dram_tensor(name, list(shape), dt, kind=kind).ap())

    def _deps(self, R, W):
        deps = {}

        def add(d):
            if d is None:
                return
            s, v = d
            k = id(s)
            if k not in deps or deps[k][1] < v:
                deps[k] = (s, v)

        for b in R:
            add(b.trk.w)
        for b in W:
            add(b.trk.w)
            for d in b.trk.r.values():
                add(d)
        return deps

    def _wait(self, e, deps, skip_sem=None):
        h = self.eng[e]
        wd = self.waited[e]
        for k, (s, v) in deps.items():
            if skip_sem is not None and s is skip_sem:
                continue
            if wd.get(k, 0) < v:
                h.wait_ge(s, v)
                wd[k] = v

    def _commit(self, d, R, W):
        for b in W:
            b.trk.w = d
            b.trk.r = {}
        for b in R:
            b.trk.r[id(d[0])] = d

    def op(self, e, fn, R=(), W=()):
        if any(b.psum for b in R):
            W = list(W) + [b for b in R if b.psum]
            R = [b for b in R if not b.psum]
        deps = self._deps(R, W)
        self._wait(e, deps, skip_sem=self.esem["pe"] if e == "pe" else None)
        ins = fn(self.eng[e])
        self.ecnt[e] += 1
        ins.then_inc(self.esem[e], 1)
        self.ninstr += 1
        self._commit((self.esem[e], self.ecnt[e]), R, W)

    def dma(self, q, out_ap, in_ap, R=(), W=(), **kw):
        deps = self._deps(R, W)
        slot = self.dsems[self.dnext]
        self.dnext = (self.dnext + 1) % len(self.dsems)
        if slot[1] > 0:
            deps[id(slot[0])] = (slot[0], slot[1])
        self._wait(q, deps)
        ins = self.eng[q].dma_start(out=out_ap, in_=in_ap, **kw)
        slot[1] += 16
        ins.then_inc(slot[0], 16)
        self.ninstr += 1
        self._commit((slot[0], slot[1]), R, W)

    def finish(self, bufs, e="sp"):
        deps = self._deps(bufs, ())
        self._wait(e, deps)

    def mm(self, out, lhsT, rhs, start, stop, R, W):
        self.op("pe", lambda h: h.matmul(out, lhsT=lhsT, rhs=rhs, start=start, stop=stop), R=R, W=W)

    def act(self, out, in_, func, R, W, e="act", **kw):
        self.op(e, lambda h: h.activation(out=out, in_=in_, func=func, **kw), R=R, W=W)

    def tt(self, e, out, in0, in1, op, R, W):
        self.op(e, lambda h: h.tensor_tensor(out=out, in0=in0, in1=in1, op=op), R=R, W=W)

    def ts(self, e, out, in0, s1, s2, op0, op1, R, W):
        if op1 is None:
            self.op(e, lambda h: h.tensor_scalar(out=out, in0=in0, scalar1=s1, scalar2=None, op0=op0), R=R, W=W)
        else:
            self.op(e, lambda h: h.tensor_scalar(out=out, in0=in0, scalar1=s1, scalar2=s2, op0=op0, op1=op1), R=R, W=W)

    def stt(self, out, in0, scalar, in1, op0, op1, R, W):
        self.op("dve", lambda h: h.scalar_tensor_tensor(out=out, in0=in0, scalar=scalar, in1=in1, op0=op0, op1=op1), R=R, W=W)

    def copy(self, e, out, in_, R, W):
        if e == "act":
            self.op(e, lambda h: h.copy(out=out, in_=in_), R=R, W=W)
        else:
            self.op(e, lambda h: h.tensor_copy(out=out, in_=in_), R=R, W=W)

    def cast(self, out, in_, R, W):
        self.ncast = getattr(self, "ncast", 0) + 1
        self.copy("dve" if self.ncast % 2 else "act", out, in_, R, W)

    def memset(self, e, ap, val, W):
        self.op(e, lambda h: h.memset(ap, val), W=W)


class Rot:
    def __init__(self, bufs):
        self.bufs = bufs
        self.i = 0

    def next(self):
        b = self.bufs[self.i]
        self.i = (self.i + 1) % len(self.bufs)
        return b


TB = 512


class NormCtx:
    def __init__(self, k, es, vec, rw, post_bank=None):
        self.k = k
        self.big = k.sb([128, KC, TB], F32, es=es)
        self.xin = k.rot(2, [128, 4, TB], F32, es=es)
        self.sq = k.rot(2, [128, 4, TB], BF16, es=es)
        self.tmp = k.rot(2, [128, TB], F32, es=es)
        self.xr = k.rot(2, [128, TB], F32, es=es)
        self.ob = k.rot(2, [128, TB], F32, es=es)
        self.rstd = k.sb([128, TB], F32, es=es)
        self.rt = k.sb([128, TB], F32, es=es)
        self.rstd_pre = k.sb([128, TB], F32, es=es)
        self.rt_pre = k.sb([128, TB], F32, es=es)
        self.ones = k.sb([128, 128], BF16, es=es)
        self.vt = k.sb([128, 5, KC], F32, es=es)
        self.sc = k.sb([128, KC], F32, es=es)
        self.gp = k.sb([128, KC], F32, es=es)
        self.epsb = k.sb([128, 1], F32, es=es)
        self.ps_pre = k.banks[0]
        self.ps_stat = post_bank if post_bank is not None else k.banks[0]
        k.memset("dve", self.ones[:], 1.0, W=[self.ones])
        k.memset("dve", self.epsb[:], EPS, W=[self.epsb])
        k.dma("sp", self.vt[:], vec.t[:, :, :], R=[vec], W=[self.vt])
        k.stt(self.sc[:], self.vt[:, 1, :], 1.0, self.vt[:, 3, :], ALU.add, ALU.mult, R=[self.vt], W=[self.sc])
        k.stt(self.gp[:], self.vt[:, 2, :], float(rw), self.vt[:, 4, :], ALU.mult, ALU.mult, R=[self.vt], W=[self.gp])

    def rstd_from(self, ps, dim, rt=None, rstd=None):
        k = self.k
        rt = rt or self.rt
        rstd = rstd or self.rstd
        k.act(rt[:], ps[:], AF.Sqrt, R=[ps, self.epsb], W=[rt], bias=self.epsb[:], scale=1.0 / dim)
        k.op("dve", lambda h: h.reciprocal(out=rstd[:], in_=rt[:]), R=[rt], W=[rstd])

    def prenorm(self, xT, xTv, t0, uT):
        k = self.k
        for q in range(4):
            xp = self.xin.next()
            k.dma("pool", xp[:], xTv[:, 4 * q:4 * q + 4, t0:t0 + TB], R=[xT], W=[xp])
            s = self.sq.next()
            k.act(s[:], xp[:], AF.Square, R=[xp], W=[s])
            for i in range(4):
                kc = 4 * q + i
                k.mm(self.ps_pre[:], self.ones[:], s[:, i, :], kc == 0, kc == KC - 1, R=[self.ones, s], W=[self.ps_pre])
        self.rstd_from(self.ps_pre, D, self.rt_pre, self.rstd_pre)
        for q in range(4):
            xp = self.xin.next()
            k.dma("pool", xp[:], xTv[:, 4 * q:4 * q + 4, t0:t0 + TB], R=[xT], W=[xp])
            for i in range(4):
                kc = 4 * q + i
                tm = self.tmp.next()
                k.stt(tm[:], xp[:, i, :], self.sc[:, kc:kc + 1], self.rstd_pre[:], ALU.mult, ALU.mult,
                      R=[xp, self.sc, self.rstd_pre], W=[tm])
                k.act(uT[:, kc, :], tm[:], AF.Identity, R=[tm, self.vt], W=[uT], bias=self.vt[:, 0, kc:kc + 1], scale=1.0)

    def take(self, py, dc):
        k = self.k
        k.copy("act", self.big[:, dc, :], py[:], R=[py], W=[self.big])
        s = self.sq.next()
        k.act(s[:, 0, :], py[:], AF.Square, R=[py], W=[s])
        k.mm(self.ps_stat[:], self.ones[:], s[:, 0, :], dc == 0, dc == KC - 1, R=[self.ones, s], W=[self.ps_stat])

    def postnorm(self, xT, xTv, oT, oTv, t0):
        k = self.k
        self.rstd_from(self.ps_stat, D)
        xs = {}
        for kc in range(KC):
            for k2 in range(kc, min(kc + 2, KC)):
                if k2 not in xs:
                    xs[k2] = self.xr.next()
                    k.dma("pool", xs[k2][:], xTv[:, k2, t0:t0 + TB], R=[xT], W=[xs[k2]])
            x_ = xs[kc]
            tm = self.tmp.next()
            k.stt(tm[:], self.big[:, kc, :], self.gp[:, kc:kc + 1], self.rstd[:], ALU.mult, ALU.mult,
                  R=[self.big, self.gp, self.rstd], W=[tm])
            o = self.ob.next()
            k.tt("pool", o[:], tm[:], x_[:], ALU.add, R=[tm, x_], W=[o])
            k.dma("pool", oTv[:, kc, t0:t0 + TB], o[:], R=[o], W=[oT])


def fview(b):
    return b.t.rearrange("(kc p) t -> p kc t", p=128)


def emit_ffn(k, xT, wg, wu, wd, vec, oT, T, rw):
    xTv, oTv = fview(xT), fview(oT)
    with ExitStack() as es:
        N = NormCtx(k, es, vec, rw, post_bank=k.banks[6])
        uTs = [k.sb([128, KC, TB], BF16, es=es) for _ in range(2)]
        aT = k.sb([128, NFF, TB], BF16, es=es)
        wgb = k.rot(3, [128, KC, 128], BF16, es=es)
        wub = k.rot(3, [128, KC, 128], BF16, es=es)
        wdb = k.rot(2, [128, NFF, 128], BF16, es=es)
        sgb = k.rot(2, [128, TB], F32, es=es)
        ps_g, ps_u, ps_y = Rot(k.banks[1:3]), Rot(k.banks[3:4]), Rot(k.banks[4:6])
        NP = T // TB
        N.prenorm(xT, xTv, 0, uTs[0])
        for p in range(NP):
            t0 = p * TB
            uT = uTs[p % 2]
            for j in range(NFF):
                wb = []
                for (wsrc, pool) in ((wg, wgb), (wu, wub)):
                    b = pool.next()
                    k.dma("sp", b[:, :, :].rearrange("p a b -> p (a b)"), wsrc.t[j], R=[wsrc], W=[b])
                    wb.append(b)
                pg, pu = ps_g.next(), ps_u.next()
                for kc in range(KC):
                    k.mm(pg[:], wb[0][:, kc, :], uT[:, kc, :], kc == 0, kc == KC - 1, R=[wb[0], uT], W=[pg])
                for kc in range(KC):
                    k.mm(pu[:], wb[1][:, kc, :], uT[:, kc, :], kc == 0, kc == KC - 1, R=[wb[1], uT], W=[pu])
                sg = sgb.next()
                k.act(sg[:], pg[:], AF.Silu, R=[pg], W=[sg])
                k.tt("dve", aT[:, j, :], sg[:], pu[:], ALU.mult, R=[sg, pu], W=[aT])
            if p + 1 < NP:
                N.prenorm(xT, xTv, t0 + TB, uTs[(p + 1) % 2])
            for dc in range(KC):
                b = wdb.next()
                k.dma("sp", b[:, :, :].rearrange("p a b -> p (a b)"), wd.t[dc], R=[wd], W=[b])
                py = ps_y.next()
                for j in range(NFF):
                    k.mm(py[:], b[:, j, :], aT[:, j, :], j == 0, j == NFF - 1, R=[b, aT], W=[py])
                N.take(py, dc)
            N.postnorm(xT, xTv, oT, oTv, t0)
        k.barrier()


def emit_convert(k, w, wb, piece):
    n, _, R_ = w.t.shape
    for c in range(n):
        for r0 in range(0, R_, piece):
            k.dma("pool", wb.t[c][:, r0:r0 + piece], w.t[c][:, r0:r0 + piece], R=[w], W=[wb])


def emit_merge(k, xT, yT, wgt, wbr, wout, vec, oT, T, y_tok=None):
    xTv, oTv = fview(xT), fview(oT)
    if y_tok is None:
        yTv = yT.t.rearrange("(c p) t -> p c t", p=128)
    with ExitStack() as es:
        N = NormCtx(k, es, vec, 1.0, post_bank=k.banks[4])
        uTs = [k.sb([128, KC, TB], BF16, es=es) for _ in range(2)]
        yb = k.sb([128, 16, TB], BF16, es=es)
        mT = k.sb([128, KC, TB], BF16, es=es)
        ystg = k.rot(2, [128, 2, TB], F32, es=es)
        if y_tok is not None:
            ytk_b = k.rot(2, [128, 2048], BF16, es=es)
            identb = k.sb([128, 128], BF16, es=es)
            k.memset("pool", identb[:], 1.0, W=[identb])
            k.op("pool", lambda h: h.affine_select(out=identb[:], in_=identb[:], pattern=[[-1, 128]], compare_op=ALU.is_equal,
                                                  fill=0.0, base=0, channel_multiplier=1), R=[identb], W=[identb])
        wgb = k.rot(4, [128, KC, 128], BF16, es=es)
        wbb = k.rot(2, [128, 16, 128], BF16, es=es)
        wob = k.rot(2, [128, KC, 128], BF16, es=es)
        sgb = k.rot(2, [128, TB], F32, es=es)
        mb = k.rot(2, [128, TB], F32, es=es)
        macc = k.rot(2, [128, TB], F32, es=es)
        ps_g, ps_b, ps_y = Rot(k.banks[1:3]), Rot(k.banks[3:4]), Rot(k.banks[5:7])
        N.prenorm(xT, xTv, 0, uTs[0])
        for p in range(T // TB):
            t0 = p * TB
            uT = uTs[p % 2]
            if y_tok is None:
                for c2 in range(8):
                    st = ystg.next()
                    k.dma("sp", st[:], yTv[:, 2 * c2:2 * c2 + 2, t0:t0 + TB], R=[yT], W=[st])
                    k.cast(yb[:, 2 * c2:2 * c2 + 2, :], st[:], R=[st], W=[yb])
            else:
                for sub in range(4):
                    tk = t0 + sub * 128
                    ytb = ytk_b.next()
                    for hf in range(2):
                        st = ystg.next()
                        k.dma("sp", st[:, :, :].rearrange("p a b -> p (a b)"), y_tok.t[tk:tk + 128, hf * 1024:(hf + 1) * 1024], R=[y_tok], W=[st])
                        k.cast(ytb[:, hf * 1024:(hf + 1) * 1024], st[:, :, :].rearrange("p a b -> p (a b)"), R=[st], W=[ytb])
                    for c4 in range(4):
                        pb = k.bankb
                        for i in range(4):
                            c = 4 * c4 + i
                            k.op("pe", lambda h: h.transpose(out=pb[:, i * 128:(i + 1) * 128], in_=ytb[:, c * 128:(c + 1) * 128],
                                                             identity=identb[:, :]), R=[ytb, identb], W=[pb])
                        k.copy("act", yb[:, 4 * c4:4 * c4 + 4, sub * 128:(sub + 1) * 128],
                               pb[:, 0:512].rearrange("p (c t) -> p c t", t=128), R=[pb], W=[yb])
            for dc in range(KC):
                wbt = wbb.next()
                k.dma("sp", wbt[:, :, :].rearrange("p a b -> p (a b)"), wbr.t[dc], R=[wbr], W=[wbt])
                acc = macc.next()
                for br in range(4):
                    wg_ = wgb.next()
                    k.dma("sp", wg_[:, :, :].rearrange("p a b -> p (a b)"), wgt.t[br * KC + dc], R=[wgt], W=[wg_])
                    pg, pb = ps_g.next(), ps_b.next()
                    for kc in range(KC):
                        k.mm(pg[:], wg_[:, kc, :], uT[:, kc, :], kc == 0, kc == KC - 1, R=[wg_, uT], W=[pg])
                    for kc in range(4):
                        k.mm(pb[:], wbt[:, br * 4 + kc, :], yb[:, br * 4 + kc, :], kc == 0, kc == 3, R=[wbt, yb], W=[pb])
                    sg = sgb.next()
                    k.act(sg[:], pg[:], AF.Sigmoid, R=[pg], W=[sg])
                    if br == 0:
                        k.tt("dve", acc[:], sg[:], pb[:], ALU.mult, R=[sg, pb], W=[acc])
                    else:
                        m_ = mb.next()
                        k.tt("dve", m_[:], sg[:], pb[:], ALU.mult, R=[sg, pb], W=[m_])
                        if br < 3:
                            k.tt("pool", acc[:], acc[:], m_[:], ALU.add, R=[acc, m_], W=[acc])
                        else:
                            k.tt("pool", mT[:, dc, :], acc[:], m_[:], ALU.add, R=[acc, m_], W=[mT])
            if p + 1 < T // TB:
                N.prenorm(xT, xTv, t0 + TB, uTs[(p + 1) % 2])
            for oc in range(KC):
                wo_ = wob.next()
                k.dma("sp", wo_[:, :, :].rearrange("p a b -> p (a b)"), wout.t[oc], R=[wout], W=[wo_])
                py = ps_y.next()
                for dc in range(KC):
                    k.mm(py[:], wo_[:, dc, :], mT[:, dc, :], dc == 0, dc == KC - 1, R=[wo_, mT], W=[py])
                N.take(py, oc)
            N.postnorm(xT, xTv, oT, oTv, t0)
        k.barrier()


def chunk_w(w, nc_):
    w = np.asarray(w, dtype=np.float32)
    K_, N_ = w.shape
    npad = (-N_) % nc_
    if npad:
        w = np.concatenate([w, np.zeros((K_, npad), np.float32)], axis=1)
    return np.ascontiguousarray(w.reshape(K_ // 128, 128, (N_ + npad) // nc_, nc_).transpose(2, 1, 0, 3))


def new_prog():
    nc = bass.Bass("TRN2", target_bir_lowering=False)
    es = ExitStack()
    k = K(nc, es)
    return nc, es, k


def acc_view(acc):
    return acc[:, :].rearrange("p (s c) -> p s c", c=128)


def attn_chunk(k, q_ap, qbufs, pairs, acc, W, scale, sbanks, ptr, q0, acc2=None, W2=0, ahead=1):
    state = {"first": True}

    def scores(pr):
        nk = pr["nk"]
        ps = sbanks.next()
        k.mm(ps[0:nk, :], pr["kT"], q_ap, True, pr.get("extra") is None, R=pr["bufs"] + qbufs, W=[ps])
        if pr.get("extra") is not None:
            el, er, eb = pr["extra"]
            k.mm(ps[0:nk, :], el, er, False, True, R=eb, W=[ps])
        pt = ptr.next()
        k.act(pt[0:nk, :], ps[0:nk, :], AF.Exp, R=[ps], W=[pt], scale=scale)
        if pr.get("mask") is not None:
            base, cm, step = pr["mask"]
            k.op("pool", lambda h: h.affine_select(out=pt[0:nk, :], in_=pt[0:nk, :], pattern=[[step, 512]],
                                                  compare_op=ALU.is_ge, fill=0.0, base=base, channel_multiplier=cm),
                 R=[pt], W=[pt])
        return pt

    def pv(pr, pt):
        nk = pr["nk"]
        for sub in range(4):
            if pr.get("kpos0") is not None and pr["kpos0"] > q0 + sub * 128 + 127:
                continue
            if sub in pr.get("skip", ()):
                continue
            c0 = sub * 128
            fst = state["first"]
            k.op("pe", lambda h: h.matmul(acc[:, c0:c0 + W], lhsT=pt[0:nk, c0:c0 + 128], rhs=pr["v"], start=fst, stop=True,
                                          skip_group_check=True), R=[pt] + pr["bufs"], W=[acc])
            if acc2 is not None:
                k.op("pe", lambda h: h.matmul(acc2[:, c0:c0 + W2], lhsT=pt[0:nk, c0:c0 + 128], rhs=pr["v2"], start=fst, stop=True,
                                              skip_group_check=True), R=[pt] + pr["bufs"], W=[acc2])
            state["first"] = False

    pend = []
    for pr in pairs:
        pend.append((pr, scores(pr)))
        if len(pend) > ahead:
            pv(*pend.pop(0))
    while pend:
        pv(*pend.pop(0))


def rstd_calc(k, ps, nparts, dim, rt, rstd, epsb, eps=EPS):
    k.act(rt[0:nparts, :], ps[0:nparts, :], AF.Sqrt, R=[ps, epsb], W=[rt], bias=epsb[0:nparts, :], scale=1.0 / dim)
    k.op("dve", lambda h: h.reciprocal(out=rstd[0:nparts, :], in_=rt[0:nparts, :]), R=[rt], W=[rstd])


def emit_mla(k, I, y, S):
    nkb, nqc = S // 128, S // 512
    SC = 96 ** -0.5
    with ExitStack() as es:
        QT = k.sb([96, 4, S], BF16, es=es)
        KT = k.sb([96, 4, S], BF16, es=es)
        Va = k.sb([128, 4, nkb, 65], BF16, es=es)
        ysb = k.sb([128, nkb, 256], F32, es=es)
        ones = k.sb([128, 128], BF16, es=es)
        epsb = k.sb([128, 1], F32, es=es)
        k.memset("dve", ones[:], 1.0, W=[ones])
        k.memset("dve", epsb[:], EPS, W=[epsb])
        k.memset("pool", Va[:, :, :, 64:65], 1.0, W=[Va])
        with ExitStack() as e1:
            wuq = k.sb([128, 3, 384], BF16, es=e1)
            wuqs = k.sb([128, 3, 384], BF16, es=e1)
            wk = k.sb([128, 256], BF16, es=e1)
            wv = k.sb([128, 256], BF16, es=e1)
            qn = k.sb([128, 3], F32, es=e1)
            kvn = k.sb([128, 1], F32, es=e1)
            wst = k.rot(2, [128, 3, 384], F32, es=e1)
            for dst, src in ((wuq, I["wuq"]), (wuqs, I["wuqs"])):
                st = wst.next()
                k.dma("sp", st[:], src.t.rearrange("(kc p) f -> p kc f", p=128), R=[src], W=[st])
                k.copy("pool", dst[:], st[:], R=[st], W=[dst])
            for dst, src in ((wk, I["wk"]), (wv, I["wv"])):
                st = wst.next()
                k.dma("sp", st[:, 0, 0:256], src.t[:, :], R=[src], W=[st])
                k.copy("pool", dst[:], st[:, 0, 0:256], R=[st], W=[dst])
            k.dma("sp", qn[:], I["qn"].t[:, :], R=[I["qn"]], W=[qn])
            k.dma("sp", kvn[:], I["kvn"].t[:, :], R=[I["kvn"]], W=[kvn])
            cqb = k.rot(2, [128, 3, TB], F32, es=e1)
            sq = k.rot(2, [128, 3, TB], BF16, es=e1)
            cqn = k.rot(2, [128, 3, TB], BF16, es=e1)
            ckb = k.rot(2, [128, TB], F32, es=e1)
            ckn = k.rot(2, [128, TB], BF16, es=e1)
            ctab = k.rot(2, [96, TB], F32, es=e1)
            stab = k.rot(2, [96, TB], F32, es=e1)
            krb = k.rot(2, [96, TB], F32, es=e1)
            krsb = k.rot(2, [96, TB], F32, es=e1)
            t1r = k.rot(3, [96, TB], F32, es=e1)
            t2r = k.rot(3, [96, TB], F32, es=e1)
            rt = k.sb([128, TB], F32, es=e1)
            rstd = k.sb([128, TB], F32, es=e1)
            rstd2 = k.sb([128, TB], F32, es=e1)
            psr = Rot(k.banks[1:7])
            cqv = I["cqT"].t.rearrange("(kc p) t -> p kc t", p=128)
            for blk in range(S // TB):
                t0 = blk * TB
                cq, ck, ct, stb, kr, krs = cqb.next(), ckb.next(), ctab.next(), stab.next(), krb.next(), krsb.next()
                k.dma("sp", cq[:], cqv[:, :, t0:t0 + TB], R=[I["cqT"]], W=[cq])
                k.dma("sp", ck[:], I["ckvT"].t[:, t0:t0 + TB], R=[I["ckvT"]], W=[ck])
                k.dma("sp", ct[:], I["c96"].t[:, t0:t0 + TB], R=[I["c96"]], W=[ct])
                k.dma("sp", stb[:], I["s96"].t[:, t0:t0 + TB], R=[I["s96"]], W=[stb])
                if "krp" in I:
                    k.dma("sp", kr[:], I["krp"].t[:, t0:t0 + TB], R=[I["krp"]], W=[kr])
                    k.dma("sp", krs[:], I["krsp"].t[:, t0:t0 + TB], R=[I["krsp"]], W=[krs])
                else:
                    k.dma("sp", kr[64:96, :], I["krT"].t[:, t0:t0 + TB], R=[I["krT"]], W=[kr])
                    k.dma("sp", krs[64:80, :], I["krT"].t[16:32, t0:t0 + TB], R=[I["krT"]], W=[krs])
                    k.dma("sp", krs[80:96, :], I["krT"].t[0:16, t0:t0 + TB], R=[I["krT"]], W=[krs])
                s_ = sq.next()
                k.act(s_[:], cq[:], AF.Square, R=[cq], W=[s_])
                ps0 = k.banks[0]
                for kc in range(3):
                    k.mm(ps0[:], ones[:], s_[:, kc, :], kc == 0, kc == 2, R=[ones, s_], W=[ps0])
                rstd_calc(k, ps0, 128, 384, rt, rstd, epsb)
                cn = cqn.next()
                for kc in range(3):
                    k.stt(cn[:, kc, :], cq[:, kc, :], qn[:, kc:kc + 1], rstd[:], ALU.mult, ALU.mult, R=[cq, qn, rstd], W=[cn])
                for h in range(4):
                    p1, p2 = psr.next(), psr.next()
                    for kc in range(3):
                        k.mm(p1[0:96, :], wuq[:, kc, h * 96:(h + 1) * 96], cn[:, kc, :], kc == 0, kc == 2, R=[wuq, cn], W=[p1])
                    for kc in range(3):
                        k.mm(p2[0:96, :], wuqs[:, kc, h * 96:(h + 1) * 96], cn[:, kc, :], kc == 0, kc == 2, R=[wuqs, cn], W=[p2])
                    t1, t2 = t1r.next(), t2r.next()
                    k.tt("dve", t1[:], p1[0:96, :], ct[:], ALU.mult, R=[p1, ct], W=[t1])
                    k.tt("dve", t2[:], p2[0:96, :], stb[:], ALU.mult, R=[p2, stb], W=[t2])
                    k.tt("pool", QT[:, h, t0:t0 + TB], t1[:], t2[:], ALU.add, R=[t1, t2], W=[QT])
                s_ = sq.next()
                k.act(s_[:, 0, :], ck[:], AF.Square, R=[ck], W=[s_])
                k.mm(ps0[:], ones[:], s_[:, 0, :], True, True, R=[ones, s_], W=[ps0])
                rstd_calc(k, ps0, 128, 128, rt, rstd2, epsb)
                kn = ckn.next()
                k.stt(kn[:], ck[:], kvn[:, 0:1], rstd2[:], ALU.mult, ALU.mult, R=[ck, kvn, rstd2], W=[kn])
                for h in range(4):
                    p1 = psr.next()
                    k.mm(p1[0:64, :], wk[:, h * 64:(h + 1) * 64], kn[:], True, True, R=[wk, kn], W=[p1])
                    k.copy("act", KT[0:64, h, t0:t0 + TB], p1[0:64, :], R=[p1], W=[KT])
                t1, t2 = t1r.next(), t2r.next()
                k.tt("dve", t1[64:96, :], kr[64:96, :], ct[64:96, :], ALU.mult, R=[kr, ct], W=[t1])
                k.tt("pool", t2[64:96, :], krs[64:96, :], stb[64:96, :], ALU.mult, R=[krs, stb], W=[t2])
                for h in range(4):
                    k.tt("pool", KT[64:96, h, t0:t0 + TB], t1[64:96, :], t2[64:96, :], ALU.add, R=[t1, t2], W=[KT])
                for sub in range(4):
                    p1 = psr.next()
                    k.mm(p1[:, 0:256], kn[:, sub * 128:(sub + 1) * 128], wv[:], True, True, R=[kn, wv], W=[p1])
                    k.copy("act", Va[:, :, 4 * blk + sub, 0:64], p1[:, 0:256].rearrange("p (h d) -> p h d", d=64), R=[p1], W=[Va])
            k.barrier()
        with ExitStack() as e2:
            ptr = k.rot(4, [128, 512], BF16, es=e2)
            rden = k.rot(2, [128, 4], F32, es=e2)
            sbanks, abanks = Rot(k.banks[0:4]), Rot(k.banks[4:7])
            for h in range(4):
                for qc in range(nqc):
                    q0 = qc * 512
                    acc = abanks.next()
                    pairs = []
                    for kb in range(4 * qc + 4):
                        pairs.append(dict(kT=KT[:, h, kb * 128:(kb + 1) * 128], nk=128, v=Va[:, h, kb, :], bufs=[KT, Va],
                                          mask=(q0 - kb * 128, -1, 1) if kb >= 4 * qc else None, kpos0=kb * 128))
                    attn_chunk(k, QT[:, h, q0:q0 + 512], [QT], pairs, acc, 65, SC, sbanks, ptr, q0, ahead=2)
                    av = acc_view(acc)
                    rd = rden.next()
                    k.op("dve", lambda hh: hh.reciprocal(out=rd[:], in_=av[:, :, 64]), R=[acc], W=[rd])
                    for sub in range(4):
                        o_ap = ysb[:, 4 * qc + sub, h * 64:(h + 1) * 64]
                        if sub % 2 == 0:
                            k.act(o_ap, av[:, sub, 0:64], AF.Copy, R=[acc, rd], W=[ysb], scale=rd[:, sub:sub + 1])
                        else:
                            k.ts("dve", o_ap, av[:, sub, 0:64], rd[:, sub:sub + 1], None, ALU.mult, None, R=[acc, rd], W=[ysb])
            yv = y.t.rearrange("(kb p) f -> p kb f", p=128)
            for q in range(4):
                n4 = nkb // 4
                k.dma("sp", yv[:, q * n4:(q + 1) * n4, :], ysb[:, q * n4:(q + 1) * n4, :], R=[ysb], W=[y])
            k.barrier()


def rope_tables(S, d, rows_before=0):
    inv = 10000.0 ** (-np.arange(0, d, 2, dtype=np.float32) / d)
    ang = np.arange(S, dtype=np.float32)[:, None] * inv[None, :]
    cos, sin = np.cos(ang).T.astype(np.float32), np.sin(ang).T.astype(np.float32)
    c = np.concatenate([np.ones((rows_before, S), np.float32), cos, cos], 0)
    s = np.concatenate([np.zeros((rows_before, S), np.float32), -sin, sin], 0)
    return np.ascontiguousarray(c), np.ascontiguousarray(s)


def mla_inputs(pm, w_uq, w_ukv, q_norm, kv_norm, hh, S):
    cq, ckv, kr = pm[:, :384], pm[:, 384:512], pm[:, 512:544]
    krp = np.zeros((96, S), np.float32)
    krsp = np.zeros((96, S), np.float32)
    krp[64:96] = kr.T
    krsp[64:80], krsp[80:96] = kr.T[16:32], kr.T[0:16]
    wq = w_uq.reshape(384, 8, 96)[:, 4 * hh:4 * hh + 4]
    wqs = wq.copy()
    wqs[:, :, 64:80], wqs[:, :, 80:96] = wq[:, :, 80:96], wq[:, :, 64:80]
    wkv = w_ukv.reshape(128, 8, 128)[:, 4 * hh:4 * hh + 4]
    c96, s96 = rope_tables(S, 32, 64)
    A = np.ascontiguousarray
    return dict(cqT=A(cq.T), ckvT=A(ckv.T), krp=krp, krsp=krsp, wuq=A(wq.reshape(384, 384)), wuqs=A(wqs.reshape(384, 384)),
                wk=A(wkv[:, :, :64].reshape(128, 256)), wv=A(wkv[:, :, 64:].reshape(128, 256)),
                qn=A(q_norm.reshape(3, 128).T), kvn=A(kv_norm.reshape(128, 1)), c96=c96, s96=s96)


MLA_SHAPES = lambda S: dict(cqT=[384, S], ckvT=[128, S], krp=[96, S], krsp=[96, S], wuq=[384, 384], wuqs=[384, 384],
                            wk=[128, 256], wv=[128, 256], qn=[128, 3], kvn=[128, 1], c96=[96, S], s96=[96, S])


def emit_nsa(k, I, y, S):
    nkb, nqc = S // 128, S // 512
    n_cmp = S // 16 - 1
    ncb = (n_cmp + 127) // 128
    n_sel = S // 64
    SC = 0.125
    BIG = 30000.0
    with ExitStack() as es:
        QT = k.sb([64, 4, S], BF16, es=es)
        KsT = k.sb([64, S], BF16, es=es)
        KwT = k.sb([64, S], BF16, es=es)
        Vsa = k.sb([128, nkb, 65], BF16, es=es)
        Vwa = k.sb([128, nkb, 65], BF16, es=es)
        KcmpT = k.sb([64, ncb * 128], BF16, es=es)
        Vca = k.sb([128, ncb, 65], BF16, es=es)
        Ov = k.sb([128, ncb, n_sel], BF16, es=es)
        selb = k.sb([128, nkb, n_sel], F32, es=es)
        gsig = k.sb([128, nkb, 12], F32, es=es)
        Ebig = k.sb([n_sel, S], BF16, es=es)
        ident = k.sb([128, 128], BF16, es=es)
        k.memset("pool", Vsa[:, :, 64:65], 1.0, W=[Vsa])
        k.memset("pool", Vwa[:, :, 64:65], 1.0, W=[Vwa])
        k.memset("pool", Vca[:], 0.0, W=[Vca])
        k.memset("pool", Vca[:, :, 64:65], 1.0, W=[Vca])
        k.memset("pool", KcmpT[:], 0.0, W=[KcmpT])
        k.memset("pool", ident[:], 1.0, W=[ident])
        k.op("pool", lambda h: h.affine_select(out=ident[:], in_=ident[:], pattern=[[-1, 128]], compare_op=ALU.is_equal,
                                              fill=0.0, base=0, channel_multiplier=1), R=[ident], W=[ident])
        with ExitStack() as e1:
            KcT = k.sb([64, S], BF16, es=e1)
            VcT = k.sb([64, S], BF16, es=e1)
            xb, xsb = k.rot(3, [64, TB], F32, es=e1), k.rot(3, [64, TB], F32, es=e1)
            cb, sb_ = k.rot(2, [64, TB], F32, es=e1), k.rot(2, [64, TB], F32, es=e1)
            t1r, t2r = k.rot(2, [64, TB], F32, es=e1), k.rot(2, [64, TB], F32, es=e1)
            st8 = k.rot(2, [128, nkb // 4, 64], F32, es=e1)
            for blk in range(S // TB):
                t0 = blk * TB
                c_, s_ = cb.next(), sb_.next()
                k.dma("sp", c_[:], I["cos"].t[:, t0:t0 + TB], R=[I["cos"]], W=[c_])
                k.dma("sp", s_[:], I["sin"].t[:, t0:t0 + TB], R=[I["sin"]], W=[s_])
                pre = "qsT" in I
                items = [(I["qT"].t[g], I["qsT"].t[g] if pre else None, QT[:, g, t0:t0 + TB], QT) for g in range(4)]
                items += [(I["kT"].t[i], I["ksT"].t[i] if pre else None, d[:, t0:t0 + TB], d) for i, d in ((0, KcT), (1, KsT), (2, KwT))]
                for src, srcs, dst, dbuf in items:
                    x_, xs_ = xb.next(), xsb.next()
                    k.dma("sp", x_[:], src[:, t0:t0 + TB], R=[I["qT"]], W=[x_])
                    if srcs is not None:
                        k.dma("sp", xs_[:], srcs[:, t0:t0 + TB], R=[I["qT"]], W=[xs_])
                    else:
                        k.dma("sp", xs_[0:32, :], src[32:64, t0:t0 + TB], R=[I["qT"]], W=[xs_])
                        k.dma("sp", xs_[32:64, :], src[0:32, t0:t0 + TB], R=[I["qT"]], W=[xs_])
                    t1, t2 = t1r.next(), t2r.next()
                    k.tt("dve", t1[:], x_[:], c_[:], ALU.mult, R=[x_, c_], W=[t1])
                    k.tt("pool", t2[:], xs_[:], s_[:], ALU.mult, R=[xs_, s_], W=[t2])
                    k.tt("dve", dst, t1[:], t2[:], ALU.add, R=[t1, t2], W=[dbuf])
                x_ = xb.next()
                k.dma("sp", x_[:], I["vcT"].t[:, t0:t0 + TB], R=[I["vcT"]], W=[x_])
                k.copy("pool", VcT[:, t0:t0 + TB], x_[:], R=[x_], W=[VcT])
            for i, dst in ((0, Vsa), (1, Vwa)):
                vv = I["v_tok"].t[i].rearrange("(kb p) d -> p kb d", p=128)
                for q in range(4):
                    st = st8.next()
                    n4 = nkb // 4
                    k.dma("sp", st[:], vv[:, q * n4:(q + 1) * n4, :], R=[I["v_tok"]], W=[st])
                    k.copy("pool", dst[:, q * n4:(q + 1) * n4, 0:64], st[:], R=[st], W=[dst])
            glt = k.sb([128, nkb, 12], F32, es=e1)
            k.dma("sp", glt[:], I["gl"].t.rearrange("(kb p) c -> p kb c", p=128), R=[I["gl"]], W=[glt])
            k.act(gsig[:], glt[:], AF.Sigmoid, R=[glt], W=[gsig])
            k.dma("sp", selb[:], I["selb"].t.rearrange("(kb p) c -> p kb c", p=128), R=[I["selb"]], W=[selb])
            est = k.rot(2, [n_sel, 1024], F32, es=e1)
            for q in range(S // 1024):
                e_ = est.next()
                k.dma("sp", e_[:], I["ebig"].t[:, q * 1024:(q + 1) * 1024], R=[I["ebig"]], W=[e_])
                k.copy("pool", Ebig[:, q * 1024:(q + 1) * 1024], e_[:], R=[e_], W=[Ebig])
            ost = k.sb([128, ncb, n_sel], F32, es=e1)
            k.dma("sp", ost[:], I["ov"].t.rearrange("(c p) j -> p c j", p=128), R=[I["ov"]], W=[ost])
            k.copy("pool", Ov[:], ost[:], R=[ost], W=[Ov])
            w1b = k.sb([64, 32, 256], BF16, es=e1)
            w1s = k.rot(2, [64, 4, 256], F32, es=e1)
            w2s = k.sb([128, 2, 64], F32, es=e1)
            w2b = k.sb([128, 2, 64], BF16, es=e1)
            pss = k.sb([64, 34], F32, es=e1)
            posb = k.sb([64, 34], BF16, es=e1)
            biasb = k.sb([128, 2], F32, es=e1)
            hs = k.rot(2, [128, n_cmp], F32, es=e1)
            tq = k.rot(2, [128, n_cmp], F32, es=e1)
            gg = [k.sb([128, ncb * 128], BF16, es=e1) for _ in range(2)]
            for which, src in ((0, KcT), (1, VcT)):
                for q in range(8):
                    st = w1s.next()
                    k.dma("sp", st[:], I["w1"].t[which, :, 4 * q:4 * q + 4, :], R=[I["w1"]], W=[st])
                    k.copy("pool", w1b[:, 4 * q:4 * q + 4, :], st[:], R=[st], W=[w1b])
                k.dma("sp", w2s[:], I["w2"].t[which].rearrange("(c p) d -> p c d", p=128), R=[I["w2"]], W=[w2s])
                k.copy("pool", w2b[:], w2s[:], R=[w2s], W=[w2b])
                k.memset("dve", pss[:], 0.0, W=[pss])
                k.dma("sp", pss[:, 0:32], I["posT"].t[which], R=[I["posT"]], W=[pss])
                k.copy("dve", posb[:], pss[:], R=[pss], W=[posb])
                for hc in range(2):
                    ph, pb = k.banks[1 + hc], k.banks[3 + hc]
                    for j in range(32):
                        k.mm(ph[:, 0:n_cmp], w1b[:, j, hc * 128:(hc + 1) * 128], src[:, j:j + 16 * (n_cmp - 1) + 1:16],
                             j == 0, j == 31, R=[w1b, src], W=[ph])
                    for j in range(32):
                        k.mm(pb[:, 0:2], w1b[:, j, hc * 128:(hc + 1) * 128], posb[:, j:j + 2], j == 0, j == 31, R=[w1b, posb], W=[pb])
                    k.copy("dve", biasb[:, hc:hc + 1], pb[:, 0:1], R=[pb], W=[biasb])
                    h_, t_ = hs.next(), tq.next()
                    k.act(h_[:], ph[:, 0:n_cmp], AF.Identity, R=[ph, biasb], W=[h_], bias=biasb[:, hc:hc + 1], scale=1.0)
                    k.tt("dve", t_[:], h_[:], h_[:], ALU.mult, R=[h_], W=[t_])
                    k.ts("dve", t_[:], t_[:], 0.044715, 1.0, ALU.mult, ALU.add, R=[t_], W=[t_])
                    k.tt("dve", t_[:], t_[:], h_[:], ALU.mult, R=[t_, h_], W=[t_])
                    k.act(t_[:], t_[:], AF.Sigmoid, R=[t_], W=[t_], scale=1.5957691216)
                    k.memset("pool", gg[hc][:], 0.0, W=[gg[hc]])
                    k.tt("dve", gg[hc][:, 0:n_cmp], h_[:], t_[:], ALU.mult, R=[h_, t_], W=[gg[hc]])
                if which == 0:
                    po = k.banks[5]
                    for hc in range(2):
                        k.mm(po[0:64, 0:n_cmp], w2b[:, hc, :], gg[hc][:, 0:n_cmp], hc == 0, hc == 1, R=[w2b, gg[hc]], W=[po])
                    k.copy("act", KcmpT[:, 0:n_cmp], po[0:64, 0:n_cmp], R=[po], W=[KcmpT])
                else:
                    for nb in range(ncb):
                        nk = min(128, n_cmp - nb * 128)
                        po = k.banks[5 + nb % 2]
                        for hc in range(2):
                            k.mm(po[0:nk, 0:64], gg[hc][:, nb * 128:nb * 128 + nk], w2b[:, hc, :], hc == 0, hc == 1, R=[w2b, gg[hc]], W=[po])
                        k.copy("act", Vca[0:nk, nb, 0:64], po[0:nk, 0:64], R=[po], W=[Vca])
            k.barrier()
        with ExitStack() as e2:
            ysb = k.sb([128, nkb, 256], F32, es=e2)
            imp = k.sb([128, nkb, n_sel], F32, es=e2)
            negT = k.sb([n_sel, S], BF16, es=e2)
            ptr = k.rot(3, [128, 512], BF16, es=e2)
            dn, rden, fc = k.rot(2, [128, 4], F32, es=e2), k.rot(2, [128, 4], F32, es=e2), k.rot(2, [128, 4], F32, es=e2)
            imod, iw = k.rot(2, [128, n_sel], F32, es=e2), k.rot(2, [128, n_sel], F32, es=e2)
            m8a, m8b = k.rot(2, [128, 8], F32, es=e2), k.rot(2, [128, 8], F32, es=e2)
            s01 = k.rot(2, [128, n_sel], F32, es=e2)
            ngm = k.rot(2, [128, n_sel], BF16, es=e2)
            sbanks, abanks, ibanks = Rot(k.banks[0:3]), Rot(k.banks[3:5]), Rot(k.banks[5:7])

            def epilogue(acc, g, qc, br, first_branch):
                av = acc_view(acc)
                d_, rd, f_ = dn.next(), rden.next(), fc.next()
                k.ts("dve", d_[:], av[:, :, 64], 1e-30, None, ALU.max, None, R=[acc], W=[d_])
                k.op("dve", lambda hh: hh.reciprocal(out=rd[:], in_=d_[:]), R=[d_], W=[rd])
                k.tt("dve", f_[:], rd[:], gsig[:, 4 * qc:4 * qc + 4, g * 3 + br], ALU.mult, R=[rd, gsig], W=[f_])
                for sub in range(4):
                    o_ap = ysb[:, 4 * qc + sub, g * 64:(g + 1) * 64]
                    if first_branch:
                        k.act(o_ap, av[:, sub, 0:64], AF.Copy, R=[acc, f_], W=[ysb], scale=f_[:, sub:sub + 1])
                    else:
                        k.stt(o_ap, av[:, sub, 0:64], f_[:, sub:sub + 1], o_ap, ALU.mult, ALU.add, R=[acc, f_, ysb], W=[ysb])
                return rd

            for g in range(4):
                for qc in range(nqc):
                    q0 = qc * 512
                    acc, ia = abanks.next(), ibanks.next()
                    pairs = []
                    for nb in range(ncb):
                        if 16 * 128 * nb + 31 > q0 + 511:
                            continue
                        nk = min(128, n_cmp - nb * 128)
                        pairs.append(dict(kT=KcmpT[:, nb * 128:nb * 128 + nk], nk=nk, v=Vca[0:nk, nb, :], v2=Ov[0:nk, nb, :],
                                          bufs=[KcmpT, Vca, Ov], mask=(q0 - 2048 * nb - 31, -16, 1)))
                    attn_chunk(k, QT[:, g, q0:q0 + 512], [QT], pairs, acc, 65, SC, sbanks, ptr, q0, acc2=ia, W2=n_sel)
                    rd = epilogue(acc, g, qc, 0, True)
                    iv = acc_view(ia)
                    for sub in range(4):
                        i_ap = imp[:, 4 * qc + sub, :]
                        if g == 0:
                            k.ts("dve", i_ap, iv[:, sub, 0:n_sel], rd[:, sub:sub + 1], None, ALU.mult, None, R=[ia, rd], W=[imp])
                        else:
                            k.stt(i_ap, iv[:, sub, 0:n_sel], rd[:, sub:sub + 1], i_ap, ALU.mult, ALU.add, R=[ia, rd, imp], W=[imp])
            for qt in range(nkb):
                im, w_, a8, b8, s_, n_ = imod.next(), iw.next(), m8a.next(), m8b.next(), s01.next(), ngm.next()
                k.tt("dve", im[:], imp[:, qt, :], selb[:, qt, :], ALU.add, R=[imp, selb], W=[im])
                k.op("dve", lambda h: h.max(out=a8[:], in_=im[:]), R=[im], W=[a8])
                k.op("dve", lambda h: h.match_replace(out=w_[:], in_to_replace=a8[:], in_values=im[:], imm_value=-3.0e38),
                     R=[a8, im], W=[w_])
                k.op("dve", lambda h: h.max(out=b8[:], in_=w_[:]), R=[w_], W=[b8])
                k.ts("dve", s_[:], im[:], b8[:, 7:8], None, ALU.is_ge, None, R=[im, b8], W=[s_])
                k.ts("dve", n_[:], s_[:], -1.0, BIG, ALU.add, ALU.mult, R=[s_], W=[n_])
                pb = k.bankb
                k.op("pe", lambda h: h.transpose(out=pb[0:n_sel, 0:128], in_=n_[:], identity=ident[:]), R=[n_, ident], W=[pb])
                k.copy("act", negT[:, qt * 128:(qt + 1) * 128], pb[0:n_sel, 0:128], R=[pb], W=[negT])
            for g in range(4):
                for qc in range(nqc):
                    q0 = qc * 512
                    acc = abanks.next()
                    pairs = []
                    for kb in range(4 * qc + 4):
                        pairs.append(dict(kT=KsT[:, kb * 128:(kb + 1) * 128], nk=128, v=Vsa[:, kb, :], bufs=[KsT, Vsa],
                                          extra=(Ebig[:, kb * 128:(kb + 1) * 128], negT[:, q0:q0 + 512], [Ebig, negT]),
                                          mask=(q0 - kb * 128, -1, 1) if kb >= 4 * qc else None, kpos0=kb * 128))
                    attn_chunk(k, QT[:, g, q0:q0 + 512], [QT], pairs, acc, 65, SC, sbanks, ptr, q0)
                    epilogue(acc, g, qc, 1, False)
            for g in range(4):
                for qc in range(nqc):
                    q0 = qc * 512
                    acc = abanks.next()
                    pairs = []
                    for kb in range(max(0, 4 * qc - 4), 4 * qc + 4):
                        if kb < 4 * qc:
                            i = kb - (4 * qc - 4)
                            pairs.append(dict(kT=KwT[:, kb * 128:(kb + 1) * 128], nk=128, v=Vwa[:, kb, :], bufs=[KwT, Vwa],
                                              mask=(kb * 128 - q0 + 511, 1, -1), skip=set(range(i + 1, 4))))
                        else:
                            pairs.append(dict(kT=KwT[:, kb * 128:(kb + 1) * 128], nk=128, v=Vwa[:, kb, :], bufs=[KwT, Vwa],
                                              mask=(q0 - kb * 128, -1, 1), kpos0=kb * 128))
                    attn_chunk(k, QT[:, g, q0:q0 + 512], [QT], pairs, acc, 65, SC, sbanks, ptr, q0)
                    epilogue(acc, g, qc, 2, False)
            yv = y.t.rearrange("(kb p) f -> p kb f", p=128)
            for q in range(4):
                n4 = nkb // 4
                k.dma("sp", yv[:, q * n4:(q + 1) * n4, :], ysb[:, q * n4:(q + 1) * n4, :], R=[ysb], W=[y])
            k.barrier()


def nsa_consts(S):
    n_cmp, n_sel = S // 16 - 1, S // 64
    ncb = (n_cmp + 127) // 128
    cmp_idx = np.arange(n_cmp)[:, None] * 16 + np.arange(32)[None, :]
    blk = np.arange(n_sel)
    ov = np.zeros((ncb * 128, n_sel), np.float32)
    ov[:n_cmp] = ((cmp_idx[:, :1] < (blk[None, :] + 1) * 64) & (cmp_idx[:, -1:] >= blk[None, :] * 64)).astype(np.float32)
    cur = (np.arange(S) // 64)[:, None]
    valid = blk[None, :] <= cur
    forced = valid & ((blk[None, :] == 0) | (blk[None, :] >= cur - 1))
    selb = np.where(forced, 1e30, np.where(valid, 0.0, -1e30)).astype(np.float32)
    ebig = (np.arange(S)[None, :] // 64 == blk[:, None]).astype(np.float32)
    return ov, selb, np.ascontiguousarray(ebig)


def nsa_inputs(pn, cmp_pos, cmp_w1, cmp_w2, hh, S):
    A = np.ascontiguousarray
    sw = lambda t: np.concatenate([t[..., 32:, :], t[..., :32, :]], axis=-2)
    q = pn[:, hh * 256:(hh + 1) * 256].reshape(S, 4, 64).transpose(1, 2, 0)
    kk = np.stack([pn[:, c + hh * 64:c + hh * 64 + 64].T for c in (512, 768, 1024)])
    vc = pn[:, 640 + hh * 64:640 + hh * 64 + 64].T
    v_tok = np.stack([pn[:, c + hh * 64:c + hh * 64 + 64] for c in (896, 1152)])
    gl = pn[:, 1280 + hh * 12:1280 + hh * 12 + 12]
    cos, sin = rope_tables(S, 64)
    ov, selb, ebig = nsa_consts(S)
    w1 = cmp_w1.reshape(2, 32, 64, 256).transpose(0, 2, 1, 3)
    return dict(qT=A(q), qsT=A(sw(q)), kT=A(kk), ksT=A(sw(kk)), vcT=A(vc), v_tok=A(v_tok), gl=A(gl), cos=cos, sin=sin,
                w1=A(w1), w2=A(cmp_w2), posT=A(cmp_pos.transpose(0, 2, 1)), ov=ov, selb=selb, ebig=ebig)


def NSA_SHAPES(S):
    n_cmp, n_sel = S // 16 - 1, S // 64
    ncb = (n_cmp + 127) // 128
    return dict(qT=[4, 64, S], qsT=[4, 64, S], kT=[3, 64, S], ksT=[3, 64, S], vcT=[64, S], v_tok=[2, S, 64], gl=[S, 12],
                cos=[64, S], sin=[64, S], w1=[2, 64, 32, 256], w2=[2, 256, 64], posT=[2, 64, 32], ov=[ncb * 128, n_sel],
                selb=[S, n_sel], ebig=[n_sel, S])


CH = 64


def tri_mask(k, es, strict, n=8):
    m = k.sb([64, n, 64], BF16, es=es)
    k.memset("pool", m[:], 1.0, W=[m])
    k.op("pool", lambda h: h.affine_select(out=m[:], in_=m[:], pattern=[[0, n], [1, 64]], compare_op=ALU.is_ge, fill=0.0,
                                          base=-1 if strict else 0, channel_multiplier=-1), R=[m], W=[m])
    return m


def chunk_start_mask(k, es, n):
    m = k.sb([64, n], F32, es=es)
    k.memset("pool", m[:], 1.0, W=[m])
    k.memset("pool", m[:, :].rearrange("p (c t) -> p c t", t=CH)[:, :, 0:1], 0.0, W=[m])
    return m


def emit_gla(k, I, y, S):
    NCH = S // CH
    with ExitStack() as es:
        ident = k.sb([128, 128], BF16, es=es)
        k.memset("pool", ident[:], 1.0, W=[ident])
        k.op("pool", lambda h: h.affine_select(out=ident[:], in_=ident[:], pattern=[[-1, 128]], compare_op=ALU.is_equal,
                                              fill=0.0, base=0, channel_multiplier=1), R=[ident], W=[ident])
        tri = tri_mask(k, es, False)
        cmask = chunk_start_mask(k, es, TB)
        Vb = k.sb([64, NCH, 256], BF16, es=es)
        aw2 = k.sb([16, 128], F32, es=es)
        ab = k.sb([64, 2], F32, es=es)
        nab = k.sb([64, 2], F32, es=es)
        ng = k.sb([64, 256], F32, es=es)
        epsb = k.sb([64, 1], F32, es=es)
        k.memset("dve", epsb[:], EPS, W=[epsb])
        k.dma("sp", aw2[:], I["aw2"].t[:, :], R=[I["aw2"]], W=[aw2])
        k.dma("sp", ab[:], I["ab"].t[:, :], R=[I["ab"]], W=[ab])
        k.dma("sp", ng[:], I["ng"].t[:, :], R=[I["ng"]], W=[ng])
        k.ts("dve", nab[:], ab[:], -1.0, None, ALU.mult, None, R=[ab], W=[nab])
        vst = k.rot(2, [64, 8, 256], F32, es=es)
        for c8 in range(NCH // 8):
            st = vst.next()
            k.dma("sp", st[:], I["v_ch"].t[:, 8 * c8:8 * c8 + 8, :], R=[I["v_ch"]], W=[st])
            k.copy("pool", Vb[:, 8 * c8:8 * c8 + 8, :], st[:], R=[st], W=[Vb])
        for h in range(2):
            with ExitStack() as e1:
                Qb = k.sb([64, S], BF16, es=e1)
                Kt = k.sb([64, S], BF16, es=e1)
                Kh = k.sb([64, NCH, 64], BF16, es=e1)
                MT = k.sb([64, NCH, 64], BF16, es=e1)
                gC = k.sb([64, NCH], F32, es=e1)
                otok = k.sb([64, NCH, 128], F32, es=e1)
                H = k.sb([64, 128], F32, es=e1)
                Hb = k.sb([64, 128], BF16, es=e1)
                alo = k.rot(2, [16, TB], F32, es=e1)
                qb, kb_ = k.rot(2, [64, TB], F32, es=e1), k.rot(2, [64, TB], F32, es=e1)
                t_e, t_l, t_b = k.rot(2, [64, TB], F32, es=e1), k.rot(2, [64, TB], F32, es=e1), k.rot(2, [64, TB], F32, es=e1)
                t_eb, t_enb, t_k = k.rot(2, [64, TB], F32, es=e1), k.rot(2, [64, TB], F32, es=e1), k.rot(2, [64, TB], F32, es=e1)
                t_kh = k.rot(2, [64, TB], BF16, es=e1)
                pz = Rot(k.banks[0:2])
                for blk in range(S // TB):
                    t0 = blk * TB
                    a_, q_, k_ = alo.next(), qb.next(), kb_.next()
                    k.dma("sp", a_[:], I["aloT"].t[:, t0:t0 + TB], R=[I["aloT"]], W=[a_])
                    k.dma("sp", q_[:], I["qT"].t[h, :, t0:t0 + TB], R=[I["qT"]], W=[q_])
                    k.dma("sp", k_[:], I["kT"].t[h, :, t0:t0 + TB], R=[I["kT"]], W=[k_])
                    p_ = pz.next()
                    k.mm(p_[0:64, :], aw2[:, h * 64:(h + 1) * 64], a_[:], True, True, R=[aw2, a_], W=[p_])
                    e_, l_, b_ = t_e.next(), t_l.next(), t_b.next()
                    k.act(e_[:], p_[0:64, :], AF.Exp, R=[p_, nab], W=[e_], bias=nab[:, h:h + 1], scale=-1.0)
                    k.act(l_[:], e_[:], AF.Ln, R=[e_], W=[l_], bias=1.0, scale=1.0)
                    k.op("dve", lambda hh: hh.tensor_tensor_scan(out=b_[:], data0=cmask[:], data1=l_[:], initial=0.0,
                                                                 op0=ALU.mult, op1=ALU.add), R=[cmask, l_], W=[b_])
                    eb, enb, kt32 = t_eb.next(), t_enb.next(), t_k.next()
                    k.act(eb[:], b_[:], AF.Exp, R=[b_], W=[eb], scale=-1.0 / 16)
                    k.act(enb[:], b_[:], AF.Exp, R=[b_], W=[enb], scale=1.0 / 16)
                    k.stt(Qb[:, t0:t0 + TB], q_[:], 0.125, eb[:], ALU.mult, ALU.mult, R=[q_, eb], W=[Qb])
                    k.tt("dve", kt32[:], k_[:], enb[:], ALU.mult, R=[k_, enb], W=[kt32])
                    k.copy("pool", Kt[:, t0:t0 + TB], kt32[:], R=[kt32], W=[Kt])
                    ebv = eb[:, :].rearrange("p (c t) -> p c t", t=CH)
                    k.copy("dve", gC[:, 8 * blk:8 * blk + 8], ebv[:, :, CH - 1], R=[eb], W=[gC])
                    kh = t_kh.next()
                    k.tt("dve", kh[:, :].rearrange("p (c t) -> p c t", t=CH), kt32[:, :].rearrange("p (c t) -> p c t", t=CH),
                         ebv[:, :, CH - 1:CH].to_broadcast([64, 8, CH]), ALU.mult, R=[kt32, eb], W=[kh])
                    pb = k.bankb
                    for c in range(8):
                        k.op("pe", lambda hh: hh.transpose(out=pb[0:64, c * 64:(c + 1) * 64], in_=kh[:, c * 64:(c + 1) * 64],
                                                           identity=ident[0:64, 0:64]), R=[kh, ident], W=[pb])
                    k.copy("act", Kh[:, 8 * blk:8 * blk + 8, :], pb[0:64, 0:512].rearrange("p (c t) -> p c t", t=64), R=[pb], W=[Kh])
                pm = Rot(k.banks[2:4])
                for c8 in range(NCH // 8):
                    p_ = pm.next()
                    for c in range(8):
                        cc = 8 * c8 + c
                        k.mm(p_[0:64, c * 64:(c + 1) * 64], Kt[:, cc * 64:(cc + 1) * 64], Qb[:, cc * 64:(cc + 1) * 64], True, True,
                             R=[Kt, Qb], W=[p_])
                    k.tt("dve", MT[:, 8 * c8:8 * c8 + 8, :], p_[0:64, :].rearrange("p (c t) -> p c t", t=64), tri[:], ALU.mult,
                         R=[p_, tri], W=[MT])
                po_r, ph_r = Rot(k.banks[4:6]), Rot([k.banks[6], k.banks[0]])
                k.memset("dve", H[:], 0.0, W=[H])
                for c in range(NCH):
                    po, ph = po_r.next(), ph_r.next()
                    vch = Vb[:, c, h * 128:(h + 1) * 128]
                    k.mm(po[0:64, 0:128], MT[:, c, :], vch, True, c == 0, R=[MT, Vb], W=[po])
                    if c > 0:
                        k.mm(po[0:64, 0:128], Qb[:, c * 64:(c + 1) * 64], Hb[:], False, True, R=[Qb, Hb], W=[po])
                    k.copy("act", otok[:, c, :], po[0:64, 0:128], R=[po], W=[otok])
                    if c < NCH - 1:
                        k.mm(ph[0:64, 0:128], Kh[:, c, :], vch, True, True, R=[Kh, Vb], W=[ph])
                        k.stt(H[:], H[:], gC[:, c:c + 1], ph[0:64, 0:128], ALU.mult, ALU.add, R=[H, gC, ph], W=[H])
                        k.copy("dve", Hb[:], H[:], R=[H], W=[Hb])
                sqp = k.rot(2, [64, 16, 128], F32, es=e1)
                rst = k.rot(2, [64, 16, 128], F32, es=e1)
                ms, rs_ = k.rot(2, [64, 16], F32, es=e1), k.rot(2, [64, 16], F32, es=e1)
                yv = y.t.rearrange("(c p) f -> p c f", p=CH)
                for c16 in range(NCH // 16):
                    cs = slice(16 * c16, 16 * c16 + 16)
                    sq, rr, m_, r_ = sqp.next(), rst.next(), ms.next(), rs_.next()
                    k.dma("sp", rr[:], I["r_ch"].t[:, cs, h * 128:(h + 1) * 128], R=[I["r_ch"]], W=[rr])
                    k.tt("dve", sq[:], otok[:, cs, :], otok[:, cs, :], ALU.mult, R=[otok], W=[sq])
                    k.op("dve", lambda hh: hh.reduce_sum(out=m_[:], in_=sq[:], axis=AX.X), R=[sq], W=[m_])
                    k.act(m_[:], m_[:], AF.Sqrt, R=[m_, epsb], W=[m_], bias=epsb[:], scale=1.0 / 128)
                    k.op("dve", lambda hh: hh.reciprocal(out=r_[:], in_=m_[:]), R=[m_], W=[r_])
                    k.tt("dve", sq[:], otok[:, cs, :], r_[:, :].unsqueeze(2).to_broadcast([64, 16, 128]), ALU.mult, R=[otok, r_], W=[sq])
                    k.tt("pool", sq[:], sq[:], ng[:, h * 128:(h + 1) * 128].unsqueeze(1).to_broadcast([64, 16, 128]), ALU.mult,
                         R=[sq, ng], W=[sq])
                    k.act(rr[:], rr[:], AF.Silu, R=[rr], W=[rr])
                    k.tt("dve", sq[:], sq[:], rr[:], ALU.mult, R=[sq, rr], W=[sq])
                    k.dma("pool", yv[:, cs, h * 128:(h + 1) * 128], sq[:], R=[sq], W=[y])
                k.barrier()


def gla_inputs(pg, alpha_w2, alpha_b, norm_g, hh, S):
    A = np.ascontiguousarray
    q = pg[:, hh * 128:(hh + 1) * 128].reshape(S, 2, 64).transpose(1, 2, 0)
    kk = pg[:, 256 + hh * 128:256 + (hh + 1) * 128].reshape(S, 2, 64).transpose(1, 2, 0)
    v = pg[:, 512 + hh * 256:512 + (hh + 1) * 256].reshape(S // CH, CH, 256).transpose(1, 0, 2)
    r = pg[:, 1040 + hh * 256:1040 + (hh + 1) * 256].reshape(S // CH, CH, 256).transpose(1, 0, 2)
    alo = pg[:, 1024:1040].T
    return dict(qT=A(q), kT=A(kk), v_ch=A(v), r_ch=A(r), aloT=A(alo), aw2=A(alpha_w2[:, hh * 128:(hh + 1) * 128]),
                ab=A(alpha_b[hh * 128:(hh + 1) * 128].reshape(2, 64).T), ng=A(np.broadcast_to(norm_g[hh * 256:(hh + 1) * 256], (64, 256))))


GLA_SHAPES = lambda S: dict(qT=[2, 64, S], kT=[2, 64, S], v_ch=[64, S // CH, 256], r_ch=[64, S // CH, 256], aloT=[16, S],
                            aw2=[16, 128], ab=[64, 2], ng=[64, 256])


class _Stop(Exception):
    pass


def emit_rwkv(k, I, y, S, dbg=99):
    try:
        _emit_rwkv(k, I, y, S, dbg)
    except _Stop:
        k.barrier()


def _emit_rwkv(k, I, y, S, dbg):
    def ck(n):
        if dbg <= n:
            raise _Stop()
    NCH = S // CH
    C0 = 0.6065306597126334
    with ExitStack() as es:
        identb = k.sb([128, 128], BF16, es=es)
        identf = k.sb([64, 64], F32, es=es)
        for idt in (identb, identf):
            n = idt.t.shape[0]
            k.memset("pool", idt[:], 1.0, W=[idt])
            k.op("pool", lambda h: h.affine_select(out=idt[:], in_=idt[:], pattern=[[-1, n]], compare_op=ALU.is_equal,
                                                  fill=0.0, base=0, channel_multiplier=1), R=[idt], W=[idt])
        tri_i = tri_mask(k, es, False)
        tri_s = tri_mask(k, es, True)
        tri_l = k.sb([64, 8, 64], BF16, es=es)
        k.memset("pool", tri_l[:], 1.0, W=[tri_l])
        k.op("pool", lambda h: h.affine_select(out=tri_l[:], in_=tri_l[:], pattern=[[0, 8], [-1, 64]], compare_op=ALU.is_ge,
                                              fill=0.0, base=-1, channel_multiplier=1), R=[tri_l], W=[tri_l])
        cmask = chunk_start_mask(k, es, TB)
        ones64 = k.sb([64, 64], BF16, es=es)
        k.memset("dve", ones64[:], 1.0, W=[ones64])
        gnb = k.sb([64, 1], F32, es=es)
        k.memset("dve", gnb[:], 64e-5, W=[gnb])
        P = {}
        for nm, shp in (("mu_r", [64, 4]), ("mu_k", [64, 4]), ("mu_v", [64, 4]), ("mu_w", [64, 1]), ("mu_a", [64, 1]), ("mu_g", [128, 1]),
                        ("w0", [64, 4]), ("a0", [64, 4]), ("k_k", [64, 4]), ("k_a", [64, 4]), ("r_k", [64, 4]),
                        ("ww2", [64, 256]), ("aw2", [64, 256]), ("gw2", [128, 256]), ("ln_w", [64, 256]), ("ln_b", [64, 256])):
            P[nm] = k.sb(shp, F32, es=es)
            k.dma("sp", P[nm][:], I[nm].t[:, :], R=[I[nm]], W=[P[nm]])
        omka = k.sb([64, 4], F32, es=es)
        k.ts("dve", omka[:], P["k_a"][:], -1.0, 1.0, ALU.mult, ALU.add, R=[P["k_a"]], W=[omka])
        gw2b = k.sb([128, 256], BF16, es=es)
        k.copy("dve", gw2b[:], P["gw2"][:], R=[P["gw2"]], W=[gw2b])
        Hs = [k.sb([64, 64], F32, es=es) for _ in range(4)]
        Hbs = [k.sb([64, 64], BF16, es=es) for _ in range(4)]
        for h in range(4):
            k.memset("dve", Hs[h][:], 0.0, W=[Hs[h]])

        def f32r(n, p=64, w=TB):
            return k.rot(n, [p, w], F32, es=es)

        ldw, lda, ldg = k.rot(2, [64, TB + 1], F32, es=es), k.rot(2, [64, TB + 1], F32, es=es), k.rot(2, [128, TB + 1], F32, es=es)
        ldr, ldk, ldv = k.rot(2, [64, TB + 1], F32, es=es), k.rot(2, [64, TB + 1], F32, es=es), k.rot(2, [64, TB + 1], F32, es=es)
        dtmp = k.rot(2, [128, TB], F32, es=es)
        tw_r, as_r = f32r(2), f32r(2)
        sgl_r = k.rot(2, [128, TB], BF16, es=es)
        rs_r, ks_r, vs_r = f32r(2), f32r(2), f32r(2)
        sgw_r, a_r, b_r, eb_r, enb_r, eb1_r = f32r(1), f32r(2), f32r(1), f32r(2), f32r(1), f32r(1)
        kkr_r, nrm_r, kk_r, kp_r, be_r, kt32_r, bt32_r = f32r(1), f32r(1), f32r(2), f32r(2), f32r(1), f32r(1), f32r(1)
        sqk_r = k.rot(2, [64, TB], BF16, es=es)

        def b16r(n):
            return k.rot(n, [64, TB], BF16, es=es)

        Rb_r, Ab_r, Kt_r, Bt_r, khf_r, bhf_r, rkx_r = b16r(4), b16r(4), b16r(2), b16r(2), b16r(2), b16r(2), b16r(2)

        def m16r(n):
            return k.rot(n, [64, 8, 64], BF16, es=es)

        Kh_r, Bh_r, Vc_r = m16r(4), m16r(4), m16r(4)
        vtok_r = k.rot(4, [64, 8, 64], F32, es=es)
        X_r, XT_r, P_r, PT_r = m16r(2), m16r(2), m16r(3), m16r(3)
        Mak_r, Mrk_r, Mrb_r, Ri_r, R16_r = m16r(4), m16r(4), m16r(4), m16r(4), m16r(2)
        Rf_r = k.rot(2, [64, 8, 64], F32, es=es)
        gC_r = k.rot(4, [64, 8], F32, es=es)
        rk_r = k.rot(4, [64, 8], F32, es=es)
        Zb_r, Un_r = k.rot(8, [64, 64], BF16, es=es), k.rot(8, [64, 64], BF16, es=es)
        ytok_r = k.rot(4, [64, 8, 64], F32, es=es)
        gt_r, po1_r, po2_r = (k.rot(2, [64, 8, 64], F32, es=es) for _ in range(3))
        st1_r, st2_r = k.rot(2, [64, 8], F32, es=es), k.rot(2, [64, 8], F32, es=es)
        bk = Rot(k.banks[0:3])
        yv = y.t.rearrange("(c p) f -> p c f", p=CH)

        def v3(ap):
            return ap.rearrange("p (c t) -> p c t", t=CH)

        def shifted(ld, src_ap, srcbuf, mu_ap, mubuf, out, np_, t0):
            t = ld.next()
            if t0 == 0:
                k.memset("pool", t[0:np_, 0:1], 0.0, W=[t])
                k.dma("sp", t[0:np_, 1:TB + 1], src_ap[:, 0:TB], R=[srcbuf], W=[t])
            else:
                k.dma("sp", t[0:np_, :], src_ap[:, t0 - 1:t0 + TB], R=[srcbuf], W=[t])
            d = dtmp.next()
            k.tt("pool", d[0:np_, :], t[0:np_, 0:TB], t[0:np_, 1:TB + 1], ALU.subtract, R=[t], W=[d])
            k.stt(out[0:np_, :], d[0:np_, :], mu_ap, t[0:np_, 1:TB + 1], ALU.mult, ALU.add, R=[d, mubuf, t], W=[out])

        for blk in range(S // TB):
            t0 = blk * TB
            tw, als, sgl = tw_r.next(), as_r.next(), sgl_r.next()
            shifted(ldw, I["wloT"].t, I["wloT"], P["mu_w"][:, 0:1], P["mu_w"], tw, 64, t0)
            k.act(tw[:], tw[:], AF.Tanh, R=[tw], W=[tw])
            shifted(lda, I["aloT"].t, I["aloT"], P["mu_a"][:, 0:1], P["mu_a"], als, 64, t0)
            gl = dtmp.next()
            shifted(ldg, I["gloT"].t, I["gloT"], P["mu_g"][:, 0:1], P["mu_g"], gl, 128, t0)
            k.act(sgl[:], gl[:], AF.Sigmoid, R=[gl], W=[sgl])
            ck(1)
            HS = []
            for h in range(4):
                hc = slice(h, h + 1)
                rs, ks, vs = rs_r.next(), ks_r.next(), vs_r.next()
                shifted(ldr, I["rT"].t[h], I["rT"], P["mu_r"][:, hc], P["mu_r"], rs, 64, t0)
                shifted(ldk, I["kT"].t[h], I["kT"], P["mu_k"][:, hc], P["mu_k"], ks, 64, t0)
                shifted(ldv, I["vT"].t[h], I["vT"], P["mu_v"][:, hc], P["mu_v"], vs, 64, t0)
                pw, pa = bk.next(), bk.next()
                k.mm(pw[0:64, :], P["ww2"][:, h * 64:(h + 1) * 64], tw[:], True, True, R=[P["ww2"], tw], W=[pw])
                k.mm(pa[0:64, :], P["aw2"][:, h * 64:(h + 1) * 64], als[:], True, True, R=[P["aw2"], als], W=[pa])
                sgw, a_, b_, eb, enb, eb1 = sgw_r.next(), a_r.next(), b_r.next(), eb_r.next(), enb_r.next(), eb1_r.next()
                k.act(sgw[:], pw[0:64, :], AF.Sigmoid, R=[pw, P["w0"]], W=[sgw], bias=P["w0"][:, hc], scale=1.0)
                k.act(a_[:], pa[0:64, :], AF.Sigmoid, R=[pa, P["a0"]], W=[a_], bias=P["a0"][:, hc], scale=1.0)
                k.op("dve", lambda hh: hh.tensor_tensor_scan(out=b_[:], data0=cmask[:], data1=sgw[:], initial=0.0,
                                                             op0=ALU.mult, op1=ALU.add), R=[cmask, sgw], W=[b_])
                k.act(eb[:], b_[:], AF.Exp, R=[b_], W=[eb], scale=-C0)
                k.act(enb[:], b_[:], AF.Exp, R=[b_], W=[enb], scale=C0)
                k.tt("pool", eb1[:], b_[:], sgw[:], ALU.subtract, R=[b_, sgw], W=[eb1])
                k.act(eb1[:], eb1[:], AF.Exp, R=[eb1], W=[eb1], scale=-C0)
                kkr, sqk, nrm, kk, kp, be = kkr_r.next(), sqk_r.next(), nrm_r.next(), kk_r.next(), kp_r.next(), be_r.next()
                k.ts("dve", kkr[:], ks[:], P["k_k"][:, hc], None, ALU.mult, None, R=[ks, P["k_k"]], W=[kkr])
                k.act(sqk[:], kkr[:], AF.Square, R=[kkr], W=[sqk])
                pn = bk.next()
                k.mm(pn[0:64, :], ones64[:], sqk[:], True, True, R=[ones64, sqk], W=[pn])
                k.act(nrm[:], pn[0:64, :], AF.Sqrt, R=[pn], W=[nrm])
                k.ts("dve", nrm[:], nrm[:], 1e-12, None, ALU.max, None, R=[nrm], W=[nrm])
                k.op("dve", lambda hh: hh.reciprocal(out=nrm[:], in_=nrm[:]), R=[nrm], W=[nrm])
                k.tt("pool", kk[:], kkr[:], nrm[:], ALU.mult, R=[kkr, nrm], W=[kk])
                k.ts("dve", kp[:], a_[:], P["k_a"][:, hc], omka[:, hc], ALU.mult, ALU.add, R=[a_, P["k_a"], omka], W=[kp])
                k.tt("pool", kp[:], kp[:], ks[:], ALU.mult, R=[kp, ks], W=[kp])
                k.tt("pool", be[:], kk[:], a_[:], ALU.mult, R=[kk, a_], W=[be])
                Rb, Ab, Kt, Bt, kt32, bt32 = Rb_r.next(), Ab_r.next(), Kt_r.next(), Bt_r.next(), kt32_r.next(), bt32_r.next()
                k.tt("dve", Rb[:], rs[:], eb[:], ALU.mult, R=[rs, eb], W=[Rb])
                k.tt("pool", Ab[:], kk[:], eb1[:], ALU.mult, R=[kk, eb1], W=[Ab])
                k.tt("dve", kt32[:], kp[:], enb[:], ALU.mult, R=[kp, enb], W=[kt32])
                k.tt("pool", bt32[:], be[:], enb[:], ALU.mult, R=[be, enb], W=[bt32])
                k.copy("act", Kt[:], kt32[:], R=[kt32], W=[Kt])
                k.copy("act", Bt[:], bt32[:], R=[bt32], W=[Bt])
                gC = gC_r.next()
                ebv = v3(eb[:, :])
                k.copy("dve", gC[:], ebv[:, :, CH - 1], R=[eb], W=[gC])
                gbc = ebv[:, :, CH - 1:CH].to_broadcast([64, 8, CH])
                khf, bhf, rkx = khf_r.next(), bhf_r.next(), rkx_r.next()
                k.tt("dve", v3(khf[:, :]), v3(kt32[:, :]), gbc, ALU.mult, R=[kt32, eb], W=[khf])
                k.tt("pool", v3(bhf[:, :]), v3(bt32[:, :]), gbc, ALU.mult, R=[bt32, eb], W=[bhf])
                k.stt(rkx[:], rs[:], P["r_k"][:, hc], kp[:], ALU.mult, ALU.mult, R=[rs, P["r_k"], kp], W=[rkx])
                ck(2)
                Kh, Bh, Vc, vtok = Kh_r.next(), Bh_r.next(), Vc_r.next(), vtok_r.next()
                pbb = k.bankb
                for src_, dst_, eng_ in ((khf, Kh, "act"), (bhf, Bh, "dve")):
                    for c in range(8):
                        k.op("pe", lambda hh: hh.transpose(out=pbb[0:64, c * 64:(c + 1) * 64], in_=src_[:, c * 64:(c + 1) * 64],
                                                           identity=identb[0:64, 0:64]), R=[src_, identb], W=[pbb])
                    k.copy(eng_, dst_[:], v3(pbb[0:64, 0:512]), R=[pbb], W=[dst_])
                ck(3)
                pv = bk.next()
                for c in range(8):
                    k.mm(pv[0:64, c * 64:(c + 1) * 64], vs[:, c * 64:(c + 1) * 64], identf[:, :], True, True, R=[vs, identf], W=[pv])
                k.copy("act", vtok[:], v3(pv[0:64, :]), R=[pv], W=[vtok])
                k.copy("dve", Vc[:], v3(pv[0:64, :]), R=[pv], W=[Vc])
                ck(4)
                prk = bk.next()
                for c in range(8):
                    k.mm(prk[0:64, 2 * c:2 * c + 2], rkx[:, c * 64:(c + 1) * 64], ones64[:, 0:2], True, True, R=[rkx, ones64], W=[prk])
                rk = rk_r.next()
                k.copy("dve", rk[:], prk[0:64, 0:16].rearrange("p (c two) -> p c two", two=2)[:, :, 0], R=[prk], W=[rk])
                ck(5)
                X, XT, Mak, Mrk, Mrb = X_r.next(), XT_r.next(), Mak_r.next(), Mrk_r.next(), Mrb_r.next()
                for (lh, rh, dst, msk) in ((Bt, Ab, X, tri_s), (Ab, Bt, XT, tri_l), (Kt, Ab, Mak, tri_s), (Kt, Rb, Mrk, tri_i), (Bt, Rb, Mrb, tri_i)):
                    pm = bk.next()
                    for c in range(8):
                        k.mm(pm[0:64, c * 64:(c + 1) * 64], lh[:, c * 64:(c + 1) * 64], rh[:, c * 64:(c + 1) * 64], True, True,
                             R=[lh, rh], W=[pm])
                    k.tt("dve", dst[:], v3(pm[0:64, :]), msk[:], ALU.mult, R=[pm, msk], W=[dst])
                ck(6)
                Rf, R16 = Rf_r.next(), R16_r.next()
                idb = identb[0:64, 0:64].unsqueeze(1).to_broadcast([64, 8, 64])
                k.tt("dve", Rf[:], idb, X[:], ALU.subtract, R=[identb, X], W=[Rf])
                k.copy("pool", R16[:], Rf[:], R=[Rf], W=[R16])
                Pc, PTc = X, XT
                for it in range(5):
                    Pn, PTn = P_r.next(), PT_r.next()
                    p1, p2 = bk.next(), bk.next()
                    for c in range(8):
                        k.mm(p1[0:64, c * 64:(c + 1) * 64], PTc[:, c, :], Pc[:, c, :], True, True, R=[PTc, Pc], W=[p1])
                    k.copy("act", Pn[:], v3(p1[0:64, :]), R=[p1], W=[Pn])
                    for c in range(8):
                        k.mm(p2[0:64, c * 64:(c + 1) * 64], Pc[:, c, :], PTc[:, c, :], True, True, R=[PTc, Pc], W=[p2])
                    k.copy("act", PTn[:], v3(p2[0:64, :]), R=[p2], W=[PTn])
                    p3 = bk.next()
                    for c in range(8):
                        k.mm(p3[0:64, c * 64:(c + 1) * 64], PTn[:, c, :], R16[:, c, :], True, True, R=[PTn, R16], W=[p3])
                    k.tt("dve", Rf[:], Rf[:], v3(p3[0:64, :]), ALU.add, R=[Rf, p3], W=[Rf])
                    R16 = R16_r.next() if it < 4 else Ri_r.next()
                    k.copy("pool", R16[:], Rf[:], R=[Rf], W=[R16])
                    Pc, PTc = Pn, PTn
                Ri = R16
                ck(7)
                ytok = ytok_r.next()
                HS.append(dict(h=h, Mak=Mak, Mrk=Mrk, Mrb=Mrb, Ri=Ri, Ab=Ab, Rb=Rb, Kh=Kh, Bh=Bh, Vc=Vc, gC=gC, vtok=vtok, rk=rk, ytok=ytok))
            for c in range(8):
                first = (blk == 0 and c == 0)
                cs = slice(c * 64, (c + 1) * 64)
                Zs, Us = [Zb_r.next() for _ in HS], [Un_r.next() for _ in HS]
                for i, T_ in enumerate(HS):
                    pb_ = k.banks[3 + i]
                    k.mm(pb_[0:64, 0:64], T_["Mak"][:, c, :], T_["Vc"][:, c, :], True, first, R=[T_["Mak"], T_["Vc"]], W=[pb_])
                    if not first:
                        k.mm(pb_[0:64, 0:64], T_["Ab"][:, cs], Hbs[T_["h"]][:], False, True, R=[T_["Ab"], Hbs[T_["h"]]], W=[pb_])
                for i, T_ in enumerate(HS):
                    pb_ = k.banks[3 + i]
                    k.copy("act", Zs[i][:], pb_[0:64, 0:64], R=[pb_], W=[Zs[i]])
                for i, T_ in enumerate(HS):
                    pb_ = k.banks[3 + i]
                    k.mm(pb_[0:64, 64:128], T_["Ri"][:, c, :], Zs[i][:], True, True, R=[T_["Ri"], Zs[i]], W=[pb_])
                for i, T_ in enumerate(HS):
                    pb_ = k.banks[3 + i]
                    k.ts("dve", Us[i][:], pb_[0:64, 64:128], -1.0, None, ALU.mult, None, R=[pb_], W=[Us[i]])
                for i, T_ in enumerate(HS):
                    pb_ = k.banks[3 + i]
                    Hb = Hbs[T_["h"]]
                    k.mm(pb_[0:64, 128:192], T_["Mrk"][:, c, :], T_["Vc"][:, c, :], True, False, R=[T_["Mrk"], T_["Vc"]], W=[pb_])
                    if not first:
                        k.mm(pb_[0:64, 128:192], T_["Rb"][:, cs], Hb[:], False, False, R=[T_["Rb"], Hb], W=[pb_])
                    k.mm(pb_[0:64, 128:192], T_["Mrb"][:, c, :], Us[i][:], False, True, R=[T_["Mrb"], Us[i]], W=[pb_])
                    k.mm(pb_[0:64, 192:256], T_["Kh"][:, c, :], T_["Vc"][:, c, :], True, False, R=[T_["Kh"], T_["Vc"]], W=[pb_])
                    k.mm(pb_[0:64, 192:256], T_["Bh"][:, c, :], Us[i][:], False, True, R=[T_["Bh"], Us[i]], W=[pb_])
                for i, T_ in enumerate(HS):
                    pb_ = k.banks[3 + i]
                    H, Hb = Hs[T_["h"]], Hbs[T_["h"]]
                    k.stt(H[:], H[:], T_["gC"][:, c:c + 1], pb_[0:64, 192:256], ALU.mult, ALU.add, R=[H, T_["gC"], pb_], W=[H])
                    k.copy("dve", Hb[:], H[:], R=[H], W=[Hb])
                    k.copy("act", T_["ytok"][:, c, :], pb_[0:64, 128:192], R=[pb_], W=[T_["ytok"]])
            for T_ in HS:
                h, vtok, rk, ytok = T_["h"], T_["vtok"], T_["rk"], T_["ytok"]
                pg = bk.next()
                for c in range(8):
                    k.mm(pg[0:64, c * 64:(c + 1) * 64], sgl[:, c * 64:(c + 1) * 64], gw2b[:, h * 64:(h + 1) * 64], True, True,
                         R=[sgl, gw2b], W=[pg])
                gt = gt_r.next()
                k.copy("act", gt[:], v3(pg[0:64, :]), R=[pg], W=[gt])
                s1, s2, o1, o2 = st1_r.next(), st2_r.next(), po1_r.next(), po2_r.next()
                k.op("dve", lambda hh: hh.reduce_sum(out=s1[:], in_=ytok[:], axis=AX.X), R=[ytok], W=[s1])
                k.ts("dve", s1[:], s1[:], 1.0 / 64, None, ALU.mult, None, R=[s1], W=[s1])
                k.tt("pool", o1[:], ytok[:], s1[:, :].unsqueeze(2).to_broadcast([64, 8, 64]), ALU.subtract, R=[ytok, s1], W=[o1])
                k.tt("pool", o2[:], o1[:], o1[:], ALU.mult, R=[o1], W=[o2])
                k.op("dve", lambda hh: hh.reduce_sum(out=s2[:], in_=o2[:], axis=AX.X), R=[o2], W=[s2])
                k.act(s2[:], s2[:], AF.Sqrt, R=[s2, gnb], W=[s2], bias=gnb[:], scale=1.0 / 64)
                k.op("dve", lambda hh: hh.reciprocal(out=s2[:], in_=s2[:]), R=[s2], W=[s2])
                k.tt("dve", o1[:], o1[:], s2[:, :].unsqueeze(2).to_broadcast([64, 8, 64]), ALU.mult, R=[o1, s2], W=[o1])
                k.tt("pool", o1[:], o1[:], P["ln_w"][:, h * 64:(h + 1) * 64].unsqueeze(1).to_broadcast([64, 8, 64]), ALU.mult,
                     R=[o1, P["ln_w"]], W=[o1])
                k.tt("pool", o1[:], o1[:], P["ln_b"][:, h * 64:(h + 1) * 64].unsqueeze(1).to_broadcast([64, 8, 64]), ALU.add,
                     R=[o1, P["ln_b"]], W=[o1])
                k.tt("dve", o2[:], vtok[:], rk[:, :].unsqueeze(2).to_broadcast([64, 8, 64]), ALU.mult, R=[vtok, rk], W=[o2])
                k.tt("pool", o1[:], o1[:], o2[:], ALU.add, R=[o1, o2], W=[o1])
                k.tt("dve", o1[:], o1[:], gt[:], ALU.mult, R=[o1, gt], W=[o1])
                k.dma("pool", yv[:, 8 * blk:8 * blk + 8, h * 64:(h + 1) * 64], o1[:], R=[o1], W=[y])
        k.barrier()


def rwkv_inputs(pr, mu, w0, w_w2, a0, a_w2, g_w2, k_k, k_a, r_k, ln_w, ln_b, hh, S):
    A = np.ascontiguousarray
    hs = slice(hh * 256, (hh + 1) * 256)
    heads = lambda t: A(t.reshape(S, 4, 64).transpose(1, 2, 0))
    col = lambda v: A(v[hs].reshape(4, 64).T)
    bc = lambda v: A(np.broadcast_to(v[hs], (64, 256)))
    return dict(rT=heads(pr[:, 0:512][:, hs]), kT=heads(pr[:, 576:1088][:, hs]), vT=heads(pr[:, 1088:1600][:, hs]),
                wloT=A(pr[:, 512:576].T), aloT=A(pr[:, 1600:1664].T), gloT=A(pr[:, 1664:1792].T),
                mu_r=col(mu[0:512]), mu_k=col(mu[576:1088]), mu_v=col(mu[1088:1600]), mu_w=A(mu[512:576].reshape(64, 1)),
                mu_a=A(mu[1600:1664].reshape(64, 1)), mu_g=A(mu[1664:1792].reshape(128, 1)),
                w0=col(w0), a0=col(a0), k_k=col(k_k), k_a=col(k_a), r_k=col(r_k),
                ww2=A(w_w2[:, hs]), aw2=A(a_w2[:, hs]), gw2=A(g_w2[:, hs]), ln_w=bc(ln_w), ln_b=bc(ln_b))


RWKV_SHAPES = lambda S: dict(rT=[4, 64, S], kT=[4, 64, S], vT=[4, 64, S], wloT=[64, S], aloT=[64, S], gloT=[128, S],
                             mu_r=[64, 4], mu_k=[64, 4], mu_v=[64, 4], mu_w=[64, 1], mu_a=[64, 1], mu_g=[128, 1],
                             w0=[64, 4], a0=[64, 4], k_k=[64, 4], k_a=[64, 4], r_k=[64, 4],
                             ww2=[64, 256], aw2=[64, 256], gw2=[128, 256], ln_w=[64, 256], ln_b=[64, 256])


SEQ = 4096
NPROJ = 5192


def _ein(k, name, shape):
    return k.dram(name, shape, F32, kind="ExternalInput")


class View:
    def __init__(self, base, ap):
        self.t = ap
        self.trk = base.trk
        self.psum = False


NF, NT = 3888, 1304


def proj_col_order():
    F, T = [], []
    r = lambda a, n: list(range(a, a + n))
    for hh in range(2):
        F += r(hh * 256, 256) + r(512 + hh * 64, 64) + r(768 + hh * 64, 64) + r(1024 + hh * 64, 64) + r(640 + hh * 64, 64)
    b = 1304
    for hh in range(2):
        F += r(b + hh * 256, 256) + r(b + 576 + hh * 256, 256) + r(b + 1088 + hh * 256, 256)
    F += r(b + 512, 64) + r(b + 1600, 64) + r(b + 1664, 128)
    b = 3096
    for hh in range(2):
        F += r(b + hh * 128, 128) + r(b + 256 + hh * 128, 128)
    F += r(b + 1024, 16)
    F += r(4648, 544)
    for hh in range(2):
        T += r(896 + hh * 64, 64) + r(1152 + hh * 64, 64) + r(1280 + hh * 12, 12)
    for hh in range(2):
        T += r(3096 + 512 + hh * 256, 256) + r(3096 + 1040 + hh * 256, 256)
    assert len(F) == NF and len(T) == NT and len(set(F + T)) == NF + NT
    return np.array(F), np.array(T)


def emit_mod_fm(k, c2, w, badd, gpre, gpost, vecs, pre=None):
    if pre is not None:
        pre()
    wv = w.t.rearrange("(kc p) f -> p kc f", p=128)
    with ExitStack() as es:
        ct = k.sb([128, KC, 2], F32, es=es)
        sc = k.sb([128, KC, 2], F32, es=es)
        bt = k.sb([128, 18, KC], F32, es=es)
        gp1 = k.sb([128, 6, KC], F32, es=es)
        gp2 = k.sb([128, 6, KC], F32, es=es)
        mt = k.sb([128, 18, KC], F32, es=es)
        wt = k.rot(2, [128, KC, 512], F32, es=es)
        v5 = k.rot(2, [128, 5, KC], F32, es=es)
        k.dma("sp", ct[:], c2.t[:, :, :], R=[c2], W=[ct])
        k.dma("sp", bt[:], badd.t[:, :, :], R=[badd], W=[bt])
        k.dma("sp", gp1[:], gpre.t[:, :, :], R=[gpre], W=[gp1])
        k.dma("sp", gp2[:], gpost.t[:, :, :], R=[gpost], W=[gp2])
        k.act(sc[:], ct[:], AF.Silu, R=[ct], W=[sc])
        ps = Rot(k.banks[0:2])
        for grp in range(18):
            p_ = ps.next()
            for q in range(4):
                w_ = wt.next()
                for q2 in range(4):
                    k.dma("sp", w_[:, 4 * q2:4 * q2 + 4, :], wv[:, 4 * q2:4 * q2 + 4, grp * D + q * 512:grp * D + (q + 1) * 512], R=[w], W=[w_])
                for m in range(4):
                    ko = q * 4 + m
                    for kc in range(KC):
                        k.mm(p_[:, 2 * ko:2 * ko + 2], w_[:, kc, m * 128:(m + 1) * 128], sc[:, kc, :], kc == 0, kc == KC - 1, R=[w_, sc], W=[p_])
            k.tt("dve", mt[:, grp, :], p_[:, 0:2 * KC].rearrange("p (c two) -> p c two", two=2)[:, :, 0], bt[:, grp, :], ALU.add,
                 R=[p_, bt], W=[mt])
        for ls in range(6):
            v_ = v5.next()
            k.copy("dve", v_[:, 0:3, :], mt[:, 3 * ls:3 * ls + 3, :], R=[mt], W=[v_])
            k.copy("dve", v_[:, 3, :], gp1[:, ls, :], R=[gp1], W=[v_])
            k.copy("dve", v_[:, 4, :], gp2[:, ls, :], R=[gp2], W=[v_])
            k.dma("sp", vecs[ls].t[:, :, :], v_[:], R=[v_], W=[vecs[ls]])
        k.barrier()


def emit_proj2(k, xT, wF, wT, vec, pF, pTok, T):
    xTv = fview(xT)
    with ExitStack() as es:
        N = NormCtx(k, es, vec, 1.0)
        uTs = [k.sb([128, KC, TB], BF16, es=es) for _ in range(2)]
        wb = k.rot(4, [128, KC, 128], BF16, es=es)
        ob = k.rot(4, [128, TB], F32, es=es)
        wtb = k.rot(2, [128, KC, 512], BF16, es=es)
        ps = Rot(k.banks[1:7])
        nch = (NF + 127) // 128
        tgroups = [(0, 512), (512, 512), (1024, NT - 1024)]
        N.prenorm(xT, xTv, 0, uTs[0])
        for p in range(T // TB):
            t0 = p * TB
            uT = uTs[p % 2]
            if p + 1 < T // TB:
                N.prenorm(xT, xTv, t0 + TB, uTs[(p + 1) % 2])
            for c in range(nch):
                c0 = c * 128
                m = min(128, NF - c0)
                b = wb.next()
                k.dma("sp", b[:, :, :].rearrange("p a b -> p (a b)"), wF.t[c], R=[wF], W=[b])
                pp = ps.next()
                for kc in range(KC):
                    k.mm(pp[0:m, :], b[:, kc, 0:m], uT[:, kc, :], kc == 0, kc == KC - 1, R=[b, uT], W=[pp])
                o = ob.next()
                if c % 2 == 0:
                    k.copy("act", o[0:m, :], pp[0:m, :], R=[pp], W=[o])
                    k.dma("act", pF.t[c0:c0 + m, t0:t0 + TB], o[0:m, :], R=[o], W=[pF])
                else:
                    k.copy("dve", o[0:m, :], pp[0:m, :], R=[pp], W=[o])
                    k.dma("pool", pF.t[c0:c0 + m, t0:t0 + TB], o[0:m, :], R=[o], W=[pF])
            for gi, (g0, gn) in enumerate(tgroups):
                wt_ = wtb.next()
                k.dma("sp", wt_[:, :, :].rearrange("p a b -> p (a b)"), wT.t[gi], R=[wT], W=[wt_])
                for sub in range(4):
                    pp = ps.next()
                    for kc in range(KC):
                        k.mm(pp[:, 0:gn], uT[:, kc, sub * 128:(sub + 1) * 128], wt_[:, kc, 0:gn], kc == 0, kc == KC - 1, R=[uT, wt_], W=[pp])
                    o = ob.next()
                    tk = t0 + sub * 128
                    if sub % 2 == 0:
                        k.copy("act", o[:, 0:gn], pp[:, 0:gn], R=[pp], W=[o])
                        k.dma("act", pTok.t[tk:tk + 128, g0:g0 + gn], o[:, 0:gn], R=[o], W=[pTok])
                    else:
                        k.copy("dve", o[:, 0:gn], pp[:, 0:gn], R=[pp], W=[o])
                        k.dma("pool", pTok.t[tk:tk + 128, g0:g0 + gn], o[:, 0:gn], R=[o], W=[pTok])
        k.barrier()


_ACT_KEYS = {"nsa": ("qT", "qsT", "kT", "ksT", "vcT", "v_tok", "gl"), "rwkv": ("rT", "kT", "vT", "wloT", "aloT", "gloT"),
             "gla": ("qT", "kT", "v_ch", "r_ch", "aloT"), "mla": ("cqT", "ckvT", "krp", "krsp")}
_SHARED_TABLES = ("cos", "sin", "ov", "selb", "ebig", "c96", "s96")


def _param_shapes():
    out = {}
    for nm, shp in (("nsa", NSA_SHAPES(SEQ)), ("rwkv", RWKV_SHAPES(SEQ)), ("gla", GLA_SHAPES(SEQ)), ("mla", MLA_SHAPES(SEQ))):
        out[nm] = {n: s for n, s in shp.items() if n not in _ACT_KEYS[nm]}
    return out


def mixer_views(pF, pTok, hh):
    f, t = pF.t, pTok.t
    V = lambda ap, base: View(base, ap)
    b = hh * 512
    nsa = dict(qT=V(f[b:b + 256, :].rearrange("(g d) s -> g d s", d=64), pF),
               kT=V(f[b + 256:b + 448, :].rearrange("(i d) s -> i d s", d=64), pF),
               vcT=V(f[b + 448:b + 512, :], pF),
               v_tok=V(t[:, hh * 140:hh * 140 + 128].rearrange("s (i d) -> i s d", d=64), pTok),
               gl=V(t[:, hh * 140 + 128:hh * 140 + 140], pTok))
    b = 1024 + hh * 768
    rw = dict(rT=V(f[b:b + 256, :].rearrange("(g d) s -> g d s", d=64), pF),
              kT=V(f[b + 256:b + 512, :].rearrange("(g d) s -> g d s", d=64), pF),
              vT=V(f[b + 512:b + 768, :].rearrange("(g d) s -> g d s", d=64), pF),
              wloT=V(f[2560:2624, :], pF), aloT=V(f[2624:2688, :], pF), gloT=V(f[2688:2816, :], pF))
    b = 2816 + hh * 256
    tb = 280 + hh * 512
    gla = dict(qT=V(f[b:b + 128, :].rearrange("(g d) s -> g d s", d=64), pF),
               kT=V(f[b + 128:b + 256, :].rearrange("(g d) s -> g d s", d=64), pF),
               aloT=V(f[3328:3344, :], pF),
               v_ch=V(t[:, tb:tb + 256].rearrange("(c p) f -> p c f", p=CH), pTok),
               r_ch=V(t[:, tb + 256:tb + 512].rearrange("(c p) f -> p c f", p=CH), pTok))
    mla = dict(cqT=V(f[3344:3728, :], pF), ckvT=V(f[3728:3856, :], pF), krT=V(f[3856:3888, :], pF))
    return dict(nsa=nsa, rwkv=rw, gla=gla, mla=mla)


def build_fused(L=2):
    nc, es, k = new_prog()
    S = SEQ
    PS = _param_shapes()
    emits = dict(nsa=emit_nsa, rwkv=emit_rwkv, gla=emit_gla, mla=emit_mla)
    with es:
        xT = _ein(k, "xT", [D, S])
        c2, wmod = _ein(k, "c2", [128, KC, 2]), _ein(k, "wmod", [D, 18 * D])
        badd, gpre, gpost = _ein(k, "badd", [128, 18, KC]), _ein(k, "gpre", [128, 6, KC]), _ein(k, "gpost", [128, 6, KC])
        tables = {n: _ein(k, "tab_" + n, (NSA_SHAPES(S) | MLA_SHAPES(S))[n]) for n in _SHARED_TABLES}
        oT = k.dram("oT", [D, S], F32, kind="ExternalOutput")
        vecs = [k.dram("vec%d" % i, [128, 5, KC], F32) for i in range(3 * L)]
        xa, xb_ = k.dram("xa", [D, S], F32), k.dram("xb", [D, S], F32)
        pF, pTok, yTok = k.dram("pF", [NF, S], F32), k.dram("pTok", [S, NT], F32), k.dram("yTok", [S, D], F32)
        WSH = dict(wg1=(NFF, KC * 128, 2048), wu1=(NFF, KC * 128, 2048), wd1=(KC, NFF * 128, 1408), wF=(31, KC * 128, 2048), wT=(3, KC * 512, 2048),
                   wgt=(4 * KC, KC * 128, 2048), wbr=(KC, 16 * 128, 2048), wout=(KC, KC * 128, 2048),
                   wg2=(NFF, KC * 128, 2048), wu2=(NFF, KC * 128, 2048), wd2=(KC, NFF * 128, 1408))
        Wf = [{n: _ein(k, "l%d_%s" % (l, n), [c, 128, r]) for n, (c, r, _) in WSH.items()} for l in range(L)]
        Wb = [{n: k.dram("l%d_%s_bf" % (l, n), [c, 128, r], BF16) for n, (c, r, _) in WSH.items()} for l in range(L)]
        groups = []
        for l in range(L):
            groups += [(l, ("wg1", "wu1", "wd1")), (l, ("wF", "wT")), (l, ("wgt", "wbr", "wout")), (l, ("wg2", "wu2", "wd2"))]
        state = {"next": 0}

        def convert_ahead(upto):
            while state["next"] < min(upto, len(groups)):
                l_, names = groups[state["next"]]
                for n in names:
                    emit_convert(k, Wf[l_][n], Wb[l_][n], WSH[n][2])
                state["next"] += 1

        emit_mod_fm(k, c2, wmod, badd, gpre, gpost, vecs, pre=lambda: convert_ahead(2))
        cur = xT
        for l in range(L):
            W = Wb[l]
            g0 = 4 * l
            emit_ffn(k, cur, W["wg1"], W["wu1"], W["wd1"], vecs[3 * l], xa, S, 0.5)
            emit_proj2(k, xa, W["wF"], W["wT"], vecs[3 * l + 1], pF, pTok, S)
            for hh in range(2):
                views = mixer_views(pF, pTok, hh)
                for br, nm in enumerate(("nsa", "rwkv", "gla", "mla")):
                    if br % 2 == 0:
                        convert_ahead(g0 + 3 + hh * 2 + br // 2)
                    I = {}
                    for n, shp in PS[nm].items():
                        I[n] = tables[n] if n in _SHARED_TABLES else _ein(k, "l%d_h%d_%s_%s" % (l, hh, nm, n), shp)
                    I.update(views[nm])
                    yv = View(yTok, yTok.t[:, br * 512 + hh * 256:br * 512 + hh * 256 + 256])
                    emits[nm](k, I, yv, S)
            emit_merge(k, xa, None, W["wgt"], W["wbr"], W["wout"], vecs[3 * l + 1], xb_, S, y_tok=yTok)
            last = l == L - 1
            emit_ffn(k, xb_, W["wg2"], W["wu2"], W["wd2"], vecs[3 * l + 2], oT if last else xa, S, 0.5)
            cur = xa
        k.finish([oT])
    return nc, k.ninstr


def fused_inputs(P, b, L=2):
    A = lambda a: np.ascontiguousarray(np.asarray(a, dtype=np.float32))
    S = SEQ
    m = {}
    m["xT"] = A(np.asarray(P["x"][b]).T)
    cb = np.asarray(P["c"][b], dtype=np.float32)
    m["c2"] = A(np.repeat(cb.reshape(KC, 128).T[:, :, None], 2, axis=2))
    m["wmod"] = P["_wmod"]
    m["badd"] = P["_badd"]
    m["gpre"], m["gpost"] = P["_gpre"], P["_gpost"]
    for n in _SHARED_TABLES:
        m["tab_" + n] = P["_tab"][n]
    for l in range(L):
        for n, v in P["_lw"][l].items():
            m["l%d_%s" % (l, n)] = v
        for hh in range(2):
            for nm in ("nsa", "rwkv", "gla", "mla"):
                for n, v in P["_mp"][l][hh][nm].items():
                    if n not in _SHARED_TABLES:
                        m["l%d_h%d_%s_%s" % (l, hh, nm, n)] = v
    return m


def kernel(**P):
    A = lambda a: np.ascontiguousarray(np.asarray(a, dtype=np.float32))
    L, S = int(np.asarray(P["ada_w"]).shape[0]), SEQ
    Fo, To = proj_col_order()
    fm = lambda v: A(np.asarray(v, dtype=np.float32).reshape(-1, KC, 128).transpose(2, 0, 1))
    P = dict(P)
    P["_wmod"] = A(np.concatenate([np.asarray(P["ada_w"][l, s]) for l in range(L) for s in range(3)], axis=1))
    P["_badd"] = fm(np.asarray(P["ada_b"]).reshape(L * 3 * 3, D))
    P["_gpre"], P["_gpost"] = fm(np.asarray(P["pre_g"]).reshape(L * 3, D)), fm(np.asarray(P["post_g"]).reshape(L * 3, D))
    z = lambda n: np.zeros((S, n), np.float32)
    lw, mp, tab = [], [], {}
    for l in range(L):
        w_in = np.asarray(P["mix_w_in"][l])
        cw = lambda a: chunk_w(a, 128)
        lw.append(dict(wg1=cw(P["ffn_wg"][l, 0]), wu1=cw(P["ffn_wu"][l, 0]), wd1=cw(P["ffn_wd"][l, 0]),
                       wg2=cw(P["ffn_wg"][l, 1]), wu2=cw(P["ffn_wu"][l, 1]), wd2=cw(P["ffn_wd"][l, 1]),
                       wF=cw(w_in[:, Fo]), wT=chunk_w(w_in[:, To], 512), wgt=cw(w_in[:, NPROJ:]),
                       wbr=cw(np.asarray(P["mix_w_branch"][l]).reshape(D, D)), wout=cw(P["mix_w_out"][l])))
        lw[-1] = {n: np.ascontiguousarray(v.reshape(v.shape[0], 128, -1)) for n, v in lw[-1].items()}
        per_hh = []
        for hh in range(2):
            g = lambda name: np.asarray(P[name][l])
            d = dict(nsa=nsa_inputs(z(1304), g("nsa_cmp_pos"), g("nsa_cmp_w1"), g("nsa_cmp_w2"), hh, S),
                     rwkv=rwkv_inputs(z(1792), *[g(n) for n in ("rwkv_mu", "rwkv_w0", "rwkv_w_w2", "rwkv_a0", "rwkv_a_w2", "rwkv_g_w2",
                                                                "rwkv_k_k", "rwkv_k_a", "rwkv_r_k", "rwkv_ln_w", "rwkv_ln_b")], hh, S),
                     gla=gla_inputs(z(1552), g("gla_alpha_w2"), g("gla_alpha_b"), g("gla_norm_g"), hh, S),
                     mla=mla_inputs(z(544), g("mla_w_uq"), g("mla_w_ukv"), g("mla_q_norm"), g("mla_kv_norm"), hh, S))
            for nm in d:
                for n in list(d[nm]):
                    if n in _SHARED_TABLES:
                        tab[n] = A(d[nm][n])
                    if n in _ACT_KEYS[nm]:
                        del d[nm][n]
                    else:
                        d[nm][n] = A(d[nm][n])
            per_hh.append(d)
        mp.append(per_hh)
    P["_lw"], P["_mp"], P["_tab"] = lw, mp, tab
    nc, _ = build_fused(L)
    in_maps = [fused_inputs(P, c % 4, L) for c in range(NCORES)]
    res = run_bass_kernel_spmd(nc, in_maps, core_ids=list(range(NCORES))).results
    out = np.empty((4, S, D), np.float32)
    for b in range(4):
        out[b] = res[b]["oT"].T
    return out
```
